# Optimizing a Trainium2 kernel written in Bass

```python
import math
import jax, jax.numpy as jnp
from jax import lax
import numpy as np

D_MODEL = 1024
BATCH = 32
SEQ = 256
DEPTH = 4
DEC_BATCH = 8
DEC_SEQ = 1024
PAST_LEN = 512

GRID_W = 64
N_MIXERS = 2
N_HYENA_LAYERS = (DEPTH + 1) // 2
N_ATTN_LAYERS = DEPTH // 2
N_HEADS = 8
N_KV_HEADS = 2
HEAD_DIM = 128
QKV_DIM = (N_HEADS + 2 * N_KV_HEADS) * HEAD_DIM
Q_BLOCK = 128
ROPE_THETA = 10000.0
QK_EPS = 1e-6
HYENA_ORDER = 2
POS_EMB_DIM = 33
FILTER_WIDTH = 64
N_INNER_MLPS = 2
FAST_DECAY_PCT = 0.3
SLOW_DECAY_PCT = 1.5
DECAY_TARGET = 1e-2
MOD_SHIFT = 0.0
D_FF = 2816
LN_EPS = 1e-5
N_MOD = 6
DN_ALPHA = (2 * DEPTH) ** 0.25
DN_BETA = (8 * DEPTH) ** -0.25

kernel_name = 'hybrid_hyena_gqa_diffusion_step'

F32 = jnp.float32


def _layer_norm(x, g, b):
    xf = x.astype(F32)
    mu = jnp.mean(xf, -1, keepdims=True)
    var = jnp.mean(jnp.square(xf - mu), -1, keepdims=True)
    return ((xf - mu) * lax.rsqrt(var + LN_EPS) * g + b).astype(x.dtype)


def _rms_norm(x, g):
    xf = x.astype(F32)
    return (xf * lax.rsqrt(jnp.mean(jnp.square(xf), -1, keepdims=True) + QK_EPS) * g).astype(x.dtype)


def _dwconv3(x, w, b):
    L = x.shape[1]
    xp = jnp.pad(x, ((0, 0), (1, 1), (0, 0)))
    return xp[:, :L] * w[0] + xp[:, 1:L + 1] * w[1] + xp[:, 2:L + 2] * w[2] + b


def _modulation(cond, w_mod, b_mod):
    m = jax.nn.silu(cond) @ w_mod + b_mod
    return m.reshape(cond.shape[0], 1, N_MOD, D_MODEL)


def _hyena_filters(L, w1, b1, w2, b2, w_out, freq):
    t = jnp.linspace(0.0, 1.0, L, dtype=F32)[:, None]
    n_bands = (POS_EMB_DIM - 1) // 2
    w = 2.0 * math.pi * jnp.arange(L, dtype=F32) / L
    f = jnp.linspace(1e-4, n_bands - 1, n_bands, dtype=F32)
    ang = w[:, None] * f[None, :]
    z = jnp.concatenate([t, jnp.cos(ang), -jnp.sin(ang)], -1)
    h = jnp.sin(freq * (z @ w1 + b1))
    for i in range(N_INNER_MLPS):
        h = jnp.sin(freq * (h @ w2[i] + b2[i]))
    k = (h @ w_out).astype(F32).reshape(L, 2, D_MODEL)
    max_decay = math.log(DECAY_TARGET) / FAST_DECAY_PCT
    min_decay = math.log(DECAY_TARGET) / SLOW_DECAY_PCT
    deltas = jnp.abs(jnp.linspace(min_decay, max_decay, D_MODEL, dtype=F32))
    decay = jnp.exp(-t * deltas)
    return k * (decay + MOD_SHIFT)[:, None, :]


def _bidir_fftconv(u, k, bias):
    L = u.shape[1]
    k_full = jnp.concatenate([k[:, 0], jnp.zeros((1, D_MODEL), F32), k[:0:-1, 1]], 0)
    uf = u.astype(F32)
    y = jnp.fft.irfft(jnp.fft.rfft(uf, n=2 * L, axis=1) * jnp.fft.rfft(k_full, axis=0)[None],
                      n=2 * L, axis=1)[:, :L]
    return (y + uf * bias.astype(F32)).astype(u.dtype)


def _hyena_mixer(h, in_w, in_b, short_w, short_b, pw1, pb1, pw2, pb2, pwout, freq, filt_bias, out_w, out_b):
    L = h.shape[1]
    z = _dwconv3(h @ in_w + in_b, short_w, short_b)
    x0, x1, v = jnp.split(z, HYENA_ORDER + 1, axis=-1)
    k = _hyena_filters(L, pw1, pb1, pw2, pb2, pwout, freq)
    v = _bidir_fftconv(v * x1, k, filt_bias)
    return (x0 * v) @ out_w + out_b


def _qkv_heads(h, w_qkv, b_qkv, q_gain, k_gain):
    B, L, _ = h.shape
    z = h @ w_qkv + b_qkv
    q, k, v = jnp.split(z, [N_HEADS * HEAD_DIM, (N_HEADS + N_KV_HEADS) * HEAD_DIM], axis=-1)
    q = _rms_norm(q.reshape(B, L, N_HEADS, HEAD_DIM), q_gain)
    k = _rms_norm(k.reshape(B, L, N_KV_HEADS, HEAD_DIM), k_gain)
    return q, k, v.reshape(B, L, N_KV_HEADS, HEAD_DIM)


def _axial_rope_tables(L):
    n_rows = L // GRID_W
    rows = jnp.repeat(jnp.arange(n_rows), GRID_W).astype(F32)
    cols = jnp.tile(jnp.arange(GRID_W), n_rows).astype(F32)
    half = HEAD_DIM // 2
    inv = ROPE_THETA ** (-jnp.arange(0, half, 2, dtype=F32) / half)
    ang = jnp.concatenate([rows[:, None] * inv, cols[:, None] * inv], -1)
    return jnp.cos(ang), jnp.sin(ang)


def _apply_rope(x, cos, sin):
    xf = x.astype(F32).reshape(x.shape[:-1] + (HEAD_DIM // 2, 2))
    x1, x2 = xf[..., 0], xf[..., 1]
    c = cos[None, :, None, :]
    s = sin[None, :, None, :]
    return jnp.stack([x1 * c - x2 * s, x1 * s + x2 * c], -1).reshape(x.shape).astype(x.dtype)


def _blocked_gqa(q, k, v):
    B, Lq = q.shape[:2]
    G = N_HEADS // N_KV_HEADS
    nb = Lq // Q_BLOCK
    qb = q.reshape(B, nb, Q_BLOCK, N_KV_HEADS, G, HEAD_DIM).transpose(1, 0, 2, 3, 4, 5)
    kf = k.astype(F32)
    vf = v.astype(F32)
    scale = HEAD_DIM ** -0.5

    def one_block(qblk):
        s = jnp.einsum('bqkgd,bskd->bkgqs', qblk.astype(F32), kf) * scale
        p = jax.nn.softmax(s, axis=-1)
        return jnp.einsum('bkgqs,bskd->bqkgd', p, vf).astype(q.dtype)

    o = lax.map(one_block, qb)
    return o.transpose(1, 0, 2, 3, 4, 5).reshape(B, Lq, N_HEADS * HEAD_DIM)


def _attn_context(h, w_qkv, b_qkv, q_gain, k_gain, w_o, b_o):
    q, k, v = _qkv_heads(h, w_qkv, b_qkv, q_gain, k_gain)
    return _blocked_gqa(q, k, v) @ w_o + b_o, k, v


def _attn_latent(h, k_ctx, v_ctx, w_qkv, b_qkv, q_gain, k_gain, w_o, b_o):
    q, k, v = _qkv_heads(h, w_qkv, b_qkv, q_gain, k_gain)
    cos, sin = _axial_rope_tables(h.shape[1])
    q = _apply_rope(q, cos, sin)
    k = _apply_rope(k, cos, sin)
    k_all = jnp.concatenate([k, k_ctx.astype(k.dtype)], axis=1)
    v_all = jnp.concatenate([v, v_ctx.astype(v.dtype)], axis=1)
    return _blocked_gqa(q, k_all, v_all) @ w_o + b_o


def _conv_ffn(h, w_in, b_in, conv_w, conv_b, w_out, b_out):
    u = _dwconv3(h @ w_in + b_in, conv_w, conv_b)
    g, val = jnp.split(u, 2, axis=-1)
    return (jax.nn.gelu(g) * val) @ w_out + b_out


def _trunk(x, cond, P, cache_k=None, cache_v=None):
    is_ctx = cache_k is None
    new_k, new_v = [], []
    for l in range(DEPTH):
        mods = _modulation(cond, P['w_mod'][l], P['b_mod'][l])
        j = l // N_MIXERS
        h = x * (1.0 + mods[:, :, 1]) + mods[:, :, 0]
        if l % N_MIXERS == 0:
            out = _hyena_mixer(h, P['hy_in_w'][j], P['hy_in_b'][j], P['hy_short_w'][j], P['hy_short_b'][j],
                               P['hy_pos_w1'][j], P['hy_pos_b1'][j], P['hy_pos_w2'][j], P['hy_pos_b2'][j],
                               P['hy_pos_wout'][j], P['hy_freq'][j], P['hy_filt_bias'][j],
                               P['hy_out_w'][j], P['hy_out_b'][j])
        elif is_ctx:
            out, k, v = _attn_context(h, P['at_qkv_w'][j], P['at_qkv_b'][j], P['at_q_gain'][j],
                                      P['at_k_gain'][j], P['at_o_w'][j], P['at_o_b'][j])
            new_k.append(k)
            new_v.append(v)
        else:
            out = _attn_latent(h, cache_k[:, j], cache_v[:, j], P['at_qkv_w'][j], P['at_qkv_b'][j],
                               P['at_q_gain'][j], P['at_k_gain'][j], P['at_o_w'][j], P['at_o_b'][j])
        x = _layer_norm(DN_ALPHA * x + (1.0 + mods[:, :, 2]) * out, P['ln_g'][l, 0], P['ln_b'][l, 0])
        h = x * (1.0 + mods[:, :, 4]) + mods[:, :, 3]
        out = _conv_ffn(h, P['ff_in_w'][l], P['ff_in_b'][l], P['ff_conv_w'][l], P['ff_conv_b'][l],
                        P['ff_out_w'][l], P['ff_out_b'][l])
        x = _layer_norm(DN_ALPHA * x + (1.0 + mods[:, :, 5]) * out, P['ln_g'][l, 1], P['ln_b'][l, 1])
    if is_ctx:
        return x, jnp.stack(new_k, axis=1), jnp.stack(new_v, axis=1)
    return x


def setup_inputs(seed: int = 0) -> dict:
    key = jax.random.key(seed)
    ks = iter(jax.random.split(key, 48))

    def nrm(shape, std):
        return std * jax.random.normal(next(ks), shape, F32)

    D = D_MODEL
    NH, NA = N_HYENA_LAYERS, N_ATTN_LAYERS
    qk_w = nrm((NA, D, (N_HEADS + N_KV_HEADS) * HEAD_DIM), D ** -0.5)
    v_w = nrm((NA, D, N_KV_HEADS * HEAD_DIM), DN_BETA * D ** -0.5)
    return {
        'x_prompt': nrm((BATCH, SEQ, D), 1.0),
        'x_sample': nrm((DEC_BATCH, DEC_SEQ, D), 1.0),
        'cache_k': nrm((DEC_BATCH, NA, PAST_LEN, N_KV_HEADS, HEAD_DIM), 1.0),
        'cache_v': nrm((DEC_BATCH, NA, PAST_LEN, N_KV_HEADS, HEAD_DIM), 0.5),
        'c': nrm((DEC_BATCH, D), 1.0),
        'c_ctx': nrm((D,), 1.0),
        'w_mod': nrm((DEPTH, D, N_MOD * D), 0.2 * D ** -0.5),
        'b_mod': nrm((DEPTH, N_MOD * D), 0.01),
        'ln_g': 1.0 + nrm((DEPTH, 2, D), 0.01),
        'ln_b': nrm((DEPTH, 2, D), 0.01),
        'hy_in_w': nrm((NH, D, (HYENA_ORDER + 1) * D), D ** -0.5),
        'hy_in_b': nrm((NH, (HYENA_ORDER + 1) * D), 0.01),
        'hy_short_w': nrm((NH, 3, (HYENA_ORDER + 1) * D), 3 ** -0.5),
        'hy_short_b': nrm((NH, (HYENA_ORDER + 1) * D), 0.01),
        'hy_pos_w1': nrm((NH, POS_EMB_DIM, FILTER_WIDTH), POS_EMB_DIM ** -0.5),
        'hy_pos_b1': nrm((NH, FILTER_WIDTH), 0.01),
        'hy_pos_w2': nrm((NH, N_INNER_MLPS, FILTER_WIDTH, FILTER_WIDTH), FILTER_WIDTH ** -0.5),
        'hy_pos_b2': nrm((NH, N_INNER_MLPS, FILTER_WIDTH), 0.01),
        'hy_pos_wout': nrm((NH, FILTER_WIDTH, 2 * (HYENA_ORDER - 1) * D), 0.1 * FILTER_WIDTH ** -0.5),
        'hy_freq': 1.0 + nrm((NH, FILTER_WIDTH), 0.01),
        'hy_filt_bias': nrm((NH, D), 0.1),
        'hy_out_w': nrm((NH, D, D), DN_BETA * D ** -0.5),
        'hy_out_b': nrm((NH, D), 0.01),
        'at_qkv_w': jnp.concatenate([qk_w, v_w], axis=-1),
        'at_qkv_b': nrm((NA, QKV_DIM), 0.01),
        'at_q_gain': 1.0 + nrm((NA, HEAD_DIM), 0.01),
        'at_k_gain': 1.0 + nrm((NA, HEAD_DIM), 0.01),
        'at_o_w': nrm((NA, N_HEADS * HEAD_DIM, D), DN_BETA * (N_HEADS * HEAD_DIM) ** -0.5),
        'at_o_b': nrm((NA, D), 0.01),
        'ff_in_w': nrm((DEPTH, D, 2 * D_FF), DN_BETA * D ** -0.5),
        'ff_in_b': nrm((DEPTH, 2 * D_FF), 0.01),
        'ff_conv_w': nrm((DEPTH, 3, 2 * D_FF), 3 ** -0.5),
        'ff_conv_b': nrm((DEPTH, 2 * D_FF), 0.01),
        'ff_out_w': nrm((DEPTH, D_FF, D), DN_BETA * D_FF ** -0.5),
        'ff_out_b': nrm((DEPTH, D), 0.01),
    }


def reference(x_prompt, x_sample, cache_k, cache_v, c, c_ctx, w_mod, b_mod, ln_g, ln_b,
              hy_in_w, hy_in_b, hy_short_w, hy_short_b, hy_pos_w1, hy_pos_b1, hy_pos_w2, hy_pos_b2,
              hy_pos_wout, hy_freq, hy_filt_bias, hy_out_w, hy_out_b,
              at_qkv_w, at_qkv_b, at_q_gain, at_k_gain, at_o_w, at_o_b,
              ff_in_w, ff_in_b, ff_conv_w, ff_conv_b, ff_out_w, ff_out_b):
    P = {
        'w_mod': w_mod, 'b_mod': b_mod, 'ln_g': ln_g, 'ln_b': ln_b,
        'hy_in_w': hy_in_w, 'hy_in_b': hy_in_b, 'hy_short_w': hy_short_w, 'hy_short_b': hy_short_b,
        'hy_pos_w1': hy_pos_w1, 'hy_pos_b1': hy_pos_b1, 'hy_pos_w2': hy_pos_w2, 'hy_pos_b2': hy_pos_b2,
        'hy_pos_wout': hy_pos_wout, 'hy_freq': hy_freq, 'hy_filt_bias': hy_filt_bias,
        'hy_out_w': hy_out_w, 'hy_out_b': hy_out_b,
        'at_qkv_w': at_qkv_w, 'at_qkv_b': at_qkv_b, 'at_q_gain': at_q_gain, 'at_k_gain': at_k_gain,
        'at_o_w': at_o_w, 'at_o_b': at_o_b,
        'ff_in_w': ff_in_w, 'ff_in_b': ff_in_b, 'ff_conv_w': ff_conv_w, 'ff_conv_b': ff_conv_b,
        'ff_out_w': ff_out_w, 'ff_out_b': ff_out_b,
    }
    y_prompt, new_cache_k, new_cache_v = _trunk(x_prompt, c_ctx[None, :], P)
    y_sample = _trunk(x_sample, c, P, cache_k, cache_v)
    return (y_prompt, y_sample, new_cache_k, new_cache_v)
```

```python
import contextlib
import numpy as np
import ml_dtypes
import concourse.bass as bass
import concourse.mybir as mybir
from concourse.bass_utils import run_bass_kernel_spmd

F32 = mybir.dt.float32
BF16 = mybir.dt.bfloat16
F32R = mybir.dt.float32r
AF = mybir.ActivationFunctionType
ALU = mybir.AluOpType
AX = mybir.AxisListType

D = 1024
NM = 8
TOK = 1024
DEPTH = 4
DFF = 2816
NFC = 22
LN_EPS = 1e-5
QK_EPS = 1e-6
ALPHA = float((2 * DEPTH) ** 0.25)
GROUPS = [dict(nseq=4, L=256), dict(nseq=1, L=1024)]
MAGIC = 12582912.0
TWO_PI = float(2 * np.pi)
NSLOT = 3
MULT_ENG = "pool"
SLOT_EL = 4096

DBG = None


class _Stop(Exception):
    pass


class _CountProxy:
    def __init__(self, h):
        self.h = h
        self.n = 0

    def matmul(self, *a, **k):
        self.n += 1
        return self.h.matmul(*a, **k)

    def transpose(self, *a, **k):
        self.n += 1
        return self.h.transpose(*a, **k)

    def __getattr__(self, name):
        return getattr(self.h, name)


class Lane:
    def __init__(self, name, step):
        self.name = name
        self.step = step
        self.ops = []
        self.sem = None
        self.cum = None

    def count_at(self, seq):
        return self.cum[seq]


class Op:
    __slots__ = ("eng", "lane", "seq", "fn", "deps", "need", "vc", "isdma", "waits", "ninc", "phase")


class Prog:
    ENGS = ("pe", "act", "dve", "pool", "sp")

    def __init__(self):
        self.lanes = {e: Lane(e, 1) for e in ("pe", "act", "dve", "pool")}
        self.dma_lanes = []
        self.eng_ops = {e: [] for e in self.ENGS}
        self.all_ops = []
        self.last_w = {}
        self.readers = {}
        self.fences = {}
        self.store_lanes = []
        self.gfence = {}
        self.phase = ""
        self.pe_log = []

    def dma_lane(self, name, store=False):
        ln = Lane(name, 16)
        self.dma_lanes.append(ln)
        if store:
            self.store_lanes.append(ln)
        return ln

    def _add(self, eng, lane, fn, reads, writes, isdma, ninc=1):
        op = Op()
        op.eng, op.lane, op.seq, op.fn, op.isdma, op.need, op.ninc = eng, lane, len(lane.ops), fn, isdma, isdma, ninc
        op.phase = self.phase
        deps = {}
        own = self.lanes.get(eng)

        def dep(ln, sq, raw):
            if (not isdma) and (ln is own) and (not raw) and eng == "pe":
                return
            cur = deps.get(ln)
            if cur is None or sq > cur:
                deps[ln] = sq

        for ln, sq in self.gfence.items():
            dep(ln, sq, True)
        for k in reads:
            f = self.fences.get(k[0])
            if f:
                for ln, sq in f.items():
                    dep(ln, sq, True)
            w = self.last_w.get(k)
            if w:
                dep(w[0], w[1], True)
        for k in writes:
            f = self.fences.get(k[0])
            if f:
                for ln, sq in f.items():
                    dep(ln, sq, True)
            w = self.last_w.get(k)
            if w:
                dep(w[0], w[1], False)
            for ln, sq in self.readers.get(k, {}).items():
                dep(ln, sq, False)
        for ln in list(deps):
            if ln.step == 16:
                deps[ln] = len(ln.ops) - 1
        if lane in deps and deps[lane] >= op.seq:
            deps[lane] = op.seq - 1
        op.deps = deps
        lane.ops.append(op)
        self.eng_ops[eng].append(op)
        self.all_ops.append(op)
        for k in reads:
            self.readers.setdefault(k, {})[lane] = op.seq
        for k in writes:
            self.last_w[k] = (lane, op.seq)
            self.readers[k] = {}
        return op

    def op(self, eng, fn, r=(), w=()):
        return self._add(eng, self.lanes[eng], fn, r, w, False)

    def dma(self, eng, lane, fn, r=(), w=(), n=1):
        return self._add(eng, lane, fn, r, w, True, n)

    def barrier(self):
        for ln in list(self.lanes.values()) + self.dma_lanes:
            if ln.ops:
                self.gfence[ln] = len(ln.ops) - 1

    def fence(self, region):
        f = dict(self.fences.get(region, {}))

        def upd(ln, sq):
            if f.get(ln, -1) < sq:
                f[ln] = sq

        for k in [k for k in self.last_w if k[0] == region]:
            ln, sq = self.last_w.pop(k)
            upd(ln, sq)
        for k in [k for k in self.readers if k[0] == region]:
            for ln, sq in self.readers.pop(k).items():
                upd(ln, sq)
        self.fences[region] = f

    def finalize(self):
        know = {e: {} for e in self.ENGS}
        for op in self.all_ops:
            K = know[op.eng]
            waits = []
            for ln, sq in op.deps.items():
                if sq < 0 or K.get(ln, -1) >= sq:
                    continue
                waits.append((ln, sq))
                tgt = ln.ops[sq]
                tgt.need = True
                for l2, s2 in tgt.vc.items():
                    if K.get(l2, -1) < s2:
                        K[l2] = s2
                if K.get(ln, -1) < sq:
                    K[ln] = sq
            op.waits = waits
            vc = dict(K)
            if vc.get(op.lane, -1) < op.seq:
                vc[op.lane] = op.seq
            op.vc = vc
        for ln in list(self.lanes.values()) + self.dma_lanes:
            c = 0
            ln.cum = []
            for o in ln.ops:
                if ln.step == 16:
                    c += 16 * o.ninc
                elif o.need:
                    c += 1
                ln.cum.append(c)

    def emit(self, eng, h):
        if eng == "pe":
            h = _CountProxy(h)
        for op in self.eng_ops[eng]:
            for ln, sq in op.waits:
                h.wait_ge(ln.sem, ln.count_at(sq))
            if eng == "pe":
                n0 = h.n
            ins = op.fn(h)
            if eng == "pe":
                self.pe_log.append((op.phase, h.n - n0))
            if op.isdma:
                if not isinstance(ins, (list, tuple)):
                    ins = [ins]
                assert len(ins) == op.ninc
                for i in ins:
                    i.then_inc(op.lane.sem, 16)
            elif op.need:
                ins.then_inc(op.lane.sem, 1)
        if eng == "sp":
            for ln in self.store_lanes:
                if ln.ops:
                    h.wait_ge(ln.sem, ln.cum[-1])


def _pp_layout():
    items = [
        ("bmod", 192), ("lng", 64), ("lnb", 64),
        ("hy_in_b", 48), ("hy_sw", 144), ("hy_sb", 48), ("hy_fb", 16), ("hy_ob", 16),
        ("at_ob", 16), ("ff_in_b", 176), ("ff_cw", 528), ("ff_cb", 176), ("ff_ob", 32),
        ("freq", 2), ("pb1", 2), ("pb2", 4), ("tn256", 2), ("tn1024", 8), ("mask0", 1),
    ]
    off = {}
    o = 0
    for n, w in items:
        off[n] = (o, w)
        o += w
    return off, o


PP_OFF, NPP = _pp_layout()


def _cp(v):
    v = np.asarray(v, np.float32)
    sh = v.shape
    c = sh[-1] // 128
    v = v.reshape(sh[:-1] + (c, 128))
    v = np.moveaxis(v, -1, 0)
    return np.ascontiguousarray(v).reshape(128, -1)


def pack_pp(inp):
    pp = np.zeros((128, NPP), np.float32)

    def put(name, arr):
        o, w = PP_OFF[name]
        assert arr.shape == (128, w), (name, arr.shape, w)
        pp[:, o:o + w] = arr

    put("bmod", _cp(inp["b_mod"]))
    put("lng", _cp(inp["ln_g"]))
    put("lnb", _cp(inp["ln_b"]))
    put("hy_in_b", _cp(inp["hy_in_b"]))
    put("hy_sw", _cp(inp["hy_short_w"]))
    put("hy_sb", _cp(inp["hy_short_b"]))
    put("hy_fb", _cp(inp["hy_filt_bias"]))
    put("hy_ob", _cp(inp["hy_out_b"]))
    put("at_ob", _cp(inp["at_o_b"]))
    put("ff_in_b", _cp(inp["ff_in_b"]))
    put("ff_cw", _cp(inp["ff_conv_w"]))
    put("ff_cb", _cp(inp["ff_conv_b"]))
    put("ff_ob", _cp(inp["ff_out_b"]))
    z = np.zeros((128, 2), np.float32)
    z[:64] = np.asarray(inp["hy_freq"], np.float32).T
    put("freq", z)
    z = np.zeros((128, 2), np.float32)
    z[:64] = np.asarray(inp["hy_pos_b1"], np.float32).T
    put("pb1", z)
    z = np.zeros((128, 4), np.float32)
    z[:64] = np.asarray(inp["hy_pos_b2"], np.float32).reshape(4, 64).T
    put("pb2", z)
    for L in (256, 1024):
        t = np.linspace(0.0, 1.0, L, dtype=np.float32)
        put("tn%d" % L, np.ascontiguousarray(-t.reshape(L // 128, 128).T))
    m = np.ones((128, 1), np.float32)
    m[0, 0] = 0.0
    put("mask0", m)
    return pp


def _round_f32r(a):
    u = np.ascontiguousarray(a, np.float32).view(np.uint32).astype(np.uint64)
    u = ((u + 0x800) & 0xFFFFF000).astype(np.uint32)
    return u.view(np.float32)


def make_consts():
    c = {}
    c["ident"] = np.eye(128).astype(ml_dtypes.bfloat16)
    for L in (256, 1024):
        N = 2 * L
        t = np.arange(L, dtype=np.float64)
        k = np.arange(L, dtype=np.float64)
        th = 2.0 * np.pi * np.outer(t, k + 0.5) / N
        C = np.cos(th)
        S = -np.sin(th)
        nj = L // 128
        Fm = np.zeros((nj, 128, nj, 256), np.float64)
        Gm = np.zeros((nj, 128, 2, L), np.float64)
        for j in range(nj):
            cr = C[:, j * 128:(j + 1) * 128].reshape(nj, 128, 128)
            ci = S[:, j * 128:(j + 1) * 128].reshape(nj, 128, 128)
            Fm[j, :, :, 0:128] = cr.transpose(1, 0, 2)
            Fm[j, :, :, 128:256] = ci.transpose(1, 0, 2)
            Gm[j, :, 0, :] = (2.0 / N) * C[:, j * 128:(j + 1) * 128].T
            Gm[j, :, 1, :] = (2.0 / N) * S[:, j * 128:(j + 1) * 128].T
        Fh = Fm.astype(np.float32).astype(ml_dtypes.bfloat16)
        Fl = (Fm - Fh.astype(np.float64)).astype(np.float32).astype(ml_dtypes.bfloat16)
        c["F%d" % L] = np.ascontiguousarray(np.stack([Fh, Fl], 2))
        Gh = Gm.astype(np.float32).astype(ml_dtypes.bfloat16)
        Gl = (Gm - Gh.astype(np.float64)).astype(np.float32).astype(ml_dtypes.bfloat16)
        c["G%d" % L] = np.ascontiguousarray(np.stack([Gh, Gl], 2))
        tl = np.linspace(0.0, 1.0, L, dtype=np.float32)[:, None]
        w = (2.0 * np.pi * np.arange(L, dtype=np.float32) / L).astype(np.float32)
        f = np.linspace(1e-4, 15, 16, dtype=np.float32)
        ang = (w[:, None] * f[None, :]).astype(np.float32)
        z = np.concatenate([tl, np.cos(ang), -np.sin(ang)], -1).astype(np.float32)
        c["zT%d" % L] = np.ascontiguousarray(z.T)
    max_decay = np.log(1e-2) / 0.3
    min_decay = np.log(1e-2) / 1.5
    c["delta"] = np.abs(np.linspace(min_decay, max_decay, D, dtype=np.float32)).astype(np.float32)
    rows = np.repeat(np.arange(16), 64).astype(np.float32)
    cols = np.tile(np.arange(64), 16).astype(np.float32)
    inv = (10000.0 ** (-np.arange(0, 64, 2, dtype=np.float32) / 64)).astype(np.float32)
    ang = np.concatenate([rows[:, None] * inv, cols[:, None] * inv], -1).astype(np.float32)
    cs = np.stack([np.cos(ang), np.sin(ang)], 1).astype(np.float32)
    c["rope"] = np.ascontiguousarray(cs.reshape(8, 128, 2, 64).transpose(1, 0, 2, 3))
    return c


def build_nc(dbg=None, ngroups=2, depth=DEPTH):
    nc = bass.Bass("TRN2", target_bir_lowering=False)
    P = Prog()
    es = contextlib.ExitStack()

    def din(name, shape, dt=F32):
        return nc.dram_tensor(name, list(shape), dt, kind="ExternalInput")

    xin = din("xin", [2, 8, 128, 1024]).ap()
    cond = din("cond", [128, 8, 2]).ap()
    cache_k = din("cache_k", [2, 512, 256]).ap()
    cache_v = din("cache_v", [2, 512, 256]).ap()
    ppd = din("pp", [128, NPP]).ap()
    w_mod = din("w_mod", [4, 1024, 6144]).ap()
    hy_in_w = din("hy_in_w", [2, 1024, 3072]).ap()
    hy_out_w = din("hy_out_w", [2, 1024, 1024]).ap()
    at_qkv_w = din("at_qkv_w", [2, 1024, 1536]).ap()
    at_o_w = din("at_o_w", [2, 1024, 1024]).ap()
    ff_in_w = din("ff_in_w", [4, 1024, 5632]).ap()
    ff_out_w = din("ff_out_w", [4, 2816, 1024]).ap()
    pos_w1 = din("pos_w1", [2, 33, 64]).ap()
    pos_w2 = din("pos_w2", [4, 64, 64]).ap()
    pos_wout = din("pos_wout", [2, 64, 2048]).ap()
    qkvb_t = din("qkvb", [2, 1536])
    qgain_t = din("qgain", [2, 128])
    kgain_t = din("kgain", [2, 128])
    ident_d = din("ident", [128, 128], BF16).ap()
    zT_d = {L: din("zT%d" % L, [33, L]).ap() for L in (256, 1024)}
    F_d = {L: din("F%d" % L, [L // 128, 128, 2, L // 128, 256], BF16).ap() for L in (256, 1024)}
    G_d = {L: din("G%d" % L, [L // 128, 128, 2, 2, L], BF16).ap() for L in (256, 1024)}
    delta_t = din("delta", [1024])
    rope_d = din("rope", [128, 8, 2, 64]).ap()

    yT = nc.dram_tensor("yT", [2, 8, 128, 1024], F32, kind="ExternalOutput").ap()
    nk_o = nc.dram_tensor("nk", [4, 2, 256, 256], F32, kind="ExternalOutput").ap()
    nv_o = nc.dram_tensor("nv", [4, 2, 256, 256], F32, kind="ExternalOutput").ap()

    def sb(name, shape, dt=F32):
        return es.enter_context(nc.sbuf_tensor(name, list(shape), dt))

    def semaphore(name):
        return es.enter_context(nc.semaphore(name))

    xT = sb("xT", [128, 8, 1024])
    hT = sb("hT", [128, 8, 1024], BF16)
    big = sb("big", [128, 22528], BF16)
    cvb = sb("cvb", [128, 4, 1024])
    dsc = sb("dsc", [128, 3, 44])
    slots = sb("slots", [128, NSLOT, SLOT_EL], BF16)
    pp = sb("ppsb", [128, NPP])
    mod = sb("mod", [128, 4 * 48 * 2])
    mod1 = sb("mod1", [128, 4 * 48 * 2])
    gbt = sb("gbt", [128, 4 * 2 * 8 * 2])
    AAt = sb("AAt", [128, 4 * 2 * 8 * 2])
    BBt = sb("BBt", [128, 4 * 2 * 8 * 2])
    condT = sb("condT", [128, 16])
    condb = sb("condb", [128, 16], BF16)
    ident = sb("identsb", [128, 128], BF16)
    onesb = sb("onesb", [128, 128], BF16)
    onesf = sb("onesf", [128, 128])
    etmp = sb("etmp", [128, 4, 512])
    NMW = 4
    modw = sb("modw", [128, NMW, 1024], BF16)
    lnst = sb("lnst", [128, 2, 512])
    xrt = sb("xrt", [128, 2, 512])
    sqt = sb("sqt", [128, 2, 512])
    fb1 = sb("fb1", [64, 8])

    SCRW = 8880
    scr = sb("scr", [128, SCRW])

    def carve():
        off = [0]

        def alloc(shape, dt=F32):
            n = int(np.prod(shape[1:]))
            nw = n if dt == F32 else (n + 1) // 2
            assert off[0] + nw <= SCRW, (off[0], nw)
            v = scr[0:shape[0], off[0]:off[0] + nw]
            off[0] += nw
            if dt == BF16:
                v = v.bitcast(BF16)
            if len(shape) == 3:
                v = v.rearrange("p (a b) -> p a b", a=shape[1])
            elif len(shape) == 4:
                v = v.rearrange("p (a b c) -> p a b c", a=shape[1], b=shape[2])
            return v
        alloc.off = off
        return alloc

    pst = es.enter_context(nc.psum_tensor("pst", [128, 4096], F32))
    ps = [pst[:, i * 512:(i + 1) * 512] for i in range(8)]

    for ln in P.lanes.values():
        ln.sem = semaphore("c_" + ln.name)
    slot_lanes = [P.dma_lane("slot%d" % i) for i in range(NSLOT)]
    misc_lane = P.dma_lane("misc")
    x_lane = P.dma_lane("xload")
    modw_lanes = [P.dma_lane("modw%d" % i) for i in range(4)]
    fg_lanes = [P.dma_lane("fg%d" % i) for i in range(4)]
    fin_lane = P.dma_lane("fin")
    zt_lane = P.dma_lane("zt")
    wo_lanes = [P.dma_lane("wo%d" % i) for i in range(2)]
    ain_lane = P.dma_lane("ain")
    kc_lane = P.dma_lane("kc")
    vc_lane = P.dma_lane("vc")
    y_lane = P.dma_lane("ystore", store=True)
    kv_lanes = [P.dma_lane("kvst%d" % i, store=True) for i in range(5)]
    dbg_lane = P.dma_lane("dbg", store=True)

    def PPv(name, *idx):
        o, w = PP_OFF[name]
        return o, w

    def ppcol(name, i):
        o, w = PP_OFF[name]
        assert 0 <= i < w
        return pp[:, o + i:o + i + 1]

    st = dict(slot=0, bank=0, zb=0, cv=0, nbank=8, pair=0)

    def load_slab(fn_list_builder, n, eng="pool"):
        i = st["slot"] % NSLOT
        st["slot"] += 1
        key = ("slot", i)
        view = slots[:, i, :]

        def fn(h, view=view):
            return fn_list_builder(h, view)
        P.dma(eng, slot_lanes[i], fn, r=(), w=(key,), n=n)
        return view, key

    def bank():
        b = st["bank"] % st["nbank"]
        st["bank"] += 1
        return b

    def bankpair():
        b = 2 * (st["pair"] % 4)
        st["pair"] += 1
        return b

    def zalloc():
        i = st["zb"] % 4
        st["zb"] += 1
        return i

    def calloc():
        i = st["cv"] % 4
        st["cv"] += 1
        return i

    def psk(b):
        return ("ps", b)

    def pro_loads(h):
        return [
            h.dma_start(out=pp[:], in_=ppd[:, :]),
            h.dma_start(out=condT[:], in_=cond.rearrange("p k c -> p (k c)")),
            h.dma_start(out=ident[:], in_=ident_d[:, :]),
        ]
    P.dma("sp", misc_lane, pro_loads, w=(("c", "pp"), ("c", "cond"), ("c", "ident")), n=3)
    P.op("dve", lambda h: h.memset(onesb[:], 1.0), w=(("c", "onesb"),))
    P.op("dve", lambda h: h.memset(etmp[:, 0, 0:128], 1.0 / 1024.0), w=(("et", 0),))
    P.op("act", lambda h: h.activation(out=onesf[:].bitcast(F32R), in_=etmp[:, 0, 0:128], func=AF.Identity), r=(("et", 0),), w=(("c", "onesf"),))
    P.op("act", lambda h: h.activation(out=condb[:], in_=condT[:], func=AF.Silu), r=(("c", "cond"),), w=(("c", "condb"),))
    o_f = PP_OFF["freq"][0]
    o_b1 = PP_OFF["pb1"][0]
    o_b2 = PP_OFF["pb2"][0]
    for j in range(2):
        P.op("dve", lambda h, j=j: h.tensor_tensor(out=fb1[:, j:j + 1], in0=pp[0:64, o_f + j:o_f + j + 1],
                                                  in1=pp[0:64, o_b1 + j:o_b1 + j + 1], op=ALU.mult),
             r=(("c", "pp"),), w=(("c", "fb1", j),))
        for i in range(2):
            P.op("dve", lambda h, j=j, i=i: h.tensor_tensor(out=fb1[:, 2 + 2 * j + i:3 + 2 * j + i], in0=pp[0:64, o_f + j:o_f + j + 1],
                                                          in1=pp[0:64, o_b2 + 2 * j + i:o_b2 + 2 * j + i + 1], op=ALU.mult),
                 r=(("c", "pp"),), w=(("c", "fb1", 2 + 2 * j + i),))

    o_bm = PP_OFF["bmod"][0]

    class ModStream:
        def __init__(self):
            self.tasks = []
            self.loaded = 0
            self.done = 0

        def add_layer(self, l):
            self.tasks += [(l, f) for f in range(48)]

        def _load(self):
            l, f = self.tasks[self.loaded]
            i = self.loaded % 4
            self.loaded += 1
            P.dma("pool", modw_lanes[i], lambda h, l=l, f=f, i=i: [h.dma_start(
                out=modw[:, i, :].rearrange("p (k n) -> p k n", k=8),
                in_=w_mod[l].rearrange("(k p) n -> p k n", p=128)[:, :, f * 128:(f + 1) * 128])], w=(("modw", i),), n=1)

        def pending(self, l):
            return any(t[0] == l for t in self.tasks[self.done:])

        def step(self, b):
            if self.done >= len(self.tasks):
                return
            while self.loaded < len(self.tasks) and self.loaded < self.done + 3:
                self._load()
            l, f = self.tasks[self.done]
            i = self.done % 4
            self.done += 1
            if self.loaded < len(self.tasks) and self.loaded < self.done + 3:
                self._load()

            def mm(h, i=i, b=b):
                ins = None
                wv = modw[:, i, :].rearrange("p (k n) -> p k n", k=8)
                for k in range(8):
                    ins = h.matmul(ps[b][:, 0:2], lhsT=wv[:, k, :], rhs=condb[:, 2 * k:2 * k + 2], start=(k == 0), stop=(k == 7))
                return ins
            P.op("pe", mm, r=(("modw", i), ("c", "condb")), w=(psk(b),))
            f0 = l * 48 + f
            P.op("act", lambda h, b=b, f0=f0: h.activation(out=mod[:, 2 * f0:2 * f0 + 2], in_=ps[b][:, 0:2], func=AF.Identity,
                                                         bias=pp[:, o_bm + f0:o_bm + f0 + 1]), r=(psk(b), ("c", "pp")), w=(("c", "mod", l),))

        def drain(self, l, b):
            while self.pending(l):
                self.step(b)

    MS = ModStream()

    def modcol(t, l, i, m, c):
        o = ((l * 48) + i * 8 + m) * 2 + c
        return t[:, o:o + 1]

    def modrow(t, l, i, c):
        o = l * 48 + i * 8
        return t[:, :].rearrange("p (x c) -> p x c", c=2)[:, o:o + 8, c]

    def t4(t, l, sub, c):
        o = (l * 2 + sub) * 8
        return t[:, :].rearrange("p (x c) -> p x c", c=2)[:, o:o + 8, c]

    def t4col(t, l, sub, m, c):
        o = ((l * 2 + sub) * 8 + m) * 2 + c
        return t[:, o:o + 1]

    def projbias(l, sub):
        if sub == 1:
            o = PP_OFF["ff_ob"][0] + l * 8
        elif l % 2 == 0:
            o = PP_OFF["hy_ob"][0] + (l // 2) * 8
        else:
            o = PP_OFF["at_ob"][0] + (l // 2) * 8
        return pp[:, o:o + 8]

    o_lng = PP_OFF["lng"][0]
    o_lnb = PP_OFF["lnb"][0]

    def aabb(l, sub):
        nl, nsub = (l, 1) if sub == 0 else (l + 1, 0)
        if nl >= depth:
            return
        lg = pp[:, o_lng + (l * 2 + sub) * 8:o_lng + (l * 2 + sub) * 8 + 8]
        lb = pp[:, o_lnb + (l * 2 + sub) * 8:o_lnb + (l * 2 + sub) * 8 + 8]
        for c in range(2):
            P.op("dve", lambda h, c=c: h.tensor_tensor(out=t4(AAt, l, sub, c), in0=lg, in1=modrow(mod1, nl, 3 * nsub + 1, c), op=ALU.mult),
                 r=(("c", "mod1", nl), ("c", "pp")), w=(("c", "AA", l, sub, c),))
            P.op("dve", lambda h, c=c: h.tensor_tensor(out=t4(BBt, l, sub, c), in0=lb, in1=modrow(mod1, nl, 3 * nsub + 1, c), op=ALU.mult),
                 r=(("c", "mod1", nl), ("c", "pp")), w=(("c", "BB", l, sub, c),))
            P.op("dve", lambda h, c=c: h.tensor_tensor(out=t4(BBt, l, sub, c), in0=t4(BBt, l, sub, c), in1=modrow(mod, nl, 3 * nsub + 0, c), op=ALU.add),
                 r=(("c", "BB", l, sub, c), ("c", "mod", nl)), w=(("c", "BB", l, sub, c),))

    def mod_finish(l):
        P.op("dve", lambda h: h.tensor_scalar(out=mod1[:, l * 96:(l + 1) * 96], in0=mod[:, l * 96:(l + 1) * 96],
                                              scalar1=1.0, scalar2=None, op0=ALU.add),
             r=(("c", "mod", l),), w=(("c", "mod1", l),))
        for sub in range(2):
            for c in range(2):
                P.op("dve", lambda h, sub=sub, c=c: h.tensor_tensor(out=t4(gbt, l, sub, c), in0=modrow(mod1, l, 3 * sub + 2, c),
                                                                    in1=projbias(l, sub), op=ALU.mult),
                     r=(("c", "mod1", l), ("c", "pp")), w=(("c", "gbt", l, sub, c),))
        aabb(l, 0)
        if l >= 1:
            aabb(l - 1, 1)

    P.phase = "prologue"
    MS.add_layer(0)
    for _ in range(16):
        MS.step(bank())
    P.op("dve", lambda h: h.tensor_scalar(out=mod1[:, 0:32], in0=mod[:, 0:32], scalar1=1.0, scalar2=None, op0=ALU.add),
         r=(("c", "mod", 0),), w=(("c", "mod1", 0),))

    def MODK(l):
        return (("c", "mod", l), ("c", "mod1", l))


    def xk(m, t):
        return ("xT", m, t)

    def hk(m, t):
        return ("hT", m, t)

    def tsl(t):
        return slice(t * 512, (t + 1) * 512)

    def tokv(ap2d, g):
        return ap2d.rearrange("p (s t) -> p s t", s=GROUPS[g]["nseq"])

    def psv(b, g):
        if g == 0:
            return ps[b][:, :].rearrange("p (s t) -> p s t", s=2)
        return ps[b][:, :].rearrange("p (s t) -> p s t", s=1)

    def conv_scalars(n, o_w, stride, o_b, o_cb):
        w0, w1, w2 = (pp[:, o_w + i * stride:o_w + i * stride + n] for i in range(3))
        bi, cb = pp[:, o_b:o_b + n], pp[:, o_cb:o_cb + n]
        K = (("dsc",),)
        R_ = (("c", "pp"),)
        P.op("dve", lambda h: h.tensor_tensor(out=dsc[:, 0, 0:n], in0=w0, in1=w1, op=ALU.add), r=R_, w=K)
        P.op("dve", lambda h: h.tensor_tensor(out=dsc[:, 0, 0:n], in0=dsc[:, 0, 0:n], in1=w2, op=ALU.add), r=R_ + K, w=K)
        P.op("dve", lambda h: h.tensor_tensor(out=dsc[:, 0, 0:n], in0=dsc[:, 0, 0:n], in1=bi, op=ALU.mult), r=R_ + K, w=K)
        P.op("dve", lambda h: h.tensor_tensor(out=dsc[:, 0, 0:n], in0=dsc[:, 0, 0:n], in1=cb, op=ALU.add), r=R_ + K, w=K)
        P.op("dve", lambda h: h.scalar_tensor_tensor(out=dsc[:, 1, 0:n], in0=w0, scalar=-1.0, in1=bi, op0=ALU.mult, op1=ALU.mult), r=R_ + K, w=K)
        P.op("dve", lambda h: h.scalar_tensor_tensor(out=dsc[:, 2, 0:n], in0=w2, scalar=-1.0, in1=bi, op0=ALU.mult, op1=ALU.mult), r=R_ + K, w=K)

    def upconv(g, lhs_fn, wkey, w0, w1, w2, ci, out2d, okey, defer=None, after=None):
        cbt, nwb0, nwb2 = dsc[:, 0, ci:ci + 1], dsc[:, 1, ci:ci + 1], dsc[:, 2, ci:ci + 1]
        DK = (("dsc",), ("c", "pp"))
        b0 = bankpair()
        PK = (psk(b0), psk(b0 + 1))

        def mm_t(t):
            b = b0 + t

            def mm(h):
                ins = None
                for k in range(8):
                    ins = h.matmul(ps[b][:, :], lhsT=lhs_fn(k), rhs=hT[:, k, tsl(t)], start=(k == 0), stop=(k == 7))
                return ins
            P.op("pe", mm, r=(wkey,) + tuple(hk(k, t) for k in range(8)), w=(psk(b),))

        def post():
            pv2 = pst[:, b0 * 512:(b0 + 2) * 512]
            if g == 0:
                pv = pv2.rearrange("p (s t) -> p s t", s=4)
                ov = tokv(out2d, 0)
                o_hi, p_lo, o_lo, p_hi = ov[:, :, 1:256], pv[:, :, 0:255], ov[:, :, 0:255], pv[:, :, 1:256]
                e0, e1 = ov[:, :, 0], ov[:, :, 255]
            else:
                pv = pv2
                ov = out2d
                o_hi, p_lo, o_lo, p_hi = ov[:, 1:1024], pv[:, 0:1023], ov[:, 0:1023], pv[:, 1:1024]
                e0, e1 = ov[:, 0:1], ov[:, 1023:1024]
            P.op("act", lambda h: h.activation(out=ov, in_=pv, func=AF.Identity, scale=w1, bias=cbt), r=PK + DK, w=(okey,))
            P.op("dve", lambda h: h.scalar_tensor_tensor(out=o_hi, in0=p_lo, scalar=w0, in1=o_hi, op0=ALU.mult, op1=ALU.add), r=PK + (okey,) + DK, w=(okey,))
            P.op("dve", lambda h: h.scalar_tensor_tensor(out=o_lo, in0=p_hi, scalar=w2, in1=o_lo, op0=ALU.mult, op1=ALU.add), r=PK + (okey,) + DK, w=(okey,))
            P.op("dve", lambda h: h.tensor_scalar(out=e0, in0=e0, scalar1=nwb0, scalar2=None, op0=ALU.add), r=(okey,) + DK, w=(okey,))
            P.op("dve", lambda h: h.tensor_scalar(out=e1, in0=e1, scalar1=nwb2, scalar2=None, op0=ALU.add), r=(okey,) + DK, w=(okey,))
            if after is not None:
                after()

        if defer is not None:
            mm_t(0)
            defer.append((lambda: mm_t(1), post))
        else:
            mm_t(0)
            mm_t(1)
            post()

    def flush_deferred(defer):
        for m1, _ in defer:
            m1()
        for _, po in defer:
            po()
        del defer[:]

    def dbg_dump(tag):
        if dbg is not None and dbg == tag:
            def f(h):
                return [h.dma_start(out=yT[0, m], in_=xT[:, m, :]) for m in range(8)] + \
                       [h.dma_start(out=yT[1, m].bitcast(BF16)[:, 0:1024], in_=hT[:, m, :]) for m in range(8)]
            P.dma("sp", dbg_lane, f, r=tuple(xk(m, t) for m in range(8) for t in range(2)) + tuple(hk(m, t) for m in range(8) for t in range(2)), n=16)
            return True
        return False

    def proj_ln(g, l, sub, nk, src_fn, src_keys_fn, wload_fn, tail_hook=None):
        c = g
        sum_b = [4, 5]
        sq_b = [6, 7]
        st["nbank"] = 4
        pend = []

        def flush(n):
            while len(pend) > n:
                m, t, si = pend.pop(0)
                P.op("pe", lambda h, m=m, t=t, si=si: h.matmul(ps[sum_b[t]][:, :], lhsT=onesf[:, :].bitcast(F32R), rhs=xrt[:, si, :].bitcast(F32R),
                                                             start=(m == 0), stop=(m == 7)),
                     r=(("xr", si), ("c", "onesf")), w=(psk(sum_b[t]),))
                P.op("pe", lambda h, m=m, t=t, si=si: h.matmul(ps[sq_b[t]][:, :], lhsT=onesf[:, :].bitcast(F32R), rhs=sqt[:, si, :].bitcast(F32R),
                                                             start=(m == 0), stop=(m == 7)),
                     r=(("sq", si), ("c", "onesf")), w=(psk(sq_b[t]),))

        has_next = not (sub == 1 and l == depth - 1)
        o_g = o_lng + (l * 2 + sub) * 8
        o_b = o_lnb + (l * 2 + sub) * 8
        cnt = dict(it=0)

        def ln_stats(t):
            P.op("act", lambda h: h.activation(out=lnst[:, 0, :], in_=ps[sum_b[t]][:, :], func=AF.Square),
                 r=(psk(sum_b[t]),), w=(("ln", 0),))
            P.op("dve", lambda h: h.scalar_tensor_tensor(out=lnst[:, 0, :], in0=ps[sq_b[t]][:, :], scalar=LN_EPS, in1=lnst[:, 0, :],
                                                         op0=ALU.add, op1=ALU.subtract),
                 r=(psk(sq_b[t]), ("ln", 0)), w=(("ln", 0),))
            P.op("act", lambda h: h.activation(out=lnst[:, 0, :], in_=lnst[:, 0, :], func=AF.Sqrt),
                 r=(("ln", 0),), w=(("ln", 0),))
            P.op("dve", lambda h: h.reciprocal(out=lnst[:, 0, :], in_=lnst[:, 0, :]), r=(("ln", 0),), w=(("ln", 0),))
            P.op("dve", lambda h: h.tensor_tensor(out=lnst[:, 1, :], in0=ps[sum_b[t]][:, :], in1=lnst[:, 0, :], op=ALU.mult),
                 r=(psk(sum_b[t]), ("ln", 0)), w=(("ln", 1),))

        def ln_apply(m, t, eng="dve"):
            ei = (cnt["it"] % 4) if t == 1 else (2 + cnt["it"] % 2)
            cnt["it"] += 1
            P.op(eng, lambda h: h.tensor_tensor(out=etmp[:, ei, :], in0=xT[:, m, tsl(t)], in1=lnst[:, 0, :], op=ALU.mult),
                 r=(xk(m, t), ("ln", 0)), w=(("et", ei),))
            P.op(eng, lambda h: h.tensor_tensor(out=etmp[:, ei, :], in0=etmp[:, ei, :], in1=lnst[:, 1, :], op=ALU.subtract),
                 r=(("et", ei), ("ln", 1)), w=(("et", ei),))
            if has_next:
                P.op("act", lambda h: h.activation(out=hT[:, m, tsl(t)], in_=etmp[:, ei, :], func=AF.Identity,
                                                   scale=t4col(AAt, l, sub, m, c), bias=t4col(BBt, l, sub, m, c)),
                     r=(("et", ei), ("c", "AA", l, sub, c), ("c", "BB", l, sub, c)), w=(hk(m, t),))
            P.op("act", lambda h: h.activation(out=xT[:, m, tsl(t)], in_=etmp[:, ei, :], func=AF.Identity,
                                               scale=pp[:, o_g + m:o_g + m + 1], bias=pp[:, o_b + m:o_b + m + 1]),
                 r=(("et", ei), ("c", "pp")), w=(xk(m, t),))

        it = 0
        for t in range(2):
            for m in range(8):
                lhs_fn, wkey = wload_fn(m)
                b = bank()

                def mm(h, b=b, t=t, lhs_fn=lhs_fn):
                    ins = None
                    for k in range(nk):
                        ins = h.matmul(ps[b][:, :], lhsT=lhs_fn(k), rhs=src_fn(k, t), start=(k == 0), stop=(k == nk - 1))
                    return ins
                P.op("pe", mm, r=(wkey,) + tuple(src_keys_fn(k, t) for k in range(nk)), w=(psk(b),))
                flush(1)
                ei = it % 2
                si = it % 2
                it += 1
                P.op("act", lambda h, b=b, m=m, ei=ei: h.activation(out=etmp[:, ei, :], in_=ps[b][:, :], func=AF.Identity,
                                                                    scale=modcol(mod1, l, 3 * sub + 2, m, c), bias=t4col(gbt, l, sub, m, c)),
                     r=(psk(b), ("c", "gbt", l, sub, c)) + MODK(l), w=(("et", ei),))
                P.op("dve", lambda h, m=m, t=t, ei=ei: h.scalar_tensor_tensor(out=xT[:, m, tsl(t)], in0=xT[:, m, tsl(t)], scalar=ALPHA,
                                                                            in1=etmp[:, ei, :], op0=ALU.mult, op1=ALU.add),
                     r=(xk(m, t), ("et", ei)), w=(xk(m, t),))
                P.op("act", lambda h, m=m, t=t, si=si: h.activation(out=sqt[:, si, :].bitcast(F32R), in_=xT[:, m, tsl(t)], func=AF.Square),
                     r=(xk(m, t),), w=(("sq", si),))
                P.op("act", lambda h, m=m, t=t, si=si: h.activation(out=xrt[:, si, :].bitcast(F32R), in_=xT[:, m, tsl(t)], func=AF.Identity),
                     r=(xk(m, t),), w=(("xr", si),))
                pend.append((m, t, si))
                if t == 1:
                    ln_apply(m, 0)
            flush(0)
            if t == 0:
                ln_stats(0)
        st["nbank"] = 8
        if tail_hook is not None:
            tail_hook()
        ln_stats(1)
        for m in range(8):
            ln_apply(m, 1)

    def mixer_tail(g, l):
        if not (g == 0 and l + 1 < depth):
            return None

        return None

    def wload_sq(wmat, nk_):
        def wl(m):
            def ld(h, view):
                return [h.dma_start(out=view[:, 0:nk_ * 128].rearrange("p (k n) -> p k n", k=nk_),
                                    in_=wmat.rearrange("(k p) n -> p k n", p=128)[:, :, m * 128:(m + 1) * 128])]
            view, key = load_slab(ld, 1)
            wv = view[:, 0:nk_ * 128].rearrange("p (k n) -> p k n", k=nk_)
            return (lambda k: wv[:, k, :]), key
        return wl

    def ffn(g, l, hook=None, hook_end=None):
        P.phase = "g%d l%d ffn-up" % (g, l)
        P.fence("big")
        act = big[:, 0:22 * 1024].rearrange("p (c t) -> p c t", c=22)
        o_ib = PP_OFF["ff_in_b"][0] + l * 44
        o_cw = PP_OFF["ff_cw"][0] + l * 132
        o_cb = PP_OFF["ff_cb"][0] + l * 44
        conv_scalars(44, o_cw, 44, o_ib, o_cb)
        def ffn_load(cp):
            def ld(h, view, cp=cp):
                dv = view[:, 0:4096].rearrange("p (k a n) -> p k a n", k=8, a=2)
                sv = ff_in_w[l].rearrange("(k p) (a n) -> p k a n", p=128, a=2)
                return [h.dma_start(out=dv[:, :, a, :], in_=sv[:, :, a, cp * 256:(cp + 1) * 256]) for a in range(2)]
            return load_slab(ld, 2)
        PF = 2
        loaded = [ffn_load(i) for i in range(PF)]
        for cp in range(11):
            if hook is not None:
                hook(cp)
            if cp + PF < 11:
                loaded.append(ffn_load(cp + PF))
            view, wkey = loaded[cp]
            wv = view[:, 0:4096].rearrange("p (k a n) -> p k a n", k=8, a=2)
            dq = [] if cp == 0 else None
            for cc in range(2):
                ch = cp * 2 + cc
                cvi = [calloc(), calloc()]

                def tail(ch=ch, cvi=cvi):
                    P.op("act", lambda h: h.activation(out=cvb[:, cvi[0], :], in_=cvb[:, cvi[0], :], func=AF.Gelu_apprx_tanh),
                         r=(("cv", cvi[0]),), w=(("cv", cvi[0]),))
                    P.op(MULT_ENG, lambda h: h.tensor_tensor(out=act[:, ch, :], in0=cvb[:, cvi[0], :], in1=cvb[:, cvi[1], :], op=ALU.mult),
                         r=(("cv", cvi[0]), ("cv", cvi[1])), w=(("big", "act", ch),))
                for a in range(2):
                    ci = a * 22 + ch
                    cv = cvi[a]
                    upconv(g, (lambda k, a=a, cc=cc, wv=wv: wv[:, k, a, cc * 128:(cc + 1) * 128]), wkey,
                           pp[:, o_cw + ci:o_cw + ci + 1], pp[:, o_cw + 44 + ci:o_cw + 44 + ci + 1], pp[:, o_cw + 88 + ci:o_cw + 88 + ci + 1],
                           ci, cvb[:, cv, :], ("cv", cv), defer=dq, after=(tail if a == 1 else None))
            if dq:
                flush_deferred(dq)
        if hook_end is not None:
            hook_end()
        P.phase = "g%d l%d ffn-down+ln" % (g, l)
        proj_ln(g, l, 1, 22, lambda k, t: act[:, k, tsl(t)], lambda k, t: ("big", "act", k), wload_sq(ff_out_w[l], 22))

    def hyena(g, l):
        j = l // 2
        L = GROUPS[g]["L"]
        nseq = GROUPS[g]["nseq"]
        nj = L // 128
        if g == 0 and l + 1 < depth:
            MS.add_layer(l + 1)
        P.phase = "g%d l%d hy-inproj" % (g, l)
        P.fence("big")
        x0 = big[:, 0:8192].rearrange("p (c t) -> p c t", c=8)
        uu = big[:, 8192:16384].rearrange("p (c t) -> p c t", c=8)
        uT = big[:, 16384:18432].rearrange("p (k n) -> p k n", k=8)
        ksd_hi = big[:, 18432:18432 + 2 * nj * 256].rearrange("p (a k n) -> p a k n", a=2, k=nj)
        cvflat = cvb[:, :, :].rearrange("p a b -> p (a b)").bitcast(BF16)
        ks_lo = cvflat[:, 0:nj * 256].rearrange("p (k n) -> p k n", k=nj)
        kd_lo = cvflat[:, 2048:2048 + nj * 256].rearrange("p (k n) -> p k n", k=nj)
        ksd_lo = [ks_lo, kd_lo]
        CVK = tuple(("cv", i) for i in range(4))
        KSK = [CVK, CVK]
        o_ib = PP_OFF["hy_in_b"][0] + j * 24
        o_sw = PP_OFF["hy_sw"][0] + j * 72
        o_sb = PP_OFF["hy_sb"][0] + j * 24
        o_fb = PP_OFF["hy_fb"][0] + j * 8
        o_tn = PP_OFF["tn%d" % L][0]
        o_m0 = PP_OFF["mask0"][0]
        if True:
            P.fence("scr")
            lsb = carve()
            w1s = lsb([33, 64])
            w2s = lsb([64, 2, 64])
            wos = lsb([64, 2, 2, 256])
            fh = lsb([64, 2, 1024])
            ctmp = lsb([128, 4, 256])
            ftmp = ctmp[0:64, :, :].rearrange("p a b -> p (a b)").rearrange("p (a b) -> p a b", a=2)
            deltab = lsb([128, 2, 256])
            dect = lsb([128, 2, 256])
            kft = lsb([128, 2, 512])
            ksb = lsb([128, 2, 512])
            zTs = ksb[0:33, :, :].rearrange("p a b -> p (a b)")
            ZTK = (("scr", "K", 0), ("scr", "K", 1))
            ybuf = lsb([128, 2, 512], BF16)

            def fl(h):
                return [h.dma_start(out=w1s[:], in_=pos_w1[j]),
                        h.dma_start(out=w2s[:], in_=pos_w2[2 * j:2 * j + 2].rearrange("i k n -> k i n")),
]
            P.dma("sp", fin_lane, fl, w=(("scr", "in"),), n=2)
            P.dma("sp", zt_lane, lambda h: [h.dma_start(out=zTs[:, 0:L], in_=zT_d[L][:, :])], w=ZTK, n=1)

            conv_scalars(24, o_sw, 24, o_ib, o_sb)
            def hy_load(c):
                def ld(h, view, c=c):
                    dv = view[:, 0:3072].rearrange("p (k a n) -> p k a n", k=8, a=3)
                    sv = hy_in_w[j].rearrange("(k p) (a n) -> p k a n", p=128, a=3)
                    return [h.dma_start(out=dv[:, :, a, :], in_=sv[:, :, a, c * 128:(c + 1) * 128]) for a in range(3)]
                return load_slab(ld, 3)
            hloaded = [hy_load(0), hy_load(1)]
            for c in range(8):
                if c + 2 < 8:
                    hloaded.append(hy_load(c + 2))
                view, wkey = hloaded[c]
                wv = view[:, 0:3072].rearrange("p (k a n) -> p k a n", k=8, a=3)
                if g == 0 and l == 0:
                    for _ in range(4):
                        MS.step(7)
                    if c == 7:
                        MS.drain(0, 7)
                        mod_finish(0)
                cvi = [calloc(), calloc(), calloc()]
                dq = [] if c == 0 else None

                def tail(c=c, cvi=cvi):
                    P.op("act", lambda h: h.activation(out=x0[:, c, :], in_=cvb[:, cvi[0], :], func=AF.Identity),
                         r=(("cv", cvi[0]),), w=(("big", "x0", c),))
                    P.op(MULT_ENG, lambda h: h.tensor_tensor(out=uu[:, c, :], in0=cvb[:, cvi[1], :], in1=cvb[:, cvi[2], :], op=ALU.mult),
                         r=(("cv", cvi[1]), ("cv", cvi[2])), w=(("big", "u", c),))
                for a in range(3):
                    ci = a * 8 + c
                    cv = cvi[a]
                    upconv(g, (lambda k, a=a, wv=wv: wv[:, k, a, :]), wkey,
                           pp[:, o_sw + ci:o_sw + ci + 1], pp[:, o_sw + 24 + ci:o_sw + 24 + ci + 1], pp[:, o_sw + 48 + ci:o_sw + 48 + ci + 1],
                           ci, cvb[:, cv, :], ("cv", cv), defer=dq, after=(tail if a == 2 else None))
                if dq:
                    flush_deferred(dq)

            P.phase = "g%d l%d hy-filter-mlp" % (g, l)
            FT0 = (("scr", "c", 0), ("scr", "c", 1))
            FT1 = (("scr", "c", 2), ("scr", "c", 3))

            def sin_layer(src_ps_fn, nt, dst, fcol, bcol, rk):
                for t in range(nt):
                    n = min(512, L)
                    sl = slice(t * 512, t * 512 + n)
                    b = src_ps_fn(t)
                    P.op("dve", lambda h, b=b, n=n: h.tensor_scalar(out=ftmp[:, 0, 0:n], in0=ps[b][0:64, 0:n], scalar1=pp[0:64, o_f + j:o_f + j + 1],
                                                                   scalar2=fb1[:, bcol:bcol + 1], op0=ALU.mult, op1=ALU.add),
                         r=(psk(b), ("c", "pp"), ("c", "fb1", bcol)), w=FT0)
                    P.op("dve", lambda h, n=n: h.tensor_scalar(out=ftmp[:, 1, 0:n], in0=ftmp[:, 0, 0:n], scalar1=float(1.0 / TWO_PI), scalar2=MAGIC,
                                                              op0=ALU.mult, op1=ALU.add), r=FT0, w=FT1)
                    P.op("dve", lambda h, n=n: h.tensor_scalar(out=ftmp[:, 1, 0:n], in0=ftmp[:, 1, 0:n], scalar1=MAGIC, scalar2=-TWO_PI,
                                                              op0=ALU.subtract, op1=ALU.mult), r=FT1, w=FT1)
                    P.op("dve", lambda h, n=n: h.tensor_tensor(out=ftmp[:, 0, 0:n], in0=ftmp[:, 0, 0:n], in1=ftmp[:, 1, 0:n], op=ALU.add),
                         r=FT0 + FT1, w=FT0)
                    P.op("act", lambda h, n=n, sl=sl: h.activation(out=dst[:, sl], in_=ftmp[:, 0, 0:n], func=AF.Sin),
                         r=FT0, w=(rk,))
            nt = max(1, L // 512)
            n512 = min(512, L)

            def l1(t):
                b = bank()
                P.op("pe", lambda h, b=b, t=t: h.matmul(ps[b][0:64, 0:n512], lhsT=w1s[:, :], rhs=zTs[:, t * 512:t * 512 + n512], start=True, stop=True),
                     r=(("scr", "in"),) + ZTK, w=(psk(b),))
                return b
            sin_layer(l1, nt, fh[:, 0, :], j, j, ("scr", "h", 0))
            if dbg == ("h1", g, l):
                P.dma("sp", dbg_lane, lambda h: [h.dma_start(out=yT[0, 0][0:64, 0:L], in_=fh[:, 0, 0:L])], r=(("scr", "h", 0),), n=1)
                raise _Stop()
            for i in range(2):
                def l2(t, i=i):
                    b = bank()
                    P.op("pe", lambda h, b=b, t=t, i=i: h.matmul(ps[b][0:64, 0:n512], lhsT=w2s[:, i, :], rhs=fh[:, i, t * 512:t * 512 + n512], start=True, stop=True),
                         r=(("scr", "in"), ("scr", "h", i)), w=(psk(b),))
                    return b
                sin_layer(l2, nt, fh[:, (i + 1) % 2, :], j, 2 + 2 * j + i, ("scr", "h", (i + 1) % 2))
                if dbg == ("h2", g, l) and i == 0:
                    P.dma("sp", dbg_lane, lambda h: [h.dma_start(out=yT[0, 0][0:64, 0:L], in_=fh[:, 1, 0:L]),
                                                      h.dma_start(out=yT[0, 1][0:64, 0:L], in_=fh[:, 0, 0:L])], r=(("scr", "h", 0), ("scr", "h", 1)), n=2)
                    raise _Stop()
            h3 = fh[:, 0, :]
            if dbg == ("h3", g, l):
                P.dma("sp", dbg_lane, lambda h: [h.dma_start(out=yT[0, 0][0:64, 0:L], in_=h3[:, 0:L])], r=(("scr", "h", 0),), n=1)
                raise _Stop()

            ACC = [0, 1, 2, 3]
            P.phase = "g%d l%d hy-dft" % (g, l)
            for ct in range(4):
                c0 = ct * 256
                for tc in range(8):
                    b = 7

                    def tr(h, tc=tc, ct=ct):
                        psb = ps[7][:, :].bitcast(BF16)
                        ins = None
                        for cc in range(2):
                            ins = h.transpose(psb[:, cc * 128:(cc + 1) * 128], uu[:, 2 * ct + cc, tc * 128:(tc + 1) * 128], ident[:, :])
                        return ins
                    P.op("pe", tr, r=(("big", "u", 2 * ct), ("big", "u", 2 * ct + 1), ("c", "ident")), w=(psk(7),))
                    P.op("act", lambda h, tc=tc: h.activation(out=uT[:, tc, :], in_=ps[7][:, :].bitcast(BF16)[:, 0:256], func=AF.Identity),
                         r=(psk(7),), w=(("big", "uT", tc),))
                wb = ct % 2
                P.dma("sp", wo_lanes[wb], lambda h, wb=wb, c0=c0: [
                    h.dma_start(out=wos[:, wb, :, :], in_=pos_wout[j].rearrange("k (a n) -> k a n", a=2)[:, :, c0:c0 + 256]),
                    h.dma_start(out=deltab[:, wb, :], in_=bass.AP(delta_t, c0, [[0, 128], [1, 256]]))], w=(("scr", "wo", wb),), n=2)
                for tc in range(nj):
                    b = 6
                    P.op("pe", lambda h, tc=tc, wb=wb: h.matmul(ps[6][:, :], lhsT=h3[:, tc * 128:(tc + 1) * 128],
                                                             rhs=wos[:, wb, :, :].rearrange("k a n -> k (a n)"), start=True, stop=True),
                         r=(("scr", "h", 0), ("scr", "wo", wb)), w=(psk(6),))
                    di = tc % 2
                    P.op("act", lambda h, tc=tc, di=di, wb=wb: h.activation(out=dect[:, di, :], in_=deltab[:, wb, :], func=AF.Exp,
                                                                          scale=pp[:, o_tn + tc:o_tn + tc + 1]),
                         r=(("scr", "wo", wb), ("c", "pp")), w=(("scr", "dec", di),))
                    P.op("dve", lambda h, di=di: h.tensor_tensor(out=kft[:, di, 0:256], in0=ps[6][:, 0:256], in1=dect[:, di, :], op=ALU.mult),
                         r=(psk(6), ("scr", "dec", di)), w=(("scr", "kf", di),))
                    if tc == 0:
                        P.op("dve", lambda h, di=di: h.tensor_scalar(out=dect[:, di, :], in0=dect[:, di, :], scalar1=pp[:, o_m0:o_m0 + 1], scalar2=None,
                                                                    op0=ALU.mult), r=(("scr", "dec", di), ("c", "pp")), w=(("scr", "dec", di),))
                    P.op("dve", lambda h, di=di: h.tensor_tensor(out=kft[:, di, 256:512], in0=ps[6][:, 256:512], in1=dect[:, di, :], op=ALU.mult),
                         r=(psk(6), ("scr", "dec", di)), w=(("scr", "kb", di),))
                    for a_, op_ in ((0, ALU.add), (1, ALU.subtract)):
                        P.op("dve", lambda h, di=di, a_=a_, op_=op_, tc=tc: h.tensor_tensor(out=ksd_hi[:, a_, tc, :], in0=kft[:, di, 0:256], in1=kft[:, di, 256:512], op=op_),
                             r=(("scr", "kf", di), ("scr", "kb", di)), w=(("big", "khi", a_),))
                if dbg == ("ks", g, l) and ct == 0:
                    def dks(h):
                        return [h.dma_start(out=yT[0, a_].bitcast(BF16)[:, 0:nj * 256], in_=ksd_hi[:, a_].rearrange("p k n -> p (k n)")) for a_ in range(2)]
                    P.dma("sp", dbg_lane, dks, r=(("big", "khi", 0), ("big", "khi", 1)), n=2)
                    raise _Stop()
                yi = 0
                fgi = [0]
                pendI = [None]
                for jf in range(nj):
                    fb_ = fgi[0] % 2
                    fgi[0] += 1
                    cvF = cvb[:, fb_, :].bitcast(BF16)
                    cvG = cvb[:, 2 + fb_, :].bitcast(BF16)
                    fkey, gkey = ("cv", fb_), ("cv", 2 + fb_)
                    P.dma("sp", fg_lanes[fb_], lambda h, jf=jf, cvF=cvF: [h.dma_start(
                        out=cvF[:, 0:nj * 256].rearrange("p (k n) -> p k n", k=nj), in_=F_d[L][jf][:, 0])], w=(fkey,), n=1)
                    P.dma("sp", fg_lanes[2 + fb_], lambda h, jf=jf, cvG=cvG: [h.dma_start(
                        out=cvG[:, 0:2 * L].rearrange("p (a n) -> p a n", a=2), in_=G_d[L][jf][:, 0])], w=(gkey,), n=1)
                    Fv = cvF[:, 0:nj * 256].rearrange("p (k n) -> p k n", k=nj)
                    Gv = cvG[:, 0:2 * L].rearrange("p (a n) -> p a n", a=2)

                    def mmK(h, Fv=Fv):
                        ins = None
                        for a in range(2):
                            for tc in range(nj):
                                ins = h.matmul(ps[6][:, a * 256:(a + 1) * 256], lhsT=Fv[:, tc, a * 128:(a + 1) * 128], rhs=ksd_hi[:, a, tc, :],
                                               start=(tc == 0), stop=(tc == nj - 1))
                        return ins
                    P.op("pe", mmK, r=(fkey, ("big", "khi", 0), ("big", "khi", 1)), w=(psk(6),))
                    kb_ = jf % 2
                    P.op("act", lambda h, kb_=kb_: h.activation(out=ksb[:, kb_, :], in_=ps[6][:, :], func=AF.Identity), r=(psk(6),), w=(("scr", "K", kb_),))
                    for s in range(nseq):
                        if g == 0 and l + 1 < depth:
                            MS.step(7)
                        ub = 4 + (yi % 2)

                        def mmU(h, Fv=Fv, s=s, ub=ub):
                            ins = None
                            for a in range(2):
                                for tc in range(nj):
                                    ins = h.matmul(ps[ub][:, a * 256:(a + 1) * 256], lhsT=Fv[:, tc, a * 128:(a + 1) * 128], rhs=uT[:, s * nj + tc, :],
                                                   start=(tc == 0), stop=(tc == nj - 1))
                            return ins
                        P.op("pe", mmU, r=(fkey,) + tuple(("big", "uT", s * nj + tc) for tc in range(nj)), w=(psk(ub),))
                        prev = pendI.pop(0)
                        if prev is not None:
                            prev()
                        yb = yi % 2
                        yi += 1
                        Ure, Uim = ps[ub][:, 0:256], ps[ub][:, 256:512]
                        Kre, Kim = ksb[:, kb_, 0:256], ksb[:, kb_, 256:512]
                        rk = (psk(ub), ("scr", "K", kb_))
                        P.op("dve", lambda h, Ure=Ure, Kre=Kre: h.tensor_tensor(out=ctmp[:, 0, :], in0=Ure, in1=Kre, op=ALU.mult), r=rk, w=(("scr", "c", 0),))
                        P.op("dve", lambda h, Uim=Uim, Kim=Kim: h.tensor_tensor(out=ctmp[:, 1, :], in0=Uim, in1=Kim, op=ALU.mult), r=rk, w=(("scr", "c", 1),))
                        P.op("dve", lambda h, yb=yb: h.tensor_tensor(out=ybuf[:, yb, 0:256], in0=ctmp[:, 0, :], in1=ctmp[:, 1, :], op=ALU.subtract),
                             r=(("scr", "c", 0), ("scr", "c", 1)), w=(("scr", "yre", yb),))
                        P.op("dve", lambda h, Ure=Ure, Kim=Kim: h.tensor_tensor(out=ctmp[:, 2, :], in0=Ure, in1=Kim, op=ALU.mult), r=rk, w=(("scr", "c", 2),))
                        P.op("dve", lambda h, Uim=Uim, Kre=Kre: h.tensor_tensor(out=ctmp[:, 3, :], in0=Uim, in1=Kre, op=ALU.mult), r=rk, w=(("scr", "c", 3),))
                        P.op("dve", lambda h, yb=yb: h.tensor_tensor(out=ybuf[:, yb, 256:512], in0=ctmp[:, 2, :], in1=ctmp[:, 3, :], op=ALU.add),
                             r=(("scr", "c", 2), ("scr", "c", 3)), w=(("scr", "yim", yb),))

                        def mmI(h, Gv=Gv, s=s, yb=yb, jf=jf):
                            ins = None
                            for cc in range(2):
                                for a in range(2):
                                    lhs = ybuf[:, yb, a * 256 + cc * 128:a * 256 + (cc + 1) * 128]
                                    first = (jf == 0 and a == 0)
                                    last = (jf == nj - 1 and a == 1)
                                    if L == 1024:
                                        for tt in range(2):
                                            ins = h.matmul(ps[ACC[cc * 2 + tt]][:, :], lhsT=lhs, rhs=Gv[:, a, tt * 512:(tt + 1) * 512], start=first, stop=last)
                                    else:
                                        ins = h.matmul(ps[ACC[s]][:, cc * 256:(cc + 1) * 256], lhsT=lhs, rhs=Gv[:, a, :],
                                                       start=(first and cc == 0), stop=(last and cc == 1))
                            return ins
                        nxt = (lambda mmI=mmI, gkey=gkey, yb=yb: P.op("pe", mmI, r=(gkey, ("scr", "yre", yb), ("scr", "yim", yb)), w=tuple(psk(a_) for a_ in ACC)))
                        pendI.append(nxt)
                pendI.pop(0)()
                for cc in range(2):
                    c = 2 * ct + cc
                    for q in range(2 if L == 1024 else 4):
                        if L == 1024:
                            accap = ps[ACC[cc * 2 + q]][:, :]
                            tok = slice(q * 512, (q + 1) * 512)
                            n = 512
                            ab = ACC[cc * 2 + q]
                        else:
                            accap = ps[ACC[q]][:, cc * 256:(cc + 1) * 256]
                            tok = slice(q * 256, (q + 1) * 256)
                            n = 256
                            ab = ACC[q]
                        ei = (cc * 4 + q) % 4
                        P.op("dve", lambda h, c=c, tok=tok, accap=accap, ei=ei, n=n: h.scalar_tensor_tensor(
                            out=etmp[:, ei, 0:n], in0=uu[:, c, tok], scalar=pp[:, o_fb + c:o_fb + c + 1], in1=accap, op0=ALU.mult, op1=ALU.add),
                            r=(("big", "u", c), psk(ab), ("c", "pp")), w=(("et", ei),))
                        P.op("dve", lambda h, c=c, tok=tok, ei=ei, n=n: h.tensor_tensor(out=x0[:, c, tok], in0=etmp[:, ei, 0:n], in1=x0[:, c, tok], op=ALU.mult),
                             r=(("et", ei), ("big", "x0", c)), w=(("big", "x0", c),))
            P.phase = "g%d l%d hy-out+ln" % (g, l)
            proj_ln(g, l, 0, 8, lambda k, t: x0[:, k, tsl(t)], lambda k, t: ("big", "x0", k), wload_sq(hy_out_w[j], 8), tail_hook=mixer_tail(g, l))

    def attention(g, l):
        j = l // 2
        latent = (g == 1)
        if g == 0 and l + 1 < depth:
            MS.add_layer(l + 1)
        P.phase = "g%d l%d at-qkv" % (g, l)
        P.fence("big")
        P.fence("scr")
        qT = big[:, 0:8192].rearrange("p (h t) -> p h t", h=8)
        oT = big[:, 8192:16384].rearrange("p (h t) -> p h t", h=8)
        kT = big[:, 16384:19456].rearrange("p (h t) -> p h t", h=2)
        Vv = big[:, 19456:22528].rearrange("p (k n) -> p k n", k=12)
        lsb = carve()
        qkvb = lsb([128, 1536])
        qgain = lsb([128, 128])
        kgain = lsb([128, 128])
        rope = lsb([128, 8, 2, 64])
        kcb = lsb([128, 4, 256], BF16)
        mark = lsb.off[0]
        NQ = 4
        qf = lsb([128, NQ, 512])
        qsq = lsb([128, 2, 512])
        qst = lsb([128, NQ, 4])
        qr = lsb([128, NQ, 512], BF16)
        rtmp = lsb([128, 4, 256])
        lsb.off[0] = mark
        pT = lsb([128, 3, 512], BF16)
        rdt = lsb([128, 2, 512])
        scale = float(128 ** -0.5)

        def al(h):
            return [h.dma_start(out=qkvb[:], in_=bass.AP(qkvb_t, j * 1536, [[0, 128], [1, 1536]])),
                    h.dma_start(out=qgain[:], in_=bass.AP(qgain_t, j * 128, [[0, 128], [1, 128]])),
                    h.dma_start(out=kgain[:], in_=bass.AP(kgain_t, j * 128, [[0, 128], [1, 128]])),
                    h.dma_start(out=rope[:], in_=rope_d[:, :, :, :])]
        P.dma("sp", ain_lane, al, w=(("scr", "ain"),), n=4)
        if latent:
            P.dma("pool", kc_lane, lambda h: [h.dma_start(out=kcb[:], in_=cache_k[j].rearrange("(c p) n -> p c n", p=128))],
                  w=(("scr", "kc"),), n=1)
            P.dma("pool", vc_lane, lambda h: [h.dma_start(out=Vv[:, 8:12, :], in_=cache_v[j].rearrange("(c p) n -> p c n", p=128))],
                  w=tuple(("big", "V", 8 + i) for i in range(4)), n=1)

        it = 0
        def qkv_load(ct):
            def ld(h, view, ct=ct):
                return [h.dma_start(out=view[:, 0:4096].rearrange("p (k n) -> p k n", k=8),
                                    in_=at_qkv_w[j].rearrange("(k p) n -> p k n", p=128)[:, :, ct * 512:(ct + 1) * 512])]
            return load_slab(ld, 1)
        qloaded = [qkv_load(ct) for ct in range(3)]
        def make_item(ct, tc, it, wv, wkey):
            nh = 4 if ct < 2 else 2
            nw = nh * 128
            gain = qgain if ct < 2 else kgain
            qi = it % NQ
            sqi = it % 2
            kq = ("scr", "qf", qi)
            ks_ = ("scr", "qst", qi)
            kr = ("scr", "qr", qi)
            qv = qf[:, qi, 0:nw].rearrange("p (a b) -> p a b", a=nh)
            st_ = {}

            def s0():
                b = bank()
                while b >= 6:
                    b = bank()
                if g == 0 and l + 1 < depth:
                    MS.step(6)

                def mm(h):
                    ins = None
                    for k in range(8):
                        ins = h.matmul(ps[b][:, :], lhsT=hT[:, k, tc * 128:(tc + 1) * 128], rhs=wv[:, k, :], start=(k == 0), stop=(k == 7))
                    return ins
                P.op("pe", mm, r=(wkey,) + tuple(hk(k, tc // 4) for k in range(8)), w=(psk(b),))
                P.op("dve", lambda h: h.tensor_tensor(out=qf[:, qi, :], in0=ps[b][:, :], in1=qkvb[:, ct * 512:(ct + 1) * 512], op=ALU.add),
                     r=(psk(b), ("scr", "ain")), w=(kq,))
                P.op("act", lambda h: h.activation(out=qsq[:, sqi, 0:nw], in_=qf[:, qi, 0:nw], func=AF.Square), r=(kq,), w=(("scr", "qsq", sqi),))

            def s1():
                P.op("dve", lambda h: h.tensor_reduce(out=qst[:, qi, 0:nh], in_=qsq[:, sqi, 0:nw].rearrange("p (a b) -> p a b", a=nh),
                                                      axis=AX.X, op=ALU.add), r=(("scr", "qsq", sqi),), w=(ks_,))
                P.op("dve", lambda h: h.tensor_scalar(out=qst[:, qi, 0:nh], in0=qst[:, qi, 0:nh], scalar1=1.0 / 128.0, scalar2=QK_EPS,
                                                      op0=ALU.mult, op1=ALU.add), r=(ks_,), w=(ks_,))
                P.op("act", lambda h: h.activation(out=qst[:, qi, 0:nh], in_=qst[:, qi, 0:nh], func=AF.Sqrt), r=(ks_,), w=(ks_,))

            def s2():
                P.op("dve", lambda h: h.reciprocal(out=qst[:, qi, 0:nh], in_=qst[:, qi, 0:nh]), r=(ks_,), w=(ks_,))
                P.op("dve", lambda h: h.tensor_tensor(out=qv, in0=qv, in1=qst[:, qi, 0:nh].unsqueeze(2).to_broadcast([128, nh, 128]), op=ALU.mult),
                     r=(kq, ks_), w=(kq,))
                P.op("pool" if latent else "dve", lambda h: h.tensor_tensor(out=qv, in0=qv, in1=gain[:, :].unsqueeze(1).to_broadcast([128, nh, 128]),
                                                                            op=ALU.mult), r=(kq, ("scr", "ain")), w=(kq,))

            def s3():
                if latent:
                    q4 = qf[:, qi, 0:nw].rearrange("p (a b two) -> p a b two", a=nh, two=2)
                    r4 = qr[:, qi, 0:nw].rearrange("p (a b two) -> p a b two", a=nh, two=2)
                    ev, od = q4[:, :, :, 0], q4[:, :, :, 1]
                    cosb = rope[:, tc, 0, :].unsqueeze(1).to_broadcast([128, nh, 64])
                    sinb = rope[:, tc, 1, :].unsqueeze(1).to_broadcast([128, nh, 64])
                    nr = nh * 64
                    rv = [rtmp[:, i, 0:nr].rearrange("p (a b) -> p a b", a=nh) for i in range(4)]
                    rr = (kq, ("scr", "ain"))
                    P.op("dve", lambda h: h.tensor_tensor(out=rv[0], in0=ev, in1=cosb, op=ALU.mult), r=rr, w=(("scr", "rt", 0),))
                    P.op("dve", lambda h: h.tensor_tensor(out=rv[1], in0=od, in1=sinb, op=ALU.mult), r=rr, w=(("scr", "rt", 1),))
                    P.op("dve", lambda h: h.tensor_tensor(out=r4[:, :, :, 0], in0=rv[0], in1=rv[1], op=ALU.subtract),
                         r=(("scr", "rt", 0), ("scr", "rt", 1)), w=(kr,))
                    P.op("pool", lambda h: h.tensor_tensor(out=rv[2], in0=ev, in1=sinb, op=ALU.mult), r=rr, w=(("scr", "rt", 2),))
                    P.op("pool", lambda h: h.tensor_tensor(out=rv[3], in0=od, in1=cosb, op=ALU.mult), r=rr, w=(("scr", "rt", 3),))
                    P.op("pool", lambda h: h.tensor_tensor(out=r4[:, :, :, 1], in0=rv[2], in1=rv[3], op=ALU.add),
                         r=(("scr", "rt", 2), ("scr", "rt", 3), kr), w=(kr,))
                else:
                    P.op("act", lambda h: h.activation(out=qr[:, qi, 0:nw], in_=qf[:, qi, 0:nw], func=AF.Identity), r=(kq,), w=(kr,))
                if ct == 2:
                    P.op("act", lambda h: h.activation(out=Vv[:, tc, :], in_=qf[:, qi, 256:512], func=AF.Identity), r=(kq,), w=(("big", "V", tc),))
                    if not latent:
                        s_, r0 = tc // 2, (tc % 2) * 128
                        P.dma("sp", kv_lanes[qi], lambda h: [
                            h.dma_start(out=nk_o[s_, j, r0:r0 + 128, :], in_=qf[:, qi, 0:256]),
                            h.dma_start(out=nv_o[s_, j, r0:r0 + 128, :], in_=qf[:, qi, 256:512])], r=(kq,), n=2)

            def s4():
                def tr(h):
                    psb = ps[7][:, :].bitcast(BF16)
                    ins = None
                    for hh in range(nh):
                        ins = h.transpose(psb[:, hh * 128:(hh + 1) * 128], qr[:, qi, hh * 128:(hh + 1) * 128], ident[:, :])
                    return ins
                P.op("pe", tr, r=(kr, ("c", "ident")), w=(psk(7),))
                if ct < 2:
                    P.op("act", lambda h: h.activation(out=qT[:, ct * 4:ct * 4 + 4, tc * 128:(tc + 1) * 128],
                                                       in_=ps[7][:, :].bitcast(BF16)[:, 0:512].rearrange("p (a b) -> p a b", a=4), func=AF.Identity),
                         r=(psk(7),), w=tuple(("big", "qT", ct * 4 + hh) for hh in range(4)))
                else:
                    P.op("act", lambda h: h.activation(out=kT[:, :, tc * 128:(tc + 1) * 128],
                                                       in_=ps[7][:, :].bitcast(BF16)[:, 0:256].rearrange("p (a b) -> p a b", a=2), func=AF.Identity),
                         r=(psk(7),), w=(("big", "kT", 0), ("big", "kT", 1)))
            return [s0, s1, s2, s3, s4]

        items = []
        for ct in range(3):
            view, wkey = qloaded[ct]
            wv = view[:, 0:4096].rearrange("p (k n) -> p k n", k=8)
            for tc in range(8):
                items.append(make_item(ct, tc, len(items), wv, wkey))
        NST = 5
        for n in range(len(items) + NST - 1):
            for sidx in range(NST):
                i = n - sidx
                if 0 <= i < len(items):
                    items[i][sidx]()
        if latent:
            for kc in range(4):
                def trc(h, kc=kc):
                    psb = ps[7][:, :].bitcast(BF16)
                    ins = None
                    for kvh in range(2):
                        ins = h.transpose(psb[:, kvh * 128:(kvh + 1) * 128], kcb[:, kc, kvh * 128:(kvh + 1) * 128], ident[:, :])
                    return ins
                P.op("pe", trc, r=(("scr", "kc"), ("c", "ident")), w=(psk(7),))
                P.op("act", lambda h, kc=kc: h.activation(out=kT[:, :, 1024 + kc * 128:1024 + (kc + 1) * 128],
                                                        in_=ps[7][:, :].bitcast(BF16)[:, 0:256].rearrange("p (a b) -> p a b", a=2), func=AF.Identity),
                     r=(psk(7),), w=(("big", "kT", 0), ("big", "kT", 1)))

        P.fence("scr")
        P.phase = "g%d l%d at-core" % (g, l)
        units = []
        if latent:
            for qt in range(2):
                for kvh in range(2):
                    for hh in range(4):
                        hd_ = kvh * 4 + hh
                        units.append(dict(kvh=kvh, rhs=qT[:, hd_, qt * 512:(qt + 1) * 512], rk=(("big", "qT", hd_),), kcs=list(range(12)),
                                          out=oT[:, hd_, qt * 512:(qt + 1) * 512], ok=(("big", "oT", hd_),), v3=False))
        else:
            for s_ in range(4):
                for kvh in range(2):
                    for hp in range(2):
                        h0 = kvh * 4 + hp * 2
                        units.append(dict(kvh=kvh, rhs=qT[:, h0:h0 + 2, s_ * 256:(s_ + 1) * 256], rk=(("big", "qT", h0), ("big", "qT", h0 + 1)),
                                          kcs=[2 * s_, 2 * s_ + 1], out=oT[:, h0:h0 + 2, s_ * 256:(s_ + 1) * 256],
                                          ok=(("big", "oT", h0), ("big", "oT", h0 + 1)), v3=True))
        srot = 0
        for ui, u in enumerate(units):
            ob = 3 + ui % 2
            db = 5 + ui % 2
            kvh = u["kvh"]
            pend = None
            nk_ = len(u["kcs"])

            def od(idx, kc, pi, u=u, ob=ob, db=db, kvh=kvh, nk_=nk_):
                def f(h):
                    h.matmul(ps[ob][:, :], lhsT=Vv[:, kc, kvh * 128:(kvh + 1) * 128], rhs=pT[:, pi, :], start=(idx == 0), stop=(idx == nk_ - 1))
                    return h.matmul(ps[db][:, :], lhsT=onesb[:, :], rhs=pT[:, pi, :], start=(idx == 0), stop=(idx == nk_ - 1))
                P.op("pe", f, r=(("big", "V", kc), ("scr", "pT", pi), ("c", "onesb")), w=(psk(ob), psk(db)))
            for idx, kc in enumerate(u["kcs"]):
                sb_ = srot % 3
                pi = srot % 3
                srot += 1
                P.op("pe", lambda h, sb_=sb_, kc=kc, u=u, kvh=kvh: h.matmul(ps[sb_][:, :], lhsT=kT[:, kvh, kc * 128:(kc + 1) * 128], rhs=u["rhs"],
                                                                         start=True, stop=True),
                     r=(("big", "kT", kvh),) + u["rk"], w=(psk(sb_),))
                P.op("act", lambda h, sb_=sb_, pi=pi: h.activation(out=pT[:, pi, :], in_=ps[sb_][:, :], func=AF.Exp, scale=scale),
                     r=(psk(sb_),), w=(("scr", "pT", pi),))
                if pend is not None:
                    od(*pend)
                pend = (idx, kc, pi)
            od(*pend)
            ri = ui % 2
            P.op("dve", lambda h, db=db, ri=ri: h.reciprocal(out=rdt[:, ri, :], in_=ps[db][:, :]), r=(psk(db),), w=(("scr", "rd", ri),))
            if u["v3"]:
                P.op("dve", lambda h, ob=ob, ri=ri, u=u: h.tensor_tensor(out=u["out"], in0=ps[ob][:, :].rearrange("p (a b) -> p a b", a=2),
                                                                      in1=rdt[:, ri, :].rearrange("p (a b) -> p a b", a=2), op=ALU.mult),
                     r=(psk(ob), ("scr", "rd", ri)), w=u["ok"])
            else:
                P.op("dve", lambda h, ob=ob, ri=ri, u=u: h.tensor_tensor(out=u["out"], in0=ps[ob][:, :], in1=rdt[:, ri, :], op=ALU.mult),
                     r=(psk(ob), ("scr", "rd", ri)), w=u["ok"])
        P.phase = "g%d l%d at-out+ln" % (g, l)
        proj_ln(g, l, 0, 8, lambda k, t: oT[:, k, tsl(t)], lambda k, t: ("big", "oT", k), wload_sq(at_o_w[j], 8), tail_hook=mixer_tail(g, l))

    allx = tuple(xk(m, t) for m in range(8) for t in range(2))
    stop = False
    for g in range(ngroups):
        c = g
        P.dma("sp", x_lane, lambda h, g=g: [h.dma_start(out=xT[:, m, :], in_=xin[g, m]) for m in range(8)], w=allx, n=8)
        for m in range(8):
            for t in range(2):
                P.op("dve", lambda h, m=m, t=t, c=c: h.tensor_scalar(out=hT[:, m, tsl(t)], in0=xT[:, m, tsl(t)], scalar1=modcol(mod1, 0, 1, m, c),
                                                                   scalar2=modcol(mod, 0, 0, m, c), op0=ALU.mult, op1=ALU.add),
                     r=(xk(m, t),) + MODK(0), w=(hk(m, t),))
        for l in range(depth):
            try:
                if l % 2 == 0:
                    hyena(g, l)
                else:
                    attention(g, l)
            except _Stop:
                stop = True
                break
            if dbg_dump((g, l, 0)):
                stop = True
                break
            if g == 0 and l + 1 < depth:
                def hk_(cp):
                    MS.step(bank())
                    MS.step(bank())

                def hk_end(l=l):
                    MS.drain(l + 1, bank())
                    mod_finish(l + 1)
                ffn(g, l, hook=hk_, hook_end=hk_end)
            else:
                ffn(g, l)
            if dbg_dump((g, l, 1)):
                stop = True
                break
        if stop:
            break
        P.dma("sp", y_lane, lambda h, g=g: [h.dma_start(out=yT[g, m], in_=xT[:, m, :]) for m in range(8)], r=allx, n=8)

    P.finalize()
    for ln in P.dma_lanes:
        ln.sem = semaphore("d_" + ln.name)
    with nc.Block() as block:
        @block.tensor
        def _(h):
            P.emit("pe", h)

        @block.scalar
        def _(h):
            P.emit("act", h)

        @block.vector
        def _(h):
            P.emit("dve", h)

        @block.gpsimd
        def _(h):
            P.emit("pool", h)

        @block.sync
        def _(h):
            P.emit("sp", h)
    es.close()
    nc._prog_stats = {e: len(P.eng_ops[e]) for e in P.ENGS}
    nc._pe_log = P.pe_log
    return nc


_NC_CACHE = {}


def make_in_maps(inp):
    f32 = lambda a: np.ascontiguousarray(np.asarray(a, np.float32))
    consts = make_consts()
    pp = pack_pp(inp)
    shared = {
        "pp": pp,
        "w_mod": f32(inp["w_mod"]), "hy_in_w": f32(inp["hy_in_w"]), "hy_out_w": f32(inp["hy_out_w"]),
        "at_qkv_w": f32(inp["at_qkv_w"]), "at_o_w": f32(inp["at_o_w"]), "ff_in_w": f32(inp["ff_in_w"]), "ff_out_w": f32(inp["ff_out_w"]),
        "pos_w1": f32(inp["hy_pos_w1"]), "pos_w2": f32(inp["hy_pos_w2"]).reshape(4, 64, 64), "pos_wout": f32(inp["hy_pos_wout"]),
        "qkvb": f32(inp["at_qkv_b"]), "qgain": f32(inp["at_q_gain"]), "kgain": f32(inp["at_k_gain"]),
        "ident": consts["ident"], "zT256": consts["zT256"], "zT1024": consts["zT1024"],
        "F256": consts["F256"], "F1024": consts["F1024"], "G256": consts["G256"], "G1024": consts["G1024"],
        "delta": consts["delta"], "rope": consts["rope"],
    }
    xp = f32(inp["x_prompt"])
    xs = f32(inp["x_sample"])
    ck = f32(inp["cache_k"])
    cv = f32(inp["cache_v"])
    cc = f32(inp["c"])
    cctx = f32(inp["c_ctx"])
    maps = []
    for core in range(8):
        x0 = xp[4 * core:4 * core + 4].reshape(1024, 1024)
        x1 = xs[core]
        xin = np.stack([x0.T.reshape(8, 128, 1024), x1.T.reshape(8, 128, 1024)], 0)
        cond = np.stack([cctx, cc[core]], 0).reshape(2, 8, 128).transpose(2, 1, 0)
        m = dict(shared)
        m["xin"] = np.ascontiguousarray(xin)
        m["cond"] = np.ascontiguousarray(cond)
        m["cache_k"] = np.ascontiguousarray(ck[core].reshape(2, 512, 256))
        m["cache_v"] = np.ascontiguousarray(cv[core].reshape(2, 512, 256))
        maps.append(m)
    return maps


def kernel(**inp):
    if "nc" not in _NC_CACHE:
        _NC_CACHE["nc"] = build_nc()
    nc = _NC_CACHE["nc"]
    maps = make_in_maps(inp)
    res = run_bass_kernel_spmd(nc, maps, core_ids=list(range(8)))
    y_prompt = np.zeros((32, 256, 1024), np.float32)
    y_sample = np.zeros((8, 1024, 1024), np.float32)
    nk = np.zeros((32, 2, 256, 2, 128), np.float32)
    nv = np.zeros((32, 2, 256, 2, 128), np.float32)
    for core in range(8):
        r = res.results[core]
        yT = np.asarray(r["yT"], np.float32)
        y_prompt[4 * core:4 * core + 4] = yT[0].reshape(1024, 1024).T.reshape(4, 256, 1024)
        y_sample[core] = yT[1].reshape(1024, 1024).T
        nk[4 * core:4 * core + 4] = np.asarray(r["nk"], np.float32).reshape(4, 2, 256, 2, 128)
        nv[4 * core:4 * core + 4] = np.asarray(r["nv"], np.float32).reshape(4, 2, 256, 2, 128)
    return (y_prompt, y_sample, nk, nv)
```

```python
import contextlib
import numpy as np
import ml_dtypes
import concourse.bass as bass
import concourse.mybir as mybir
from concourse.bass_utils import run_bass_kernel_spmd

F32 = mybir.dt.float32
BF16 = mybir.dt.bfloat16
F32R = mybir.dt.float32r
AF = mybir.ActivationFunctionType
ALU = mybir.AluOpType
AX = mybir.AxisListType

D = 1024
NM = 8
TOK = 1024
DEPTH = 4
DFF = 2816
NFC = 22
LN_EPS = 1e-5
QK_EPS = 1e-6
ALPHA = float((2 * DEPTH) ** 0.25)
GROUPS = [dict(nseq=4, L=256), dict(nseq=1, L=1024)]
MAGIC = 12582912.0
TWO_PI = float(2 * np.pi)
NSLOT = 3
MULT_ENG = "pool"
SLOT_EL = 4096

DBG = None


class _Stop(Exception):
    pass


class _CountProxy:
    def __init__(self, h):
        self.h = h
        self.n = 0

    def matmul(self, *a, **k):
        self.n += 1
        return self.h.matmul(*a, **k)

    def transpose(self, *a, **k):
        self.n += 1
        return self.h.transpose(*a, **k)

    def __getattr__(self, name):
        return getattr(self.h, name)


class Lane:
    def __init__(self, name, step):
        self.name = name
        self.step = step
        self.ops = []
        self.sem = None
        self.cum = None

    def count_at(self, seq):
        return self.cum[seq]


class Op:
    __slots__ = ("eng", "lane", "seq", "fn", "deps", "need", "vc", "isdma", "waits", "ninc", "phase")


class Prog:
    ENGS = ("pe", "act", "dve", "pool", "sp")

    def __init__(self):
        self.lanes = {e: Lane(e, 1) for e in ("pe", "act", "dve", "pool")}
        self.dma_lanes = []
        self.eng_ops = {e: [] for e in self.ENGS}
        self.all_ops = []
        self.last_w = {}
        self.readers = {}
        self.fences = {}
        self.store_lanes = []
        self.gfence = {}
        self.phase = ""
        self.pe_log = []

    def dma_lane(self, name, store=False):
        ln = Lane(name, 16)
        self.dma_lanes.append(ln)
        if store:
            self.store_lanes.append(ln)
        return ln

    def _add(self, eng, lane, fn, reads, writes, isdma, ninc=1):
        op = Op()
        op.eng, op.lane, op.seq, op.fn, op.isdma, op.need, op.ninc = eng, lane, len(lane.ops), fn, isdma, isdma, ninc
        op.phase = self.phase
        deps = {}
        own = self.lanes.get(eng)

        def dep(ln, sq, raw):
            if (not isdma) and (ln is own) and (not raw) and eng == "pe":
                return
            cur = deps.get(ln)
            if cur is None or sq > cur:
                deps[ln] = sq

        for ln, sq in self.gfence.items():
            dep(ln, sq, True)
        for k in reads:
            f = self.fences.get(k[0])
            if f:
                for ln, sq in f.items():
                    dep(ln, sq, True)
            w = self.last_w.get(k)
            if w:
                dep(w[0], w[1], True)
        for k in writes:
            f = self.fences.get(k[0])
            if f:
                for ln, sq in f.items():
                    dep(ln, sq, True)
            w = self.last_w.get(k)
            if w:
                dep(w[0], w[1], False)
            for ln, sq in self.readers.get(k, {}).items():
                dep(ln, sq, False)
        for ln in list(deps):
            if ln.step == 16:
                deps[ln] = len(ln.ops) - 1
        if lane in deps and deps[lane] >= op.seq:
            deps[lane] = op.seq - 1
        op.deps = deps
        lane.ops.append(op)
        self.eng_ops[eng].append(op)
        self.all_ops.append(op)
        for k in reads:
            self.readers.setdefault(k, {})[lane] = op.seq
        for k in writes:
            self.last_w[k] = (lane, op.seq)
            self.readers[k] = {}
        return op

    def op(self, eng, fn, r=(), w=()):
        return self._add(eng, self.lanes[eng], fn, r, w, False)

    def dma(self, eng, lane, fn, r=(), w=(), n=1):
        return self._add(eng, lane, fn, r, w, True, n)

    def barrier(self):
        for ln in list(self.lanes.values()) + self.dma_lanes:
            if ln.ops:
                self.gfence[ln] = len(ln.ops) - 1

    def fence(self, region):
        f = dict(self.fences.get(region, {}))

        def upd(ln, sq):
            if f.get(ln, -1) < sq:
                f[ln] = sq

        for k in [k for k in self.last_w if k[0] == region]:
            ln, sq = self.last_w.pop(k)
            upd(ln, sq)
        for k in [k for k in self.readers if k[0] == region]:
            for ln, sq in self.readers.pop(k).items():
                upd(ln, sq)
        self.fences[region] = f

    def finalize(self):
        know = {e: {} for e in self.ENGS}
        for op in self.all_ops:
            K = know[op.eng]
            waits = []
            for ln, sq in op.deps.items():
                if sq < 0 or K.get(ln, -1) >= sq:
                    continue
                waits.append((ln, sq))
                tgt = ln.ops[sq]
                tgt.need = True
                for l2, s2 in tgt.vc.items():
                    if K.get(l2, -1) < s2:
                        K[l2] = s2
                if K.get(ln, -1) < sq:
                    K[ln] = sq
            op.waits = waits
            vc = dict(K)
            if vc.get(op.lane, -1) < op.seq:
                vc[op.lane] = op.seq
            op.vc = vc
        for ln in list(self.lanes.values()) + self.dma_lanes:
            c = 0
            ln.cum = []
            for o in ln.ops:
                if ln.step == 16:
                    c += 16 * o.ninc
                elif o.need:
                    c += 1
                ln.cum.append(c)

    def emit(self, eng, h):
        if eng == "pe":
            h = _CountProxy(h)
        for op in self.eng_ops[eng]:
            for ln, sq in op.waits:
                h.wait_ge(ln.sem, ln.count_at(sq))
            if eng == "pe":
                n0 = h.n
            ins = op.fn(h)
            if eng == "pe":
                self.pe_log.append((op.phase, h.n - n0))
            if op.isdma:
                if not isinstance(ins, (list, tuple)):
                    ins = [ins]
                assert len(ins) == op.ninc
                for i in ins:
                    i.then_inc(op.lane.sem, 16)
            elif op.need:
                ins.then_inc(op.lane.sem, 1)
        if eng == "sp":
            for ln in self.store_lanes:
                if ln.ops:
                    h.wait_ge(ln.sem, ln.cum[-1])


def _pp_layout():
    items = [
        ("bmod", 192), ("lng", 64), ("lnb", 64),
        ("hy_in_b", 48), ("hy_sw", 144), ("hy_sb", 48), ("hy_fb", 16), ("hy_ob", 16),
        ("at_ob", 16), ("ff_in_b", 176), ("ff_cw", 528), ("ff_cb", 176), ("ff_ob", 32),
        ("freq", 2), ("pb1", 2), ("pb2", 4), ("tn256", 2), ("tn1024", 8), ("mask0", 1),
    ]
    off = {}
    o = 0
    for n, w in items:
        off[n] = (o, w)
        o += w
    return off, o


PP_OFF, NPP = _pp_layout()


def _cp(v):
    v = np.asarray(v, np.float32)
    sh = v.shape
    c = sh[-1] // 128
    v = v.reshape(sh[:-1] + (c, 128))
    v = np.moveaxis(v, -1, 0)
    return np.ascontiguousarray(v).reshape(128, -1)


def pack_pp(inp):
    pp = np.zeros((128, NPP), np.float32)

    def put(name, arr):
        o, w = PP_OFF[name]
        assert arr.shape == (128, w), (name, arr.shape, w)
        pp[:, o:o + w] = arr

    put("bmod", _cp(inp["b_mod"]))
    put("lng", _cp(inp["ln_g"]))
    put("lnb", _cp(inp["ln_b"]))
    put("hy_in_b", _cp(inp["hy_in_b"]))
    put("hy_sw", _cp(inp["hy_short_w"]))
    put("hy_sb", _cp(inp["hy_short_b"]))
    put("hy_fb", _cp(inp["hy_filt_bias"]))
    put("hy_ob", _cp(inp["hy_out_b"]))
    put("at_ob", _cp(inp["at_o_b"]))
    put("ff_in_b", _cp(inp["ff_in_b"]))
    put("ff_cw", _cp(inp["ff_conv_w"]))
    put("ff_cb", _cp(inp["ff_conv_b"]))
    put("ff_ob", _cp(inp["ff_out_b"]))
    z = np.zeros((128, 2), np.float32)
    z[:64] = np.asarray(inp["hy_freq"], np.float32).T
    put("freq", z)
    z = np.zeros((128, 2), np.float32)
    z[:64] = np.asarray(inp["hy_pos_b1"], np.float32).T
    put("pb1", z)
    z = np.zeros((128, 4), np.float32)
    z[:64] = np.asarray(inp["hy_pos_b2"], np.float32).reshape(4, 64).T
    put("pb2", z)
    for L in (256, 1024):
        t = np.linspace(0.0, 1.0, L, dtype=np.float32)
        put("tn%d" % L, np.ascontiguousarray(-t.reshape(L // 128, 128).T))
    m = np.ones((128, 1), np.float32)
    m[0, 0] = 0.0
    put("mask0", m)
    return pp


def _round_f32r(a):
    u = np.ascontiguousarray(a, np.float32).view(np.uint32).astype(np.uint64)
    u = ((u + 0x800) & 0xFFFFF000).astype(np.uint32)
    return u.view(np.float32)


def make_consts():
    c = {}
    c["ident"] = np.eye(128).astype(ml_dtypes.bfloat16)
    for L in (256, 1024):
        N = 2 * L
        t = np.arange(L, dtype=np.float64)
        k = np.arange(L, dtype=np.float64)
        th = 2.0 * np.pi * np.outer(t, k + 0.5) / N
        C = np.cos(th)
        S = -np.sin(th)
        nj = L // 128
        Fm = np.zeros((nj, 128, nj, 256), np.float64)
        Gm = np.zeros((nj, 128, 2, L), np.float64)
        for j in range(nj):
            cr = C[:, j * 128:(j + 1) * 128].reshape(nj, 128, 128)
            ci = S[:, j * 128:(j + 1) * 128].reshape(nj, 128, 128)
            Fm[j, :, :, 0:128] = cr.transpose(1, 0, 2)
            Fm[j, :, :, 128:256] = ci.transpose(1, 0, 2)
            Gm[j, :, 0, :] = (2.0 / N) * C[:, j * 128:(j + 1) * 128].T
            Gm[j, :, 1, :] = (2.0 / N) * S[:, j * 128:(j + 1) * 128].T
        Fh = Fm.astype(np.float32).astype(ml_dtypes.bfloat16)
        Fl = (Fm - Fh.astype(np.float64)).astype(np.float32).astype(ml_dtypes.bfloat16)
        c["F%d" % L] = np.ascontiguousarray(np.stack([Fh, Fl], 2))
        Gh = Gm.astype(np.float32).astype(ml_dtypes.bfloat16)
        Gl = (Gm - Gh.astype(np.float64)).astype(np.float32).astype(ml_dtypes.bfloat16)
        c["G%d" % L] = np.ascontiguousarray(np.stack([Gh, Gl], 2))
        tl = np.linspace(0.0, 1.0, L, dtype=np.float32)[:, None]
        w = (2.0 * np.pi * np.arange(L, dtype=np.float32) / L).astype(np.float32)
        f = np.linspace(1e-4, 15, 16, dtype=np.float32)
        ang = (w[:, None] * f[None, :]).astype(np.float32)
        z = np.concatenate([tl, np.cos(ang), -np.sin(ang)], -1).astype(np.float32)
        c["zT%d" % L] = np.ascontiguousarray(z.T)
    max_decay = np.log(1e-2) / 0.3
    min_decay = np.log(1e-2) / 1.5
    c["delta"] = np.abs(np.linspace(min_decay, max_decay, D, dtype=np.float32)).astype(np.float32)
    rows = np.repeat(np.arange(16), 64).astype(np.float32)
    cols = np.tile(np.arange(64), 16).astype(np.float32)
    inv = (10000.0 ** (-np.arange(0, 64, 2, dtype=np.float32) / 64)).astype(np.float32)
    ang = np.concatenate([rows[:, None] * inv, cols[:, None] * inv], -1).astype(np.float32)
    cs = np.stack([np.cos(ang), np.sin(ang)], 1).astype(np.float32)
    c["rope"] = np.ascontiguousarray(cs.reshape(8, 128, 2, 64).transpose(1, 0, 2, 3))
    return c


def build_nc(dbg=None, ngroups=2, depth=DEPTH):
    nc = bass.Bass("TRN2", target_bir_lowering=False)
    P = Prog()
    es = contextlib.ExitStack()

    def din(name, shape, dt=F32):
        return nc.dram_tensor(name, list(shape), dt, kind="ExternalInput")

    xin = din("xin", [2, 8, 128, 1024]).ap()
    cond = din("cond", [128, 8, 2]).ap()
    cache_k = din("cache_k", [2, 512, 256]).ap()
    cache_v = din("cache_v", [2, 512, 256]).ap()
    ppd = din("pp", [128, NPP]).ap()
    w_mod = din("w_mod", [4, 1024, 6144]).ap()
    hy_in_w = din("hy_in_w", [2, 1024, 3072]).ap()
    hy_out_w = din("hy_out_w", [2, 1024, 1024]).ap()
    at_qkv_w = din("at_qkv_w", [2, 1024, 1536]).ap()
    at_o_w = din("at_o_w", [2, 1024, 1024]).ap()
    ff_in_w = din("ff_in_w", [4, 1024, 5632]).ap()
    ff_out_w = din("ff_out_w", [4, 2816, 1024]).ap()
    pos_w1 = din("pos_w1", [2, 33, 64]).ap()
    pos_w2 = din("pos_w2", [4, 64, 64]).ap()
    pos_wout = din("pos_wout", [2, 64, 2048]).ap()
    qkvb_t = din("qkvb", [2, 1536])
    qgain_t = din("qgain", [2, 128])
    kgain_t = din("kgain", [2, 128])
    ident_d = din("ident", [128, 128], BF16).ap()
    zT_d = {L: din("zT%d" % L, [33, L]).ap() for L in (256, 1024)}
    F_d = {L: din("F%d" % L, [L // 128, 128, 2, L // 128, 256], BF16).ap() for L in (256, 1024)}
    G_d = {L: din("G%d" % L, [L // 128, 128, 2, 2, L], BF16).ap() for L in (256, 1024)}
    delta_t = din("delta", [1024])
    rope_d = din("rope", [128, 8, 2, 64]).ap()

    yT = nc.dram_tensor("yT", [2, 8, 128, 1024], F32, kind="ExternalOutput").ap()
    nk_o = nc.dram_tensor("nk", [4, 2, 256, 256], F32, kind="ExternalOutput").ap()
    nv_o = nc.dram_tensor("nv", [4, 2, 256, 256], F32, kind="ExternalOutput").ap()

    def sb(name, shape, dt=F32):
        return es.enter_context(nc.sbuf_tensor(name, list(shape), dt))

    def semaphore(name):
        return es.enter_context(nc.semaphore(name))

    xT = sb("xT", [128, 8, 1024])
    hT = sb("hT", [128, 8, 1024], BF16)
    big = sb("big", [128, 22528], BF16)
    cvb = sb("cvb", [128, 4, 1024])
    dsc = sb("dsc", [128, 3, 44])
    slots = sb("slots", [128, NSLOT, SLOT_EL], BF16)
    pp = sb("ppsb", [128, NPP])
    mod = sb("mod", [128, 4 * 48 * 2])
    mod1 = sb("mod1", [128, 4 * 48 * 2])
    gbt = sb("gbt", [128, 4 * 2 * 8 * 2])
    AAt = sb("AAt", [128, 4 * 2 * 8 * 2])
    BBt = sb("BBt", [128, 4 * 2 * 8 * 2])
    condT = sb("condT", [128, 16])
    condb = sb("condb", [128, 16], BF16)
    ident = sb("identsb", [128, 128], BF16)
    onesb = sb("onesb", [128, 128], BF16)
    onesf = sb("onesf", [128, 128])
    etmp = sb("etmp", [128, 4, 512])
    NMW = 4
    modw = sb("modw", [128, NMW, 1024], BF16)
    lnst = sb("lnst", [128, 2, 512])
    xrt = sb("xrt", [128, 2, 512])
    sqt = sb("sqt", [128, 2, 512])
    fb1 = sb("fb1", [64, 8])

    SCRW = 8880
    scr = sb("scr", [128, SCRW])

    def carve():
        off = [0]

        def alloc(shape, dt=F32):
            n = int(np.prod(shape[1:]))
            nw = n if dt == F32 else (n + 1) // 2
            assert off[0] + nw <= SCRW, (off[0], nw)
            v = scr[0:shape[0], off[0]:off[0] + nw]
            off[0] += nw
            if dt == BF16:
                v = v.bitcast(BF16)
            if len(shape) == 3:
                v = v.rearrange("p (a b) -> p a b", a=shape[1])
            elif len(shape) == 4:
                v = v.rearrange("p (a b c) -> p a b c", a=shape[1], b=shape[2])
            return v
        alloc.off = off
        return alloc

    pst = es.enter_context(nc.psum_tensor("pst", [128, 4096], F32))
    ps = [pst[:, i * 512:(i + 1) * 512] for i in range(8)]

    for ln in P.lanes.values():
        ln.sem = semaphore("c_" + ln.name)
    slot_lanes = [P.dma_lane("slot%d" % i) for i in range(NSLOT)]
    misc_lane = P.dma_lane("misc")
    x_lane = P.dma_lane("xload")
    modw_lanes = [P.dma_lane("modw%d" % i) for i in range(4)]
    fg_lanes = [P.dma_lane("fg%d" % i) for i in range(4)]
    fin_lane = P.dma_lane("fin")
    zt_lane = P.dma_lane("zt")
    wo_lanes = [P.dma_lane("wo%d" % i) for i in range(2)]
    ain_lane = P.dma_lane("ain")
    kc_lane = P.dma_lane("kc")
    vc_lane = P.dma_lane("vc")
    y_lane = P.dma_lane("ystore", store=True)
    kv_lanes = [P.dma_lane("kvst%d" % i, store=True) for i in range(5)]
    dbg_lane = P.dma_lane("dbg", store=True)

    def PPv(name, *idx):
        o, w = PP_OFF[name]
        return o, w

    def ppcol(name, i):
        o, w = PP_OFF[name]
        assert 0 <= i < w
        return pp[:, o + i:o + i + 1]

    st = dict(slot=0, bank=0, zb=0, cv=0, nbank=8, pair=0)

    def load_slab(fn_list_builder, n, eng="pool"):
        i = st["slot"] % NSLOT
        st["slot"] += 1
        key = ("slot", i)
        view = slots[:, i, :]

        def fn(h, view=view):
            return fn_list_builder(h, view)
        P.dma(eng, slot_lanes[i], fn, r=(), w=(key,), n=n)
        return view, key

    def bank():
        b = st["bank"] % st["nbank"]
        st["bank"] += 1
        return b

    def bankpair():
        b = 2 * (st["pair"] % 4)
        st["pair"] += 1
        return b

    def zalloc():
        i = st["zb"] % 4
        st["zb"] += 1
        return i

    def calloc():
        i = st["cv"] % 4
        st["cv"] += 1
        return i

    def psk(b):
        return ("ps", b)

    def pro_loads(h):
        return [
            h.dma_start(out=pp[:], in_=ppd[:, :]),
            h.dma_start(out=condT[:], in_=cond.rearrange("p k c -> p (k c)")),
            h.dma_start(out=ident[:], in_=ident_d[:, :]),
        ]
    P.dma("sp", misc_lane, pro_loads, w=(("c", "pp"), ("c", "cond"), ("c", "ident")), n=3)
    P.op("dve", lambda h: h.memset(onesb[:], 1.0), w=(("c", "onesb"),))
    P.op("dve", lambda h: h.memset(etmp[:, 0, 0:128], 1.0 / 1024.0), w=(("et", 0),))
    P.op("act", lambda h: h.activation(out=onesf[:].bitcast(F32R), in_=etmp[:, 0, 0:128], func=AF.Identity), r=(("et", 0),), w=(("c", "onesf"),))
    P.op("act", lambda h: h.activation(out=condb[:], in_=condT[:], func=AF.Silu), r=(("c", "cond"),), w=(("c", "condb"),))
    o_f = PP_OFF["freq"][0]
    o_b1 = PP_OFF["pb1"][0]
    o_b2 = PP_OFF["pb2"][0]
    for j in range(2):
        P.op("dve", lambda h, j=j: h.tensor_tensor(out=fb1[:, j:j + 1], in0=pp[0:64, o_f + j:o_f + j + 1],
                                                  in1=pp[0:64, o_b1 + j:o_b1 + j + 1], op=ALU.mult),
             r=(("c", "pp"),), w=(("c", "fb1", j),))
        for i in range(2):
            P.op("dve", lambda h, j=j, i=i: h.tensor_tensor(out=fb1[:, 2 + 2 * j + i:3 + 2 * j + i], in0=pp[0:64, o_f + j:o_f + j + 1],
                                                          in1=pp[0:64, o_b2 + 2 * j + i:o_b2 + 2 * j + i + 1], op=ALU.mult),
                 r=(("c", "pp"),), w=(("c", "fb1", 2 + 2 * j + i),))

    o_bm = PP_OFF["bmod"][0]

    class ModStream:
        def __init__(self):
            self.tasks = []
            self.loaded = 0
            self.done = 0

        def add_layer(self, l):
            self.tasks += [(l, f) for f in range(48)]

        def _load(self):
            l, f = self.tasks[self.loaded]
            i = self.loaded % 4
            self.loaded += 1
            P.dma("pool", modw_lanes[i], lambda h, l=l, f=f, i=i: [h.dma_start(
                out=modw[:, i, :].rearrange("p (k n) -> p k n", k=8),
                in_=w_mod[l].rearrange("(k p) n -> p k n", p=128)[:, :, f * 128:(f + 1) * 128])], w=(("modw", i),), n=1)

        def pending(self, l):
            return any(t[0] == l for t in self.tasks[self.done:])

        def step(self, b):
            if self.done >= len(self.tasks):
                return
            while self.loaded < len(self.tasks) and self.loaded < self.done + 3:
                self._load()
            l, f = self.tasks[self.done]
            i = self.done % 4
            self.done += 1
            if self.loaded < len(self.tasks) and self.loaded < self.done + 3:
                self._load()

            def mm(h, i=i, b=b):
                ins = None
                wv = modw[:, i, :].rearrange("p (k n) -> p k n", k=8)
                for k in range(8):
                    ins = h.matmul(ps[b][:, 0:2], lhsT=wv[:, k, :], rhs=condb[:, 2 * k:2 * k + 2], start=(k == 0), stop=(k == 7))
                return ins
            P.op("pe", mm, r=(("modw", i), ("c", "condb")), w=(psk(b),))
            f0 = l * 48 + f
            P.op("act", lambda h, b=b, f0=f0: h.activation(out=mod[:, 2 * f0:2 * f0 + 2], in_=ps[b][:, 0:2], func=AF.Identity,
                                                         bias=pp[:, o_bm + f0:o_bm + f0 + 1]), r=(psk(b), ("c", "pp")), w=(("c", "mod", l),))

        def drain(self, l, b):
            while self.pending(l):
                self.step(b)

    MS = ModStream()

    def modcol(t, l, i, m, c):
        o = ((l * 48) + i * 8 + m) * 2 + c
        return t[:, o:o + 1]

    def modrow(t, l, i, c):
        o = l * 48 + i * 8
        return t[:, :].rearrange("p (x c) -> p x c", c=2)[:, o:o + 8, c]

    def t4(t, l, sub, c):
        o = (l * 2 + sub) * 8
        return t[:, :].rearrange("p (x c) -> p x c", c=2)[:, o:o + 8, c]

    def t4col(t, l, sub, m, c):
        o = ((l * 2 + sub) * 8 + m) * 2 + c
        return t[:, o:o + 1]

    def projbias(l, sub):
        if sub == 1:
            o = PP_OFF["ff_ob"][0] + l * 8
        elif l % 2 == 0:
            o = PP_OFF["hy_ob"][0] + (l // 2) * 8
        else:
            o = PP_OFF["at_ob"][0] + (l // 2) * 8
        return pp[:, o:o + 8]

    o_lng = PP_OFF["lng"][0]
    o_lnb = PP_OFF["lnb"][0]

    def aabb(l, sub):
        nl, nsub = (l, 1) if sub == 0 else (l + 1, 0)
        if nl >= depth:
            return
        lg = pp[:, o_lng + (l * 2 + sub) * 8:o_lng + (l * 2 + sub) * 8 + 8]
        lb = pp[:, o_lnb + (l * 2 + sub) * 8:o_lnb + (l * 2 + sub) * 8 + 8]
        for c in range(2):
            P.op("dve", lambda h, c=c: h.tensor_tensor(out=t4(AAt, l, sub, c), in0=lg, in1=modrow(mod1, nl, 3 * nsub + 1, c), op=ALU.mult),
                 r=(("c", "mod1", nl), ("c", "pp")), w=(("c", "AA", l, sub, c),))
            P.op("dve", lambda h, c=c: h.tensor_tensor(out=t4(BBt, l, sub, c), in0=lb, in1=modrow(mod1, nl, 3 * nsub + 1, c), op=ALU.mult),
                 r=(("c", "mod1", nl), ("c", "pp")), w=(("c", "BB", l, sub, c),))
            P.op("dve", lambda h, c=c: h.tensor_tensor(out=t4(BBt, l, sub, c), in0=t4(BBt, l, sub, c), in1=modrow(mod, nl, 3 * nsub + 0, c), op=ALU.add),
                 r=(("c", "BB", l, sub, c), ("c", "mod", nl)), w=(("c", "BB", l, sub, c),))

    def mod_finish(l):
        P.op("dve", lambda h: h.tensor_scalar(out=mod1[:, l * 96:(l + 1) * 96], in0=mod[:, l * 96:(l + 1) * 96],
                                              scalar1=1.0, scalar2=None, op0=ALU.add),
             r=(("c", "mod", l),), w=(("c", "mod1", l),))
        for sub in range(2):
            for c in range(2):
                P.op("dve", lambda h, sub=sub, c=c: h.tensor_tensor(out=t4(gbt, l, sub, c), in0=modrow(mod1, l, 3 * sub + 2, c),
                                                                    in1=projbias(l, sub), op=ALU.mult),
                     r=(("c", "mod1", l), ("c", "pp")), w=(("c", "gbt", l, sub, c),))
        aabb(l, 0)
        if l >= 1:
            aabb(l - 1, 1)

    P.phase = "prologue"
    MS.add_layer(0)
    for _ in range(16):
        MS.step(bank())
    P.op("dve", lambda h: h.tensor_scalar(out=mod1[:, 0:32], in0=mod[:, 0:32], scalar1=1.0, scalar2=None, op0=ALU.add),
         r=(("c", "mod", 0),), w=(("c", "mod1", 0),))

    def MODK(l):
        return (("c", "mod", l), ("c", "mod1", l))


    def xk(m, t):
        return ("xT", m, t)

    def hk(m, t):
        return ("hT", m, t)

    def tsl(t):
        return slice(t * 512, (t + 1) * 512)

    def tokv(ap2d, g):
        return ap2d.rearrange("p (s t) -> p s t", s=GROUPS[g]["nseq"])

    def psv(b, g):
        if g == 0:
            return ps[b][:, :].rearrange("p (s t) -> p s t", s=2)
        return ps[b][:, :].rearrange("p (s t) -> p s t", s=1)

    def conv_scalars(n, o_w, stride, o_b, o_cb):
        w0, w1, w2 = (pp[:, o_w + i * stride:o_w + i * stride + n] for i in range(3))
        bi, cb = pp[:, o_b:o_b + n], pp[:, o_cb:o_cb + n]
        K = (("dsc",),)
        R_ = (("c", "pp"),)
        P.op("dve", lambda h: h.tensor_tensor(out=dsc[:, 0, 0:n], in0=w0, in1=w1, op=ALU.add), r=R_, w=K)
        P.op("dve", lambda h: h.tensor_tensor(out=dsc[:, 0, 0:n], in0=dsc[:, 0, 0:n], in1=w2, op=ALU.add), r=R_ + K, w=K)
        P.op("dve", lambda h: h.tensor_tensor(out=dsc[:, 0, 0:n], in0=dsc[:, 0, 0:n], in1=bi, op=ALU.mult), r=R_ + K, w=K)
        P.op("dve", lambda h: h.tensor_tensor(out=dsc[:, 0, 0:n], in0=dsc[:, 0, 0:n], in1=cb, op=ALU.add), r=R_ + K, w=K)
        P.op("dve", lambda h: h.scalar_tensor_tensor(out=dsc[:, 1, 0:n], in0=w0, scalar=-1.0, in1=bi, op0=ALU.mult, op1=ALU.mult), r=R_ + K, w=K)
        P.op("dve", lambda h: h.scalar_tensor_tensor(out=dsc[:, 2, 0:n], in0=w2, scalar=-1.0, in1=bi, op0=ALU.mult, op1=ALU.mult), r=R_ + K, w=K)

    def upconv(g, lhs_fn, wkey, w0, w1, w2, ci, out2d, okey, defer=None, after=None):
        cbt, nwb0, nwb2 = dsc[:, 0, ci:ci + 1], dsc[:, 1, ci:ci + 1], dsc[:, 2, ci:ci + 1]
        DK = (("dsc",), ("c", "pp"))
        b0 = bankpair()
        PK = (psk(b0), psk(b0 + 1))

        def mm_t(t):
            b = b0 + t

            def mm(h):
                ins = None
                for k in range(8):
                    ins = h.matmul(ps[b][:, :], lhsT=lhs_fn(k), rhs=hT[:, k, tsl(t)], start=(k == 0), stop=(k == 7))
                return ins
            P.op("pe", mm, r=(wkey,) + tuple(hk(k, t) for k in range(8)), w=(psk(b),))

        def post():
            pv2 = pst[:, b0 * 512:(b0 + 2) * 512]
            if g == 0:
                pv = pv2.rearrange("p (s t) -> p s t", s=4)
                ov = tokv(out2d, 0)
                o_hi, p_lo, o_lo, p_hi = ov[:, :, 1:256], pv[:, :, 0:255], ov[:, :, 0:255], pv[:, :, 1:256]
                e0, e1 = ov[:, :, 0], ov[:, :, 255]
            else:
                pv = pv2
                ov = out2d
                o_hi, p_lo, o_lo, p_hi = ov[:, 1:1024], pv[:, 0:1023], ov[:, 0:1023], pv[:, 1:1024]
                e0, e1 = ov[:, 0:1], ov[:, 1023:1024]
            P.op("act", lambda h: h.activation(out=ov, in_=pv, func=AF.Identity, scale=w1, bias=cbt), r=PK + DK, w=(okey,))
            P.op("dve", lambda h: h.scalar_tensor_tensor(out=o_hi, in0=p_lo, scalar=w0, in1=o_hi, op0=ALU.mult, op1=ALU.add), r=PK + (okey,) + DK, w=(okey,))
            P.op("dve", lambda h: h.scalar_tensor_tensor(out=o_lo, in0=p_hi, scalar=w2, in1=o_lo, op0=ALU.mult, op1=ALU.add), r=PK + (okey,) + DK, w=(okey,))
            P.op("dve", lambda h: h.tensor_scalar(out=e0, in0=e0, scalar1=nwb0, scalar2=None, op0=ALU.add), r=(okey,) + DK, w=(okey,))
            P.op("dve", lambda h: h.tensor_scalar(out=e1, in0=e1, scalar1=nwb2, scalar2=None, op0=ALU.add), r=(okey,) + DK, w=(okey,))
            if after is not None:
                after()

        if defer is not None:
            mm_t(0)
            defer.append((lambda: mm_t(1), post))
        else:
            mm_t(0)
            mm_t(1)
            post()

    def flush_deferred(defer):
        for m1, _ in defer:
            m1()
        for _, po in defer:
            po()
        del defer[:]

    def dbg_dump(tag):
        if dbg is not None and dbg == tag:
            def f(h):
                return [h.dma_start(out=yT[0, m], in_=xT[:, m, :]) for m in range(8)] + \
                       [h.dma_start(out=yT[1, m].bitcast(BF16)[:, 0:1024], in_=hT[:, m, :]) for m in range(8)]
            P.dma("sp", dbg_lane, f, r=tuple(xk(m, t) for m in range(8) for t in range(2)) + tuple(hk(m, t) for m in range(8) for t in range(2)), n=16)
            return True
        return False

    def proj_ln(g, l, sub, nk, src_fn, src_keys_fn, wload_fn, tail_hook=None):
        c = g
        sum_b = [4, 5]
        sq_b = [6, 7]
        st["nbank"] = 4
        pend = []

        def flush(n):
            while len(pend) > n:
                m, t, si = pend.pop(0)
                P.op("pe", lambda h, m=m, t=t, si=si: h.matmul(ps[sum_b[t]][:, :], lhsT=onesf[:, :].bitcast(F32R), rhs=xrt[:, si, :].bitcast(F32R),
                                                             start=(m == 0), stop=(m == 7)),
                     r=(("xr", si), ("c", "onesf")), w=(psk(sum_b[t]),))
                P.op("pe", lambda h, m=m, t=t, si=si: h.matmul(ps[sq_b[t]][:, :], lhsT=onesf[:, :].bitcast(F32R), rhs=sqt[:, si, :].bitcast(F32R),
                                                             start=(m == 0), stop=(m == 7)),
                     r=(("sq", si), ("c", "onesf")), w=(psk(sq_b[t]),))

        has_next = not (sub == 1 and l == depth - 1)
        o_g = o_lng + (l * 2 + sub) * 8
        o_b = o_lnb + (l * 2 + sub) * 8
        cnt = dict(it=0)

        def ln_stats(t):
            P.op("act", lambda h: h.activation(out=lnst[:, 0, :], in_=ps[sum_b[t]][:, :], func=AF.Square),
                 r=(psk(sum_b[t]),), w=(("ln", 0),))
            P.op("dve", lambda h: h.scalar_tensor_tensor(out=lnst[:, 0, :], in0=ps[sq_b[t]][:, :], scalar=LN_EPS, in1=lnst[:, 0, :],
                                                         op0=ALU.add, op1=ALU.subtract),
                 r=(psk(sq_b[t]), ("ln", 0)), w=(("ln", 0),))
            P.op("act", lambda h: h.activation(out=lnst[:, 0, :], in_=lnst[:, 0, :], func=AF.Sqrt),
                 r=(("ln", 0),), w=(("ln", 0),))
            P.op("dve", lambda h: h.reciprocal(out=lnst[:, 0, :], in_=lnst[:, 0, :]), r=(("ln", 0),), w=(("ln", 0),))
            P.op("dve", lambda h: h.tensor_tensor(out=lnst[:, 1, :], in0=ps[sum_b[t]][:, :], in1=lnst[:, 0, :], op=ALU.mult),
                 r=(psk(sum_b[t]), ("ln", 0)), w=(("ln", 1),))

        def ln_apply(m, t, eng="dve"):
            ei = (cnt["it"] % 4) if t == 1 else (2 + cnt["it"] % 2)
            cnt["it"] += 1
            P.op(eng, lambda h: h.tensor_tensor(out=etmp[:, ei, :], in0=xT[:, m, tsl(t)], in1=lnst[:, 0, :], op=ALU.mult),
                 r=(xk(m, t), ("ln", 0)), w=(("et", ei),))
            P.op(eng, lambda h: h.tensor_tensor(out=etmp[:, ei, :], in0=etmp[:, ei, :], in1=lnst[:, 1, :], op=ALU.subtract),
                 r=(("et", ei), ("ln", 1)), w=(("et", ei),))
            if has_next:
                P.op("act", lambda h: h.activation(out=hT[:, m, tsl(t)], in_=etmp[:, ei, :], func=AF.Identity,
                                                   scale=t4col(AAt, l, sub, m, c), bias=t4col(BBt, l, sub, m, c)),
                     r=(("et", ei), ("c", "AA", l, sub, c), ("c", "BB", l, sub, c)), w=(hk(m, t),))
            P.op("act", lambda h: h.activation(out=xT[:, m, tsl(t)], in_=etmp[:, ei, :], func=AF.Identity,
                                               scale=pp[:, o_g + m:o_g + m + 1], bias=pp[:, o_b + m:o_b + m + 1]),
                 r=(("et", ei), ("c", "pp")), w=(xk(m, t),))

        it = 0
        for t in range(2):
            for m in range(8):
                lhs_fn, wkey = wload_fn(m)
                b = bank()

                def mm(h, b=b, t=t, lhs_fn=lhs_fn):
                    ins = None
                    for k in range(nk):
                        ins = h.matmul(ps[b][:, :], lhsT=lhs_fn(k), rhs=src_fn(k, t), start=(k == 0), stop=(k == nk - 1))
                    return ins
                P.op("pe", mm, r=(wkey,) + tuple(src_keys_fn(k, t) for k in range(nk)), w=(psk(b),))
                flush(1)
                ei = it % 2
                si = it % 2
                it += 1
                P.op("act", lambda h, b=b, m=m, ei=ei: h.activation(out=etmp[:, ei, :], in_=ps[b][:, :], func=AF.Identity,
                                                                    scale=modcol(mod1, l, 3 * sub + 2, m, c), bias=t4col(gbt, l, sub, m, c)),
                     r=(psk(b), ("c", "gbt", l, sub, c)) + MODK(l), w=(("et", ei),))
                P.op("dve", lambda h, m=m, t=t, ei=ei: h.scalar_tensor_tensor(out=xT[:, m, tsl(t)], in0=xT[:, m, tsl(t)], scalar=ALPHA,
                                                                            in1=etmp[:, ei, :], op0=ALU.mult, op1=ALU.add),
                     r=(xk(m, t), ("et", ei)), w=(xk(m, t),))
                P.op("act", lambda h, m=m, t=t, si=si: h.activation(out=sqt[:, si, :].bitcast(F32R), in_=xT[:, m, tsl(t)], func=AF.Square),
                     r=(xk(m, t),), w=(("sq", si),))
                P.op("act", lambda h, m=m, t=t, si=si: h.activation(out=xrt[:, si, :].bitcast(F32R), in_=xT[:, m, tsl(t)], func=AF.Identity),
                     r=(xk(m, t),), w=(("xr", si),))
                pend.append((m, t, si))
                if t == 1:
                    ln_apply(m, 0)
            flush(0)
            if t == 0:
                ln_stats(0)
        st["nbank"] = 8
        if tail_hook is not None:
            tail_hook()
        ln_stats(1)
        for m in range(8):
            ln_apply(m, 1)

    def mixer_tail(g, l):
        if not (g == 0 and l + 1 < depth):
            return None

        return None

    def wload_sq(wmat, nk_):
        def wl(m):
            def ld(h, view):
                return [h.dma_start(out=view[:, 0:nk_ * 128].rearrange("p (k n) -> p k n", k=nk_),
                                    in_=wmat.rearrange("(k p) n -> p k n", p=128)[:, :, m * 128:(m + 1) * 128])]
            view, key = load_slab(ld, 1)
            wv = view[:, 0:nk_ * 128].rearrange("p (k n) -> p k n", k=nk_)
            return (lambda k: wv[:, k, :]), key
        return wl

    def ffn(g, l, hook=None, hook_end=None):
        P.phase = "g%d l%d ffn-up" % (g, l)
        P.fence("big")
        act = big[:, 0:22 * 1024].rearrange("p (c t) -> p c t", c=22)
        o_ib = PP_OFF["ff_in_b"][0] + l * 44
        o_cw = PP_OFF["ff_cw"][0] + l * 132
        o_cb = PP_OFF["ff_cb"][0] + l * 44
        conv_scalars(44, o_cw, 44, o_ib, o_cb)
        def ffn_load(cp):
            def ld(h, view, cp=cp):
                dv = view[:, 0:4096].rearrange("p (k a n) -> p k a n", k=8, a=2)
                sv = ff_in_w[l].rearrange("(k p) (a n) -> p k a n", p=128, a=2)
                return [h.dma_start(out=dv[:, :, a, :], in_=sv[:, :, a, cp * 256:(cp + 1) * 256]) for a in range(2)]
            return load_slab(ld, 2)
        PF = 2
        loaded = [ffn_load(i) for i in range(PF)]
        for cp in range(11):
            if hook is not None:
                hook(cp)
            if cp + PF < 11:
                loaded.append(ffn_load(cp + PF))
            view, wkey = loaded[cp]
            wv = view[:, 0:4096].rearrange("p (k a n) -> p k a n", k=8, a=2)
            dq = [] if cp == 0 else None
            for cc in range(2):
                ch = cp * 2 + cc
                cvi = [calloc(), calloc()]

                def tail(ch=ch, cvi=cvi):
                    P.op("act", lambda h: h.activation(out=cvb[:, cvi[0], :], in_=cvb[:, cvi[0], :], func=AF.Gelu_apprx_tanh),
                         r=(("cv", cvi[0]),), w=(("cv", cvi[0]),))
                    P.op(MULT_ENG, lambda h: h.tensor_tensor(out=act[:, ch, :], in0=cvb[:, cvi[0], :], in1=cvb[:, cvi[1], :], op=ALU.mult),
                         r=(("cv", cvi[0]), ("cv", cvi[1])), w=(("big", "act", ch),))
                for a in range(2):
                    ci = a * 22 + ch
                    cv = cvi[a]
                    upconv(g, (lambda k, a=a, cc=cc, wv=wv: wv[:, k, a, cc * 128:(cc + 1) * 128]), wkey,
                           pp[:, o_cw + ci:o_cw + ci + 1], pp[:, o_cw + 44 + ci:o_cw + 44 + ci + 1], pp[:, o_cw + 88 + ci:o_cw + 88 + ci + 1],
                           ci, cvb[:, cv, :], ("cv", cv), defer=dq, after=(tail if a == 1 else None))
            if dq:
                flush_deferred(dq)
        if hook_end is not None:
            hook_end()
        P.phase = "g%d l%d ffn-down+ln" % (g, l)
        proj_ln(g, l, 1, 22, lambda k, t: act[:, k, tsl(t)], lambda k, t: ("big", "act", k), wload_sq(ff_out_w[l], 22))

    def hyena(g, l):
        j = l // 2
        L = GROUPS[g]["L"]
        nseq = GROUPS[g]["nseq"]
        nj = L // 128
        if g == 0 and l + 1 < depth:
            MS.add_layer(l + 1)
        P.phase = "g%d l%d hy-inproj" % (g, l)
        P.fence("big")
        x0 = big[:, 0:8192].rearrange("p (c t) -> p c t", c=8)
        uu = big[:, 8192:16384].rearrange("p (c t) -> p c t", c=8)
        uT = big[:, 16384:18432].rearrange("p (k n) -> p k n", k=8)
        ksd_hi = big[:, 18432:18432 + 2 * nj * 256].rearrange("p (a k n) -> p a k n", a=2, k=nj)
        cvflat = cvb[:, :, :].rearrange("p a b -> p (a b)").bitcast(BF16)
        ks_lo = cvflat[:, 0:nj * 256].rearrange("p (k n) -> p k n", k=nj)
        kd_lo = cvflat[:, 2048:2048 + nj * 256].rearrange("p (k n) -> p k n", k=nj)
        ksd_lo = [ks_lo, kd_lo]
        CVK = tuple(("cv", i) for i in range(4))
        KSK = [CVK, CVK]
        o_ib = PP_OFF["hy_in_b"][0] + j * 24
        o_sw = PP_OFF["hy_sw"][0] + j * 72
        o_sb = PP_OFF["hy_sb"][0] + j * 24
        o_fb = PP_OFF["hy_fb"][0] + j * 8
        o_tn = PP_OFF["tn%d" % L][0]
        o_m0 = PP_OFF["mask0"][0]
        if True:
            P.fence("scr")
            lsb = carve()
            w1s = lsb([33, 64])
            w2s = lsb([64, 2, 64])
            wos = lsb([64, 2, 2, 256])
            fh = lsb([64, 2, 1024])
            ctmp = lsb([128, 4, 256])
            ftmp = ctmp[0:64, :, :].rearrange("p a b -> p (a b)").rearrange("p (a b) -> p a b", a=2)
            deltab = lsb([128, 2, 256])
            dect = lsb([128, 2, 256])
            kft = lsb([128, 2, 512])
            ksb = lsb([128, 2, 512])
            zTs = ksb[0:33, :, :].rearrange("p a b -> p (a b)")
            ZTK = (("scr", "K", 0), ("scr", "K", 1))
            ybuf = lsb([128, 2, 512], BF16)

            def fl(h):
                return [h.dma_start(out=w1s[:], in_=pos_w1[j]),
                        h.dma_start(out=w2s[:], in_=pos_w2[2 * j:2 * j + 2].rearrange("i k n -> k i n")),
]
            P.dma("sp", fin_lane, fl, w=(("scr", "in"),), n=2)
            P.dma("sp", zt_lane, lambda h: [h.dma_start(out=zTs[:, 0:L], in_=zT_d[L][:, :])], w=ZTK, n=1)

            conv_scalars(24, o_sw, 24, o_ib, o_sb)
            def hy_load(c):
                def ld(h, view, c=c):
                    dv = view[:, 0:3072].rearrange("p (k a n) -> p k a n", k=8, a=3)
                    sv = hy_in_w[j].rearrange("(k p) (a n) -> p k a n", p=128, a=3)
                    return [h.dma_start(out=dv[:, :, a, :], in_=sv[:, :, a, c * 128:(c + 1) * 128]) for a in range(3)]
                return load_slab(ld, 3)
            hloaded = [hy_load(0), hy_load(1)]
            for c in range(8):
                if c + 2 < 8:
                    hloaded.append(hy_load(c + 2))
                view, wkey = hloaded[c]
                wv = view[:, 0:3072].rearrange("p (k a n) -> p k a n", k=8, a=3)

                cvi = [calloc(), calloc(), calloc()]
                dq = [] if c == 0 else None

                def tail(c=c, cvi=cvi):
                    P.op("act", lambda h: h.activation(out=x0[:, c, :], in_=cvb[:, cvi[0], :], func=AF.Identity),
                         r=(("cv", cvi[0]),), w=(("big", "x0", c),))
                    P.op(MULT_ENG, lambda h: h.tensor_tensor(out=uu[:, c, :], in0=cvb[:, cvi[1], :], in1=cvb[:, cvi[2], :], op=ALU.mult),
                         r=(("cv", cvi[1]), ("cv", cvi[2])), w=(("big", "u", c),))
                for a in range(3):
                    ci = a * 8 + c
                    cv = cvi[a]
                    upconv(g, (lambda k, a=a, wv=wv: wv[:, k, a, :]), wkey,
                           pp[:, o_sw + ci:o_sw + ci + 1], pp[:, o_sw + 24 + ci:o_sw + 24 + ci + 1], pp[:, o_sw + 48 + ci:o_sw + 48 + ci + 1],
                           ci, cvb[:, cv, :], ("cv", cv), defer=dq, after=(tail if a == 2 else None))
                if dq:
                    flush_deferred(dq)

            P.phase = "g%d l%d hy-filter-mlp" % (g, l)
            FT0 = (("scr", "c", 0), ("scr", "c", 1))
            FT1 = (("scr", "c", 2), ("scr", "c", 3))

            def sin_layer(src_ps_fn, nt, dst, fcol, bcol, rk):
                for t in range(nt):
                    n = min(512, L)
                    sl = slice(t * 512, t * 512 + n)
                    b = src_ps_fn(t)
                    P.op("dve", lambda h, b=b, n=n: h.tensor_scalar(out=ftmp[:, 0, 0:n], in0=ps[b][0:64, 0:n], scalar1=pp[0:64, o_f + j:o_f + j + 1],
                                                                   scalar2=fb1[:, bcol:bcol + 1], op0=ALU.mult, op1=ALU.add),
                         r=(psk(b), ("c", "pp"), ("c", "fb1", bcol)), w=FT0)
                    P.op("dve", lambda h, n=n: h.tensor_scalar(out=ftmp[:, 1, 0:n], in0=ftmp[:, 0, 0:n], scalar1=float(1.0 / TWO_PI), scalar2=MAGIC,
                                                              op0=ALU.mult, op1=ALU.add), r=FT0, w=FT1)
                    P.op("dve", lambda h, n=n: h.tensor_scalar(out=ftmp[:, 1, 0:n], in0=ftmp[:, 1, 0:n], scalar1=MAGIC, scalar2=-TWO_PI,
                                                              op0=ALU.subtract, op1=ALU.mult), r=FT1, w=FT1)
                    P.op("dve", lambda h, n=n: h.tensor_tensor(out=ftmp[:, 0, 0:n], in0=ftmp[:, 0, 0:n], in1=ftmp[:, 1, 0:n], op=ALU.add),
                         r=FT0 + FT1, w=FT0)
                    P.op("act", lambda h, n=n, sl=sl: h.activation(out=dst[:, sl], in_=ftmp[:, 0, 0:n], func=AF.Sin),
                         r=FT0, w=(rk,))
            nt = max(1, L // 512)
            n512 = min(512, L)

            def l1(t):
                b = bank()
                P.op("pe", lambda h, b=b, t=t: h.matmul(ps[b][0:64, 0:n512], lhsT=w1s[:, :], rhs=zTs[:, t * 512:t * 512 + n512], start=True, stop=True),
                     r=(("scr", "in"),) + ZTK, w=(psk(b),))
                return b
            sin_layer(l1, nt, fh[:, 0, :], j, j, ("scr", "h", 0))
            if dbg == ("h1", g, l):
                P.dma("sp", dbg_lane, lambda h: [h.dma_start(out=yT[0, 0][0:64, 0:L], in_=fh[:, 0, 0:L])], r=(("scr", "h", 0),), n=1)
                raise _Stop()
            for i in range(2):
                def l2(t, i=i):
                    b = bank()
                    P.op("pe", lambda h, b=b, t=t, i=i: h.matmul(ps[b][0:64, 0:n512], lhsT=w2s[:, i, :], rhs=fh[:, i, t * 512:t * 512 + n512], start=True, stop=True),
                         r=(("scr", "in"), ("scr", "h", i)), w=(psk(b),))
                    return b
                sin_layer(l2, nt, fh[:, (i + 1) % 2, :], j, 2 + 2 * j + i, ("scr", "h", (i + 1) % 2))
                if dbg == ("h2", g, l) and i == 0:
                    P.dma("sp", dbg_lane, lambda h: [h.dma_start(out=yT[0, 0][0:64, 0:L], in_=fh[:, 1, 0:L]),
                                                      h.dma_start(out=yT[0, 1][0:64, 0:L], in_=fh[:, 0, 0:L])], r=(("scr", "h", 0), ("scr", "h", 1)), n=2)
                    raise _Stop()
            h3 = fh[:, 0, :]
            if dbg == ("h3", g, l):
                P.dma("sp", dbg_lane, lambda h: [h.dma_start(out=yT[0, 0][0:64, 0:L], in_=h3[:, 0:L])], r=(("scr", "h", 0),), n=1)
                raise _Stop()

            ACC = [0, 1, 2, 3]
            P.phase = "g%d l%d hy-dft" % (g, l)
            for ct in range(4):
                c0 = ct * 256
                for tc in range(8):
                    b = 7

                    def tr(h, tc=tc, ct=ct):
                        psb = ps[7][:, :].bitcast(BF16)
                        ins = None
                        for cc in range(2):
                            ins = h.transpose(psb[:, cc * 128:(cc + 1) * 128], uu[:, 2 * ct + cc, tc * 128:(tc + 1) * 128], ident[:, :])
                        return ins
                    P.op("pe", tr, r=(("big", "u", 2 * ct), ("big", "u", 2 * ct + 1), ("c", "ident")), w=(psk(7),))
                    P.op("act", lambda h, tc=tc: h.activation(out=uT[:, tc, :], in_=ps[7][:, :].bitcast(BF16)[:, 0:256], func=AF.Identity),
                         r=(psk(7),), w=(("big", "uT", tc),))
                wb = ct % 2
                P.dma("sp", wo_lanes[wb], lambda h, wb=wb, c0=c0: [
                    h.dma_start(out=wos[:, wb, :, :], in_=pos_wout[j].rearrange("k (a n) -> k a n", a=2)[:, :, c0:c0 + 256]),
                    h.dma_start(out=deltab[:, wb, :], in_=bass.AP(delta_t, c0, [[0, 128], [1, 256]]))], w=(("scr", "wo", wb),), n=2)
                for tc in range(nj):
                    b = 6
                    P.op("pe", lambda h, tc=tc, wb=wb: h.matmul(ps[6][:, :], lhsT=h3[:, tc * 128:(tc + 1) * 128],
                                                             rhs=wos[:, wb, :, :].rearrange("k a n -> k (a n)"), start=True, stop=True),
                         r=(("scr", "h", 0), ("scr", "wo", wb)), w=(psk(6),))
                    di = tc % 2
                    P.op("act", lambda h, tc=tc, di=di, wb=wb: h.activation(out=dect[:, di, :], in_=deltab[:, wb, :], func=AF.Exp,
                                                                          scale=pp[:, o_tn + tc:o_tn + tc + 1]),
                         r=(("scr", "wo", wb), ("c", "pp")), w=(("scr", "dec", di),))
                    P.op("dve", lambda h, di=di: h.tensor_tensor(out=kft[:, di, 0:256], in0=ps[6][:, 0:256], in1=dect[:, di, :], op=ALU.mult),
                         r=(psk(6), ("scr", "dec", di)), w=(("scr", "kf", di),))
                    if tc == 0:
                        P.op("dve", lambda h, di=di: h.tensor_scalar(out=dect[:, di, :], in0=dect[:, di, :], scalar1=pp[:, o_m0:o_m0 + 1], scalar2=None,
                                                                    op0=ALU.mult), r=(("scr", "dec", di), ("c", "pp")), w=(("scr", "dec", di),))
                    P.op("dve", lambda h, di=di: h.tensor_tensor(out=kft[:, di, 256:512], in0=ps[6][:, 256:512], in1=dect[:, di, :], op=ALU.mult),
                         r=(psk(6), ("scr", "dec", di)), w=(("scr", "kb", di),))
                    for a_, op_ in ((0, ALU.add), (1, ALU.subtract)):
                        P.op("dve", lambda h, di=di, a_=a_, op_=op_, tc=tc: h.tensor_tensor(out=ksd_hi[:, a_, tc, :], in0=kft[:, di, 0:256], in1=kft[:, di, 256:512], op=op_),
                             r=(("scr", "kf", di), ("scr", "kb", di)), w=(("big", "khi", a_),))
                if dbg == ("ks", g, l) and ct == 0:
                    def dks(h):
                        return [h.dma_start(out=yT[0, a_].bitcast(BF16)[:, 0:nj * 256], in_=ksd_hi[:, a_].rearrange("p k n -> p (k n)")) for a_ in range(2)]
                    P.dma("sp", dbg_lane, dks, r=(("big", "khi", 0), ("big", "khi", 1)), n=2)
                    raise _Stop()
                yi = 0
                fgi = [0]
                pendI = [None]
                for jf in range(nj):
                    fb_ = fgi[0] % 2
                    fgi[0] += 1
                    cvF = cvb[:, fb_, :].bitcast(BF16)
                    cvG = cvb[:, 2 + fb_, :].bitcast(BF16)
                    fkey, gkey = ("cv", fb_), ("cv", 2 + fb_)
                    P.dma("sp", fg_lanes[fb_], lambda h, jf=jf, cvF=cvF: [h.dma_start(
                        out=cvF[:, 0:nj * 256].rearrange("p (k n) -> p k n", k=nj), in_=F_d[L][jf][:, 0])], w=(fkey,), n=1)
                    P.dma("sp", fg_lanes[2 + fb_], lambda h, jf=jf, cvG=cvG: [h.dma_start(
                        out=cvG[:, 0:2 * L].rearrange("p (a n) -> p a n", a=2), in_=G_d[L][jf][:, 0])], w=(gkey,), n=1)
                    Fv = cvF[:, 0:nj * 256].rearrange("p (k n) -> p k n", k=nj)
                    Gv = cvG[:, 0:2 * L].rearrange("p (a n) -> p a n", a=2)

                    def mmK(h, Fv=Fv):
                        ins = None
                        for a in range(2):
                            for tc in range(nj):
                                ins = h.matmul(ps[6][:, a * 256:(a + 1) * 256], lhsT=Fv[:, tc, a * 128:(a + 1) * 128], rhs=ksd_hi[:, a, tc, :],
                                               start=(tc == 0), stop=(tc == nj - 1))
                        return ins
                    P.op("pe", mmK, r=(fkey, ("big", "khi", 0), ("big", "khi", 1)), w=(psk(6),))
                    kb_ = jf % 2
                    P.op("act", lambda h, kb_=kb_: h.activation(out=ksb[:, kb_, :], in_=ps[6][:, :], func=AF.Identity), r=(psk(6),), w=(("scr", "K", kb_),))
                    for s in range(nseq):
                        if g == 0 and l + 1 < depth:
                            MS.step(7)
                            if l == 0:
                                MS.step(7)
                        ub = 4 + (yi % 2)

                        def mmU(h, Fv=Fv, s=s, ub=ub):
                            ins = None
                            for a in range(2):
                                for tc in range(nj):
                                    ins = h.matmul(ps[ub][:, a * 256:(a + 1) * 256], lhsT=Fv[:, tc, a * 128:(a + 1) * 128], rhs=uT[:, s * nj + tc, :],
                                                   start=(tc == 0), stop=(tc == nj - 1))
                            return ins
                        P.op("pe", mmU, r=(fkey,) + tuple(("big", "uT", s * nj + tc) for tc in range(nj)), w=(psk(ub),))
                        prev = pendI.pop(0)
                        if prev is not None:
                            prev()
                        yb = yi % 2
                        yi += 1
                        Ure, Uim = ps[ub][:, 0:256], ps[ub][:, 256:512]
                        Kre, Kim = ksb[:, kb_, 0:256], ksb[:, kb_, 256:512]
                        rk = (psk(ub), ("scr", "K", kb_))
                        P.op("dve", lambda h, Ure=Ure, Kre=Kre: h.tensor_tensor(out=ctmp[:, 0, :], in0=Ure, in1=Kre, op=ALU.mult), r=rk, w=(("scr", "c", 0),))
                        P.op("dve", lambda h, Uim=Uim, Kim=Kim: h.tensor_tensor(out=ctmp[:, 1, :], in0=Uim, in1=Kim, op=ALU.mult), r=rk, w=(("scr", "c", 1),))
                        P.op("dve", lambda h, yb=yb: h.tensor_tensor(out=ybuf[:, yb, 0:256], in0=ctmp[:, 0, :], in1=ctmp[:, 1, :], op=ALU.subtract),
                             r=(("scr", "c", 0), ("scr", "c", 1)), w=(("scr", "yre", yb),))
                        P.op("dve", lambda h, Ure=Ure, Kim=Kim: h.tensor_tensor(out=ctmp[:, 2, :], in0=Ure, in1=Kim, op=ALU.mult), r=rk, w=(("scr", "c", 2),))
                        P.op("dve", lambda h, Uim=Uim, Kre=Kre: h.tensor_tensor(out=ctmp[:, 3, :], in0=Uim, in1=Kre, op=ALU.mult), r=rk, w=(("scr", "c", 3),))
                        P.op("dve", lambda h, yb=yb: h.tensor_tensor(out=ybuf[:, yb, 256:512], in0=ctmp[:, 2, :], in1=ctmp[:, 3, :], op=ALU.add),
                             r=(("scr", "c", 2), ("scr", "c", 3)), w=(("scr", "yim", yb),))

                        def mmI(h, Gv=Gv, s=s, yb=yb, jf=jf):
                            ins = None
                            for cc in range(2):
                                for a in range(2):
                                    lhs = ybuf[:, yb, a * 256 + cc * 128:a * 256 + (cc + 1) * 128]
                                    first = (jf == 0 and a == 0)
                                    last = (jf == nj - 1 and a == 1)
                                    if L == 1024:
                                        for tt in range(2):
                                            ins = h.matmul(ps[ACC[cc * 2 + tt]][:, :], lhsT=lhs, rhs=Gv[:, a, tt * 512:(tt + 1) * 512], start=first, stop=last)
                                    else:
                                        ins = h.matmul(ps[ACC[s]][:, cc * 256:(cc + 1) * 256], lhsT=lhs, rhs=Gv[:, a, :],
                                                       start=(first and cc == 0), stop=(last and cc == 1))
                            return ins
                        nxt = (lambda mmI=mmI, gkey=gkey, yb=yb: P.op("pe", mmI, r=(gkey, ("scr", "yre", yb), ("scr", "yim", yb)), w=tuple(psk(a_) for a_ in ACC)))
                        pendI.append(nxt)
                pendI.pop(0)()
                for cc in range(2):
                    c = 2 * ct + cc
                    for q in range(2 if L == 1024 else 4):
                        if L == 1024:
                            accap = ps[ACC[cc * 2 + q]][:, :]
                            tok = slice(q * 512, (q + 1) * 512)
                            n = 512
                            ab = ACC[cc * 2 + q]
                        else:
                            accap = ps[ACC[q]][:, cc * 256:(cc + 1) * 256]
                            tok = slice(q * 256, (q + 1) * 256)
                            n = 256
                            ab = ACC[q]
                        ei = (cc * 4 + q) % 4
                        P.op("dve", lambda h, c=c, tok=tok, accap=accap, ei=ei, n=n: h.scalar_tensor_tensor(
                            out=etmp[:, ei, 0:n], in0=uu[:, c, tok], scalar=pp[:, o_fb + c:o_fb + c + 1], in1=accap, op0=ALU.mult, op1=ALU.add),
                            r=(("big", "u", c), psk(ab), ("c", "pp")), w=(("et", ei),))
                        P.op("dve", lambda h, c=c, tok=tok, ei=ei, n=n: h.tensor_tensor(out=x0[:, c, tok], in0=etmp[:, ei, 0:n], in1=x0[:, c, tok], op=ALU.mult),
                             r=(("et", ei), ("big", "x0", c)), w=(("big", "x0", c),))
            if g == 0 and l == 0:
                MS.drain(0, 7)
                mod_finish(0)
            P.phase = "g%d l%d hy-out+ln" % (g, l)
            proj_ln(g, l, 0, 8, lambda k, t: x0[:, k, tsl(t)], lambda k, t: ("big", "x0", k), wload_sq(hy_out_w[j], 8), tail_hook=mixer_tail(g, l))

    def attention(g, l):
        j = l // 2
        latent = (g == 1)
        if g == 0 and l + 1 < depth:
            MS.add_layer(l + 1)
        P.phase = "g%d l%d at-qkv" % (g, l)
        P.fence("big")
        P.fence("scr")
        qT = big[:, 0:8192].rearrange("p (h t) -> p h t", h=8)
        oT = big[:, 8192:16384].rearrange("p (h t) -> p h t", h=8)
        kT = big[:, 16384:19456].rearrange("p (h t) -> p h t", h=2)
        Vv = big[:, 19456:22528].rearrange("p (k n) -> p k n", k=12)
        lsb = carve()
        qkvb = lsb([128, 1536])
        qgain = lsb([128, 128])
        kgain = lsb([128, 128])
        rope = lsb([128, 8, 2, 64])
        kcb = lsb([128, 4, 256], BF16)
        mark = lsb.off[0]
        NQ = 4
        qf = lsb([128, NQ, 512])
        qsq = lsb([128, 2, 512])
        qst = lsb([128, NQ, 4])
        qr = lsb([128, NQ, 512], BF16)
        rtmp = lsb([128, 4, 256])
        lsb.off[0] = mark
        pT = lsb([128, 3, 512], BF16)
        rdt = lsb([128, 2, 512])
        scale = float(128 ** -0.5)

        def al(h):
            return [h.dma_start(out=qkvb[:], in_=bass.AP(qkvb_t, j * 1536, [[0, 128], [1, 1536]])),
                    h.dma_start(out=qgain[:], in_=bass.AP(qgain_t, j * 128, [[0, 128], [1, 128]])),
                    h.dma_start(out=kgain[:], in_=bass.AP(kgain_t, j * 128, [[0, 128], [1, 128]])),
                    h.dma_start(out=rope[:], in_=rope_d[:, :, :, :])]
        P.dma("sp", ain_lane, al, w=(("scr", "ain"),), n=4)
        if latent:
            P.dma("pool", kc_lane, lambda h: [h.dma_start(out=kcb[:], in_=cache_k[j].rearrange("(c p) n -> p c n", p=128))],
                  w=(("scr", "kc"),), n=1)
            P.dma("pool", vc_lane, lambda h: [h.dma_start(out=Vv[:, 8:12, :], in_=cache_v[j].rearrange("(c p) n -> p c n", p=128))],
                  w=tuple(("big", "V", 8 + i) for i in range(4)), n=1)

        it = 0
        def qkv_load(ct):
            def ld(h, view, ct=ct):
                return [h.dma_start(out=view[:, 0:4096].rearrange("p (k n) -> p k n", k=8),
                                    in_=at_qkv_w[j].rearrange("(k p) n -> p k n", p=128)[:, :, ct * 512:(ct + 1) * 512])]
            return load_slab(ld, 1)
        qloaded = [qkv_load(ct) for ct in range(3)]
        def make_item(ct, tc, it, wv, wkey):
            nh = 4 if ct < 2 else 2
            nw = nh * 128
            gain = qgain if ct < 2 else kgain
            qi = it % NQ
            sqi = it % 2
            kq = ("scr", "qf", qi)
            ks_ = ("scr", "qst", qi)
            kr = ("scr", "qr", qi)
            qv = qf[:, qi, 0:nw].rearrange("p (a b) -> p a b", a=nh)
            st_ = {}

            def s0():
                b = bank()
                while b >= 6:
                    b = bank()
                if g == 0 and l + 1 < depth:
                    MS.step(6)

                def mm(h):
                    ins = None
                    for k in range(8):
                        ins = h.matmul(ps[b][:, :], lhsT=hT[:, k, tc * 128:(tc + 1) * 128], rhs=wv[:, k, :], start=(k == 0), stop=(k == 7))
                    return ins
                P.op("pe", mm, r=(wkey,) + tuple(hk(k, tc // 4) for k in range(8)), w=(psk(b),))
                P.op("dve", lambda h: h.tensor_tensor(out=qf[:, qi, :], in0=ps[b][:, :], in1=qkvb[:, ct * 512:(ct + 1) * 512], op=ALU.add),
                     r=(psk(b), ("scr", "ain")), w=(kq,))
                P.op("act", lambda h: h.activation(out=qsq[:, sqi, 0:nw], in_=qf[:, qi, 0:nw], func=AF.Square), r=(kq,), w=(("scr", "qsq", sqi),))

            def s1():
                P.op("dve", lambda h: h.tensor_reduce(out=qst[:, qi, 0:nh], in_=qsq[:, sqi, 0:nw].rearrange("p (a b) -> p a b", a=nh),
                                                      axis=AX.X, op=ALU.add), r=(("scr", "qsq", sqi),), w=(ks_,))
                P.op("dve", lambda h: h.tensor_scalar(out=qst[:, qi, 0:nh], in0=qst[:, qi, 0:nh], scalar1=1.0 / 128.0, scalar2=QK_EPS,
                                                      op0=ALU.mult, op1=ALU.add), r=(ks_,), w=(ks_,))
                P.op("act", lambda h: h.activation(out=qst[:, qi, 0:nh], in_=qst[:, qi, 0:nh], func=AF.Sqrt), r=(ks_,), w=(ks_,))

            def s2():
                P.op("dve", lambda h: h.reciprocal(out=qst[:, qi, 0:nh], in_=qst[:, qi, 0:nh]), r=(ks_,), w=(ks_,))
                P.op("dve", lambda h: h.tensor_tensor(out=qv, in0=qv, in1=qst[:, qi, 0:nh].unsqueeze(2).to_broadcast([128, nh, 128]), op=ALU.mult),
                     r=(kq, ks_), w=(kq,))
                P.op("pool" if latent else "dve", lambda h: h.tensor_tensor(out=qv, in0=qv, in1=gain[:, :].unsqueeze(1).to_broadcast([128, nh, 128]),
                                                                            op=ALU.mult), r=(kq, ("scr", "ain")), w=(kq,))

            def s3():
                if latent:
                    q4 = qf[:, qi, 0:nw].rearrange("p (a b two) -> p a b two", a=nh, two=2)
                    r4 = qr[:, qi, 0:nw].rearrange("p (a b two) -> p a b two", a=nh, two=2)
                    ev, od = q4[:, :, :, 0], q4[:, :, :, 1]
                    cosb = rope[:, tc, 0, :].unsqueeze(1).to_broadcast([128, nh, 64])
                    sinb = rope[:, tc, 1, :].unsqueeze(1).to_broadcast([128, nh, 64])
                    nr = nh * 64
                    rv = [rtmp[:, i, 0:nr].rearrange("p (a b) -> p a b", a=nh) for i in range(4)]
                    rr = (kq, ("scr", "ain"))
                    P.op("dve", lambda h: h.tensor_tensor(out=rv[0], in0=ev, in1=cosb, op=ALU.mult), r=rr, w=(("scr", "rt", 0),))
                    P.op("dve", lambda h: h.tensor_tensor(out=rv[1], in0=od, in1=sinb, op=ALU.mult), r=rr, w=(("scr", "rt", 1),))
                    P.op("dve", lambda h: h.tensor_tensor(out=r4[:, :, :, 0], in0=rv[0], in1=rv[1], op=ALU.subtract),
                         r=(("scr", "rt", 0), ("scr", "rt", 1)), w=(kr,))
                    P.op("pool", lambda h: h.tensor_tensor(out=rv[2], in0=ev, in1=sinb, op=ALU.mult), r=rr, w=(("scr", "rt", 2),))
                    P.op("pool", lambda h: h.tensor_tensor(out=rv[3], in0=od, in1=cosb, op=ALU.mult), r=rr, w=(("scr", "rt", 3),))
                    P.op("pool", lambda h: h.tensor_tensor(out=r4[:, :, :, 1], in0=rv[2], in1=rv[3], op=ALU.add),
                         r=(("scr", "rt", 2), ("scr", "rt", 3), kr), w=(kr,))
                else:
                    P.op("act", lambda h: h.activation(out=qr[:, qi, 0:nw], in_=qf[:, qi, 0:nw], func=AF.Identity), r=(kq,), w=(kr,))
                if ct == 2:
                    P.op("act", lambda h: h.activation(out=Vv[:, tc, :], in_=qf[:, qi, 256:512], func=AF.Identity), r=(kq,), w=(("big", "V", tc),))
                    if not latent:
                        s_, r0 = tc // 2, (tc % 2) * 128
                        P.dma("sp", kv_lanes[qi], lambda h: [
                            h.dma_start(out=nk_o[s_, j, r0:r0 + 128, :], in_=qf[:, qi, 0:256]),
                            h.dma_start(out=nv_o[s_, j, r0:r0 + 128, :], in_=qf[:, qi, 256:512])], r=(kq,), n=2)

            def s4():
                def tr(h):
                    psb = ps[7][:, :].bitcast(BF16)
                    ins = None
                    for hh in range(nh):
                        ins = h.transpose(psb[:, hh * 128:(hh + 1) * 128], qr[:, qi, hh * 128:(hh + 1) * 128], ident[:, :])
                    return ins
                P.op("pe", tr, r=(kr, ("c", "ident")), w=(psk(7),))
                if ct < 2:
                    P.op("act", lambda h: h.activation(out=qT[:, ct * 4:ct * 4 + 4, tc * 128:(tc + 1) * 128],
                                                       in_=ps[7][:, :].bitcast(BF16)[:, 0:512].rearrange("p (a b) -> p a b", a=4), func=AF.Identity),
                         r=(psk(7),), w=tuple(("big", "qT", ct * 4 + hh) for hh in range(4)))
                else:
                    P.op("act", lambda h: h.activation(out=kT[:, :, tc * 128:(tc + 1) * 128],
                                                       in_=ps[7][:, :].bitcast(BF16)[:, 0:256].rearrange("p (a b) -> p a b", a=2), func=AF.Identity),
                         r=(psk(7),), w=(("big", "kT", 0), ("big", "kT", 1)))
            return [s0, s1, s2, s3, s4]

        items = []
        for ct in range(3):
            view, wkey = qloaded[ct]
            wv = view[:, 0:4096].rearrange("p (k n) -> p k n", k=8)
            for tc in range(8):
                items.append(make_item(ct, tc, len(items), wv, wkey))
        NST = 5
        for n in range(len(items) + NST - 1):
            for sidx in range(NST):
                i = n - sidx
                if 0 <= i < len(items):
                    items[i][sidx]()
        if latent:
            for kc in range(4):
                def trc(h, kc=kc):
                    psb = ps[7][:, :].bitcast(BF16)
                    ins = None
                    for kvh in range(2):
                        ins = h.transpose(psb[:, kvh * 128:(kvh + 1) * 128], kcb[:, kc, kvh * 128:(kvh + 1) * 128], ident[:, :])
                    return ins
                P.op("pe", trc, r=(("scr", "kc"), ("c", "ident")), w=(psk(7),))
                P.op("act", lambda h, kc=kc: h.activation(out=kT[:, :, 1024 + kc * 128:1024 + (kc + 1) * 128],
                                                        in_=ps[7][:, :].bitcast(BF16)[:, 0:256].rearrange("p (a b) -> p a b", a=2), func=AF.Identity),
                     r=(psk(7),), w=(("big", "kT", 0), ("big", "kT", 1)))

        P.fence("scr")
        P.phase = "g%d l%d at-core" % (g, l)
        units = []
        if latent:
            for qt in range(2):
                for kvh in range(2):
                    for hh in range(4):
                        hd_ = kvh * 4 + hh
                        units.append(dict(kvh=kvh, rhs=qT[:, hd_, qt * 512:(qt + 1) * 512], rk=(("big", "qT", hd_),), kcs=list(range(12)),
                                          out=oT[:, hd_, qt * 512:(qt + 1) * 512], ok=(("big", "oT", hd_),), v3=False))
        else:
            for s_ in range(4):
                for kvh in range(2):
                    for hp in range(2):
                        h0 = kvh * 4 + hp * 2
                        units.append(dict(kvh=kvh, rhs=qT[:, h0:h0 + 2, s_ * 256:(s_ + 1) * 256], rk=(("big", "qT", h0), ("big", "qT", h0 + 1)),
                                          kcs=[2 * s_, 2 * s_ + 1], out=oT[:, h0:h0 + 2, s_ * 256:(s_ + 1) * 256],
                                          ok=(("big", "oT", h0), ("big", "oT", h0 + 1)), v3=True))
        srot = 0
        for ui, u in enumerate(units):
            ob = 3 + ui % 2
            db = 5 + ui % 2
            kvh = u["kvh"]
            pend = None
            nk_ = len(u["kcs"])

            def od(idx, kc, pi, u=u, ob=ob, db=db, kvh=kvh, nk_=nk_):
                def f(h):
                    h.matmul(ps[ob][:, :], lhsT=Vv[:, kc, kvh * 128:(kvh + 1) * 128], rhs=pT[:, pi, :], start=(idx == 0), stop=(idx == nk_ - 1))
                    return h.matmul(ps[db][:, :], lhsT=onesb[:, :], rhs=pT[:, pi, :], start=(idx == 0), stop=(idx == nk_ - 1))
                P.op("pe", f, r=(("big", "V", kc), ("scr", "pT", pi), ("c", "onesb")), w=(psk(ob), psk(db)))
            for idx, kc in enumerate(u["kcs"]):
                sb_ = srot % 3
                pi = srot % 3
                srot += 1
                P.op("pe", lambda h, sb_=sb_, kc=kc, u=u, kvh=kvh: h.matmul(ps[sb_][:, :], lhsT=kT[:, kvh, kc * 128:(kc + 1) * 128], rhs=u["rhs"],
                                                                         start=True, stop=True),
                     r=(("big", "kT", kvh),) + u["rk"], w=(psk(sb_),))
                P.op("act", lambda h, sb_=sb_, pi=pi: h.activation(out=pT[:, pi, :], in_=ps[sb_][:, :], func=AF.Exp, scale=scale),
                     r=(psk(sb_),), w=(("scr", "pT", pi),))
                if pend is not None:
                    od(*pend)
                pend = (idx, kc, pi)
            od(*pend)
            ri = ui % 2
            P.op("dve", lambda h, db=db, ri=ri: h.reciprocal(out=rdt[:, ri, :], in_=ps[db][:, :]), r=(psk(db),), w=(("scr", "rd", ri),))
            if u["v3"]:
                P.op("dve", lambda h, ob=ob, ri=ri, u=u: h.tensor_tensor(out=u["out"], in0=ps[ob][:, :].rearrange("p (a b) -> p a b", a=2),
                                                                      in1=rdt[:, ri, :].rearrange("p (a b) -> p a b", a=2), op=ALU.mult),
                     r=(psk(ob), ("scr", "rd", ri)), w=u["ok"])
            else:
                P.op("dve", lambda h, ob=ob, ri=ri, u=u: h.tensor_tensor(out=u["out"], in0=ps[ob][:, :], in1=rdt[:, ri, :], op=ALU.mult),
                     r=(psk(ob), ("scr", "rd", ri)), w=u["ok"])
        P.phase = "g%d l%d at-out+ln" % (g, l)
        proj_ln(g, l, 0, 8, lambda k, t: oT[:, k, tsl(t)], lambda k, t: ("big", "oT", k), wload_sq(at_o_w[j], 8), tail_hook=mixer_tail(g, l))

    allx = tuple(xk(m, t) for m in range(8) for t in range(2))
    stop = False
    for g in range(ngroups):
        c = g
        P.dma("sp", x_lane, lambda h, g=g: [h.dma_start(out=xT[:, m, :], in_=xin[g, m]) for m in range(8)], w=allx, n=8)
        for m in range(8):
            for t in range(2):
                P.op("dve", lambda h, m=m, t=t, c=c: h.tensor_scalar(out=hT[:, m, tsl(t)], in0=xT[:, m, tsl(t)], scalar1=modcol(mod1, 0, 1, m, c),
                                                                   scalar2=modcol(mod, 0, 0, m, c), op0=ALU.mult, op1=ALU.add),
                     r=(xk(m, t),) + MODK(0), w=(hk(m, t),))
        for l in range(depth):
            try:
                if l % 2 == 0:
                    hyena(g, l)
                else:
                    attention(g, l)
            except _Stop:
                stop = True
                break
            if dbg_dump((g, l, 0)):
                stop = True
                break
            if g == 0 and l + 1 < depth:
                def hk_(cp):
                    MS.step(bank())
                    MS.step(bank())

                def hk_end(l=l):
                    MS.drain(l + 1, bank())
                    mod_finish(l + 1)
                ffn(g, l, hook=hk_, hook_end=hk_end)
            else:
                ffn(g, l)
            if dbg_dump((g, l, 1)):
                stop = True
                break
        if stop:
            break
        P.dma("sp", y_lane, lambda h, g=g: [h.dma_start(out=yT[g, m], in_=xT[:, m, :]) for m in range(8)], r=allx, n=8)

    P.finalize()
    for ln in P.dma_lanes:
        ln.sem = semaphore("d_" + ln.name)
    with nc.Block() as block:
        @block.tensor
        def _(h):
            P.emit("pe", h)

        @block.scalar
        def _(h):
            P.emit("act", h)

        @block.vector
        def _(h):
            P.emit("dve", h)

        @block.gpsimd
        def _(h):
            P.emit("pool", h)

        @block.sync
        def _(h):
            P.emit("sp", h)
    es.close()
    nc._prog_stats = {e: len(P.eng_ops[e]) for e in P.ENGS}
    nc._pe_log = P.pe_log
    return nc


_NC_CACHE = {}


def make_in_maps(inp):
    f32 = lambda a: np.ascontiguousarray(np.asarray(a, np.float32))
    consts = make_consts()
    pp = pack_pp(inp)
    shared = {
        "pp": pp,
        "w_mod": f32(inp["w_mod"]), "hy_in_w": f32(inp["hy_in_w"]), "hy_out_w": f32(inp["hy_out_w"]),
        "at_qkv_w": f32(inp["at_qkv_w"]), "at_o_w": f32(inp["at_o_w"]), "ff_in_w": f32(inp["ff_in_w"]), "ff_out_w": f32(inp["ff_out_w"]),
        "pos_w1": f32(inp["hy_pos_w1"]), "pos_w2": f32(inp["hy_pos_w2"]).reshape(4, 64, 64), "pos_wout": f32(inp["hy_pos_wout"]),
        "qkvb": f32(inp["at_qkv_b"]), "qgain": f32(inp["at_q_gain"]), "kgain": f32(inp["at_k_gain"]),
        "ident": consts["ident"], "zT256": consts["zT256"], "zT1024": consts["zT1024"],
        "F256": consts["F256"], "F1024": consts["F1024"], "G256": consts["G256"], "G1024": consts["G1024"],
        "delta": consts["delta"], "rope": consts["rope"],
    }
    xp = f32(inp["x_prompt"])
    xs = f32(inp["x_sample"])
    ck = f32(inp["cache_k"])
    cv = f32(inp["cache_v"])
    cc = f32(inp["c"])
    cctx = f32(inp["c_ctx"])
    maps = []
    for core in range(8):
        x0 = xp[4 * core:4 * core + 4].reshape(1024, 1024)
        x1 = xs[core]
        xin = np.stack([x0.T.reshape(8, 128, 1024), x1.T.reshape(8, 128, 1024)], 0)
        cond = np.stack([cctx, cc[core]], 0).reshape(2, 8, 128).transpose(2, 1, 0)
        m = dict(shared)
        m["xin"] = np.ascontiguousarray(xin)
        m["cond"] = np.ascontiguousarray(cond)
        m["cache_k"] = np.ascontiguousarray(ck[core].reshape(2, 512, 256))
        m["cache_v"] = np.ascontiguousarray(cv[core].reshape(2, 512, 256))
        maps.append(m)
    return maps


def kernel(**inp):
    if "nc" not in _NC_CACHE:
        _NC_CACHE["nc"] = build_nc()
    nc = _NC_CACHE["nc"]
    maps = make_in_maps(inp)
    res = run_bass_kernel_spmd(nc, maps, core_ids=list(range(8)))
    y_prompt = np.zeros((32, 256, 1024), np.float32)
    y_sample = np.zeros((8, 1024, 1024), np.float32)
    nk = np.zeros((32, 2, 256, 2, 128), np.float32)
    nv = np.zeros((32, 2, 256, 2, 128), np.float32)
    for core in range(8):
        r = res.results[core]
        yT = np.asarray(r["yT"], np.float32)
        y_prompt[4 * core:4 * core + 4] = yT[0].reshape(1024, 1024).T.reshape(4, 256, 1024)
        y_sample[core] = yT[1].reshape(1024, 1024).T
        nk[4 * core:4 * core + 4] = np.asarray(r["nk"], np.float32).reshape(4, 2, 256, 2, 128)
        nv[4 * core:4 * core + 4] = np.asarray(r["nv"], np.float32).reshape(4, 2, 256, 2, 128)
    return (y_prompt, y_sample, nk, nv)
```

```python
import contextlib
import numpy as np
import ml_dtypes
import concourse.bass as bass
import concourse.mybir as mybir
from concourse.bass_utils import run_bass_kernel_spmd

F32 = mybir.dt.float32
BF16 = mybir.dt.bfloat16
F32R = mybir.dt.float32r
AF = mybir.ActivationFunctionType
ALU = mybir.AluOpType
AX = mybir.AxisListType

D = 1024
NM = 8
TOK = 1024
DEPTH = 4
DFF = 2816
NFC = 22
LN_EPS = 1e-5
QK_EPS = 1e-6
ALPHA = float((2 * DEPTH) ** 0.25)
GROUPS = [dict(nseq=4, L=256), dict(nseq=1, L=1024)]
MAGIC = 12582912.0
TWO_PI = float(2 * np.pi)
NSLOT = 3
MULT_ENG = "pool"
SLOT_EL = 4096

DBG = None


class _Stop(Exception):
    pass


class _CountProxy:
    def __init__(self, h):
        self.h = h
        self.n = 0

    def matmul(self, *a, **k):
        self.n += 1
        return self.h.matmul(*a, **k)

    def transpose(self, *a, **k):
        self.n += 1
        return self.h.transpose(*a, **k)

    def __getattr__(self, name):
        return getattr(self.h, name)


class Lane:
    def __init__(self, name, step):
        self.name = name
        self.step = step
        self.ops = []
        self.sem = None
        self.cum = None

    def count_at(self, seq):
        return self.cum[seq]


class Op:
    __slots__ = ("eng", "lane", "seq", "fn", "deps", "need", "vc", "isdma", "waits", "ninc", "phase")


class Prog:
    ENGS = ("pe", "act", "dve", "pool", "sp")

    def __init__(self):
        self.lanes = {e: Lane(e, 1) for e in ("pe", "act", "dve", "pool")}
        self.dma_lanes = []
        self.eng_ops = {e: [] for e in self.ENGS}
        self.all_ops = []
        self.last_w = {}
        self.readers = {}
        self.fences = {}
        self.store_lanes = []
        self.gfence = {}
        self.phase = ""
        self.pe_log = []

    def dma_lane(self, name, store=False):
        ln = Lane(name, 16)
        self.dma_lanes.append(ln)
        if store:
            self.store_lanes.append(ln)
        return ln

    def _add(self, eng, lane, fn, reads, writes, isdma, ninc=1):
        op = Op()
        op.eng, op.lane, op.seq, op.fn, op.isdma, op.need, op.ninc = eng, lane, len(lane.ops), fn, isdma, isdma, ninc
        op.phase = self.phase
        deps = {}
        own = self.lanes.get(eng)

        def dep(ln, sq, raw):
            if (not isdma) and (ln is own) and (not raw) and eng == "pe":
                return
            cur = deps.get(ln)
            if cur is None or sq > cur:
                deps[ln] = sq

        for ln, sq in self.gfence.items():
            dep(ln, sq, True)
        for k in reads:
            f = self.fences.get(k[0])
            if f:
                for ln, sq in f.items():
                    dep(ln, sq, True)
            w = self.last_w.get(k)
            if w:
                dep(w[0], w[1], True)
        for k in writes:
            f = self.fences.get(k[0])
            if f:
                for ln, sq in f.items():
                    dep(ln, sq, True)
            w = self.last_w.get(k)
            if w:
                dep(w[0], w[1], False)
            for ln, sq in self.readers.get(k, {}).items():
                dep(ln, sq, False)
        for ln in list(deps):
            if ln.step == 16:
                deps[ln] = len(ln.ops) - 1
        if lane in deps and deps[lane] >= op.seq:
            deps[lane] = op.seq - 1
        op.deps = deps
        lane.ops.append(op)
        self.eng_ops[eng].append(op)
        self.all_ops.append(op)
        for k in reads:
            self.readers.setdefault(k, {})[lane] = op.seq
        for k in writes:
            self.last_w[k] = (lane, op.seq)
            self.readers[k] = {}
        return op

    def op(self, eng, fn, r=(), w=()):
        return self._add(eng, self.lanes[eng], fn, r, w, False)

    def dma(self, eng, lane, fn, r=(), w=(), n=1):
        return self._add(eng, lane, fn, r, w, True, n)

    def barrier(self):
        for ln in list(self.lanes.values()) + self.dma_lanes:
            if ln.ops:
                self.gfence[ln] = len(ln.ops) - 1

    def fence(self, region):
        f = dict(self.fences.get(region, {}))

        def upd(ln, sq):
            if f.get(ln, -1) < sq:
                f[ln] = sq

        for k in [k for k in self.last_w if k[0] == region]:
            ln, sq = self.last_w.pop(k)
            upd(ln, sq)
        for k in [k for k in self.readers if k[0] == region]:
            for ln, sq in self.readers.pop(k).items():
                upd(ln, sq)
        self.fences[region] = f

    def finalize(self):
        know = {e: {} for e in self.ENGS}
        for op in self.all_ops:
            K = know[op.eng]
            waits = []
            for ln, sq in op.deps.items():
                if sq < 0 or K.get(ln, -1) >= sq:
                    continue
                waits.append((ln, sq))
                tgt = ln.ops[sq]
                tgt.need = True
                for l2, s2 in tgt.vc.items():
                    if K.get(l2, -1) < s2:
                        K[l2] = s2
                if K.get(ln, -1) < sq:
                    K[ln] = sq
            op.waits = waits
            vc = dict(K)
            if vc.get(op.lane, -1) < op.seq:
                vc[op.lane] = op.seq
            op.vc = vc
        for ln in list(self.lanes.values()) + self.dma_lanes:
            c = 0
            ln.cum = []
            for o in ln.ops:
                if ln.step == 16:
                    c += 16 * o.ninc
                elif o.need:
                    c += 1
                ln.cum.append(c)

    def emit(self, eng, h):
        if eng == "pe":
            h = _CountProxy(h)
        for op in self.eng_ops[eng]:
            for ln, sq in op.waits:
                h.wait_ge(ln.sem, ln.count_at(sq))
            if eng == "pe":
                n0 = h.n
            ins = op.fn(h)
            if eng == "pe":
                self.pe_log.append((op.phase, h.n - n0))
            if op.isdma:
                if not isinstance(ins, (list, tuple)):
                    ins = [ins]
                assert len(ins) == op.ninc
                for i in ins:
                    i.then_inc(op.lane.sem, 16)
            elif op.need:
                ins.then_inc(op.lane.sem, 1)
        if eng == "sp":
            for ln in self.store_lanes:
                if ln.ops:
                    h.wait_ge(ln.sem, ln.cum[-1])


def _pp_layout():
    items = [
        ("bmod", 192), ("lng", 64), ("lnb", 64),
        ("hy_in_b", 48), ("hy_sw", 144), ("hy_sb", 48), ("hy_fb", 16), ("hy_ob", 16),
        ("at_ob", 16), ("ff_in_b", 176), ("ff_cw", 528), ("ff_cb", 176), ("ff_ob", 32),
        ("freq", 2), ("pb1", 2), ("pb2", 4), ("tn256", 2), ("tn1024", 8), ("mask0", 1),
    ]
    off = {}
    o = 0
    for n, w in items:
        off[n] = (o, w)
        o += w
    return off, o


PP_OFF, NPP = _pp_layout()


def _cp(v):
    v = np.asarray(v, np.float32)
    sh = v.shape
    c = sh[-1] // 128
    v = v.reshape(sh[:-1] + (c, 128))
    v = np.moveaxis(v, -1, 0)
    return np.ascontiguousarray(v).reshape(128, -1)


def pack_pp(inp):
    pp = np.zeros((128, NPP), np.float32)

    def put(name, arr):
        o, w = PP_OFF[name]
        assert arr.shape == (128, w), (name, arr.shape, w)
        pp[:, o:o + w] = arr

    put("bmod", _cp(inp["b_mod"]))
    put("lng", _cp(inp["ln_g"]))
    put("lnb", _cp(inp["ln_b"]))
    put("hy_in_b", _cp(inp["hy_in_b"]))
    put("hy_sw", _cp(inp["hy_short_w"]))
    put("hy_sb", _cp(inp["hy_short_b"]))
    put("hy_fb", _cp(inp["hy_filt_bias"]))
    put("hy_ob", _cp(inp["hy_out_b"]))
    put("at_ob", _cp(inp["at_o_b"]))
    put("ff_in_b", _cp(inp["ff_in_b"]))
    put("ff_cw", _cp(inp["ff_conv_w"]))
    put("ff_cb", _cp(inp["ff_conv_b"]))
    put("ff_ob", _cp(inp["ff_out_b"]))
    z = np.zeros((128, 2), np.float32)
    z[:64] = np.asarray(inp["hy_freq"], np.float32).T
    put("freq", z)
    z = np.zeros((128, 2), np.float32)
    z[:64] = np.asarray(inp["hy_pos_b1"], np.float32).T
    put("pb1", z)
    z = np.zeros((128, 4), np.float32)
    z[:64] = np.asarray(inp["hy_pos_b2"], np.float32).reshape(4, 64).T
    put("pb2", z)
    for L in (256, 1024):
        t = np.linspace(0.0, 1.0, L, dtype=np.float32)
        put("tn%d" % L, np.ascontiguousarray(-t.reshape(L // 128, 128).T))
    m = np.ones((128, 1), np.float32)
    m[0, 0] = 0.0
    put("mask0", m)
    return pp


def _round_f32r(a):
    u = np.ascontiguousarray(a, np.float32).view(np.uint32).astype(np.uint64)
    u = ((u + 0x800) & 0xFFFFF000).astype(np.uint32)
    return u.view(np.float32)


def make_consts():
    c = {}
    c["ident"] = np.eye(128).astype(ml_dtypes.bfloat16)
    for L in (256, 1024):
        N = 2 * L
        t = np.arange(L, dtype=np.float64)
        k = np.arange(L, dtype=np.float64)
        th = 2.0 * np.pi * np.outer(t, k + 0.5) / N
        C = np.cos(th)
        S = -np.sin(th)
        nj = L // 128
        Fm = np.zeros((nj, 128, nj, 256), np.float64)
        Gm = np.zeros((nj, 128, 2, L), np.float64)
        for j in range(nj):
            cr = C[:, j * 128:(j + 1) * 128].reshape(nj, 128, 128)
            ci = S[:, j * 128:(j + 1) * 128].reshape(nj, 128, 128)
            Fm[j, :, :, 0:128] = cr.transpose(1, 0, 2)
            Fm[j, :, :, 128:256] = ci.transpose(1, 0, 2)
            Gm[j, :, 0, :] = (2.0 / N) * C[:, j * 128:(j + 1) * 128].T
            Gm[j, :, 1, :] = (2.0 / N) * S[:, j * 128:(j + 1) * 128].T
        Fh = Fm.astype(np.float32).astype(ml_dtypes.bfloat16)
        Fl = (Fm - Fh.astype(np.float64)).astype(np.float32).astype(ml_dtypes.bfloat16)
        c["F%d" % L] = np.ascontiguousarray(np.stack([Fh, Fl], 2))
        Gh = Gm.astype(np.float32).astype(ml_dtypes.bfloat16)
        Gl = (Gm - Gh.astype(np.float64)).astype(np.float32).astype(ml_dtypes.bfloat16)
        c["G%d" % L] = np.ascontiguousarray(np.stack([Gh, Gl], 2))
        tl = np.linspace(0.0, 1.0, L, dtype=np.float32)[:, None]
        w = (2.0 * np.pi * np.arange(L, dtype=np.float32) / L).astype(np.float32)
        f = np.linspace(1e-4, 15, 16, dtype=np.float32)
        ang = (w[:, None] * f[None, :]).astype(np.float32)
        z = np.concatenate([tl, np.cos(ang), -np.sin(ang)], -1).astype(np.float32)
        c["zT%d" % L] = np.ascontiguousarray(z.T)
    max_decay = np.log(1e-2) / 0.3
    min_decay = np.log(1e-2) / 1.5
    c["delta"] = np.abs(np.linspace(min_decay, max_decay, D, dtype=np.float32)).astype(np.float32)
    rows = np.repeat(np.arange(16), 64).astype(np.float32)
    cols = np.tile(np.arange(64), 16).astype(np.float32)
    inv = (10000.0 ** (-np.arange(0, 64, 2, dtype=np.float32) / 64)).astype(np.float32)
    ang = np.concatenate([rows[:, None] * inv, cols[:, None] * inv], -1).astype(np.float32)
    cs = np.stack([np.cos(ang), np.sin(ang)], 1).astype(np.float32)
    c["rope"] = np.ascontiguousarray(cs.reshape(8, 128, 2, 64).transpose(1, 0, 2, 3))
    return c


def build_nc(dbg=None, ngroups=2, depth=DEPTH):
    nc = bass.Bass("TRN2", target_bir_lowering=False)
    P = Prog()
    es = contextlib.ExitStack()

    def din(name, shape, dt=F32):
        return nc.dram_tensor(name, list(shape), dt, kind="ExternalInput")

    xin = din("xin", [2, 8, 128, 1024]).ap()
    cond = din("cond", [128, 8, 2]).ap()
    cache_k = din("cache_k", [2, 512, 256]).ap()
    cache_v = din("cache_v", [2, 512, 256]).ap()
    ppd = din("pp", [128, NPP]).ap()
    w_mod = din("w_mod", [4, 1024, 6144]).ap()
    hy_in_w = din("hy_in_w", [2, 1024, 3072]).ap()
    hy_out_w = din("hy_out_w", [2, 1024, 1024]).ap()
    at_qkv_w = din("at_qkv_w", [2, 1024, 1536]).ap()
    at_o_w = din("at_o_w", [2, 1024, 1024]).ap()
    ff_in_w = din("ff_in_w", [4, 1024, 5632]).ap()
    ff_out_w = din("ff_out_w", [4, 2816, 1024]).ap()
    pos_w1 = din("pos_w1", [2, 33, 64]).ap()
    pos_w2 = din("pos_w2", [4, 64, 64]).ap()
    pos_wout = din("pos_wout", [2, 64, 2048]).ap()
    qkvb_t = din("qkvb", [2, 1536])
    qgain_t = din("qgain", [2, 128])
    kgain_t = din("kgain", [2, 128])
    ident_d = din("ident", [128, 128], BF16).ap()
    zT_d = {L: din("zT%d" % L, [33, L]).ap() for L in (256, 1024)}
    F_d = {L: din("F%d" % L, [L // 128, 128, 2, L // 128, 256], BF16).ap() for L in (256, 1024)}
    G_d = {L: din("G%d" % L, [L // 128, 128, 2, 2, L], BF16).ap() for L in (256, 1024)}
    delta_t = din("delta", [1024])
    rope_d = din("rope", [128, 8, 2, 64]).ap()

    yT = nc.dram_tensor("yT", [2, 8, 128, 1024], F32, kind="ExternalOutput").ap()
    nk_o = nc.dram_tensor("nk", [4, 2, 256, 256], F32, kind="ExternalOutput").ap()
    nv_o = nc.dram_tensor("nv", [4, 2, 256, 256], F32, kind="ExternalOutput").ap()

    def sb(name, shape, dt=F32):
        return es.enter_context(nc.sbuf_tensor(name, list(shape), dt))

    def semaphore(name):
        return es.enter_context(nc.semaphore(name))

    xT = sb("xT", [128, 8, 1024])
    hT = sb("hT", [128, 8, 1024], BF16)
    big = sb("big", [128, 22528], BF16)
    cvb = sb("cvb", [128, 4, 1024])
    dsc = sb("dsc", [128, 3, 44])
    slots = sb("slots", [128, NSLOT, SLOT_EL], BF16)
    pp = sb("ppsb", [128, NPP])
    mod = sb("mod", [128, 4 * 48 * 2])
    mod1 = sb("mod1", [128, 4 * 48 * 2])
    gbt = sb("gbt", [128, 4 * 2 * 8 * 2])
    AAt = sb("AAt", [128, 4 * 2 * 8 * 2])
    BBt = sb("BBt", [128, 4 * 2 * 8 * 2])
    condT = sb("condT", [128, 16])
    condb = sb("condb", [128, 16], BF16)
    ident = sb("identsb", [128, 128], BF16)
    onesb = sb("onesb", [128, 128], BF16)
    onesf = sb("onesf", [128, 128])
    etmp = sb("etmp", [128, 4, 512])
    NMW = 4
    modw = sb("modw", [128, NMW, 1024], BF16)
    lnst = sb("lnst", [128, 2, 512])
    xrt = sb("xrt", [128, 2, 512])
    sqt = sb("sqt", [128, 2, 512])
    fb1 = sb("fb1", [64, 8])

    SCRW = 8880
    scr = sb("scr", [128, SCRW])

    def carve():
        off = [0]

        def alloc(shape, dt=F32):
            n = int(np.prod(shape[1:]))
            nw = n if dt == F32 else (n + 1) // 2
            assert off[0] + nw <= SCRW, (off[0], nw)
            v = scr[0:shape[0], off[0]:off[0] + nw]
            off[0] += nw
            if dt == BF16:
                v = v.bitcast(BF16)
            if len(shape) == 3:
                v = v.rearrange("p (a b) -> p a b", a=shape[1])
            elif len(shape) == 4:
                v = v.rearrange("p (a b c) -> p a b c", a=shape[1], b=shape[2])
            return v
        alloc.off = off
        return alloc

    pst = es.enter_context(nc.psum_tensor("pst", [128, 4096], F32))
    ps = [pst[:, i * 512:(i + 1) * 512] for i in range(8)]

    for ln in P.lanes.values():
        ln.sem = semaphore("c_" + ln.name)
    slot_lanes = [P.dma_lane("slot%d" % i) for i in range(NSLOT)]
    misc_lane = P.dma_lane("misc")
    x_lane = P.dma_lane("xload")
    modw_lanes = [P.dma_lane("modw%d" % i) for i in range(4)]
    fg_lanes = [P.dma_lane("fg%d" % i) for i in range(4)]
    fin_lane = P.dma_lane("fin")
    zt_lane = P.dma_lane("zt")
    wo_lanes = [P.dma_lane("wo%d" % i) for i in range(2)]
    ain_lane = P.dma_lane("ain")
    kc_lane = P.dma_lane("kc")
    vc_lane = P.dma_lane("vc")
    y_lane = P.dma_lane("ystore", store=True)
    kv_lanes = [P.dma_lane("kvst%d" % i, store=True) for i in range(5)]
    dbg_lane = P.dma_lane("dbg", store=True)

    def PPv(name, *idx):
        o, w = PP_OFF[name]
        return o, w

    def ppcol(name, i):
        o, w = PP_OFF[name]
        assert 0 <= i < w
        return pp[:, o + i:o + i + 1]

    st = dict(slot=0, bank=0, zb=0, cv=0, nbank=8, pair=0)

    def load_slab(fn_list_builder, n, eng="pool"):
        i = st["slot"] % NSLOT
        st["slot"] += 1
        key = ("slot", i)
        view = slots[:, i, :]

        def fn(h, view=view):
            return fn_list_builder(h, view)
        P.dma(eng, slot_lanes[i], fn, r=(), w=(key,), n=n)
        return view, key

    def bank():
        b = st["bank"] % st["nbank"]
        st["bank"] += 1
        return b

    def bankpair():
        b = 2 * (st["pair"] % 4)
        st["pair"] += 1
        return b

    def zalloc():
        i = st["zb"] % 4
        st["zb"] += 1
        return i

    def calloc():
        i = st["cv"] % 4
        st["cv"] += 1
        return i

    def psk(b):
        return ("ps", b)

    def pro_loads(h):
        return [
            h.dma_start(out=pp[:], in_=ppd[:, :]),
            h.dma_start(out=condT[:], in_=cond.rearrange("p k c -> p (k c)")),
            h.dma_start(out=ident[:], in_=ident_d[:, :]),
        ]
    P.dma("sp", misc_lane, pro_loads, w=(("c", "pp"), ("c", "cond"), ("c", "ident")), n=3)
    P.op("dve", lambda h: h.memset(onesb[:], 1.0), w=(("c", "onesb"),))
    P.op("dve", lambda h: h.memset(etmp[:, 0, 0:128], 1.0 / 1024.0), w=(("et", 0),))
    P.op("act", lambda h: h.activation(out=onesf[:].bitcast(F32R), in_=etmp[:, 0, 0:128], func=AF.Identity), r=(("et", 0),), w=(("c", "onesf"),))
    P.op("act", lambda h: h.activation(out=condb[:], in_=condT[:], func=AF.Silu), r=(("c", "cond"),), w=(("c", "condb"),))
    o_f = PP_OFF["freq"][0]
    o_b1 = PP_OFF["pb1"][0]
    o_b2 = PP_OFF["pb2"][0]
    for j in range(2):
        P.op("dve", lambda h, j=j: h.tensor_tensor(out=fb1[:, j:j + 1], in0=pp[0:64, o_f + j:o_f + j + 1],
                                                  in1=pp[0:64, o_b1 + j:o_b1 + j + 1], op=ALU.mult),
             r=(("c", "pp"),), w=(("c", "fb1", j),))
        for i in range(2):
            P.op("dve", lambda h, j=j, i=i: h.tensor_tensor(out=fb1[:, 2 + 2 * j + i:3 + 2 * j + i], in0=pp[0:64, o_f + j:o_f + j + 1],
                                                          in1=pp[0:64, o_b2 + 2 * j + i:o_b2 + 2 * j + i + 1], op=ALU.mult),
                 r=(("c", "pp"),), w=(("c", "fb1", 2 + 2 * j + i),))

    o_bm = PP_OFF["bmod"][0]

    class ModStream:
        def __init__(self):
            self.tasks = []
            self.loaded = 0
            self.done = 0

        def add_layer(self, l):
            self.tasks += [(l, f) for f in range(48)]

        def _load(self):
            l, f = self.tasks[self.loaded]
            i = self.loaded % 4
            self.loaded += 1
            P.dma("pool", modw_lanes[i], lambda h, l=l, f=f, i=i: [h.dma_start(
                out=modw[:, i, :].rearrange("p (k n) -> p k n", k=8),
                in_=w_mod[l].rearrange("(k p) n -> p k n", p=128)[:, :, f * 128:(f + 1) * 128])], w=(("modw", i),), n=1)

        def pending(self, l):
            return any(t[0] == l for t in self.tasks[self.done:])

        def step(self, b):
            if self.done >= len(self.tasks):
                return
            while self.loaded < len(self.tasks) and self.loaded < self.done + 3:
                self._load()
            l, f = self.tasks[self.done]
            i = self.done % 4
            self.done += 1
            if self.loaded < len(self.tasks) and self.loaded < self.done + 3:
                self._load()

            def mm(h, i=i, b=b):
                ins = None
                wv = modw[:, i, :].rearrange("p (k n) -> p k n", k=8)
                for k in range(8):
                    ins = h.matmul(ps[b][:, 0:2], lhsT=wv[:, k, :], rhs=condb[:, 2 * k:2 * k + 2], start=(k == 0), stop=(k == 7))
                return ins
            P.op("pe", mm, r=(("modw", i), ("c", "condb")), w=(psk(b),))
            f0 = l * 48 + f
            P.op("act", lambda h, b=b, f0=f0: h.activation(out=mod[:, 2 * f0:2 * f0 + 2], in_=ps[b][:, 0:2], func=AF.Identity,
                                                         bias=pp[:, o_bm + f0:o_bm + f0 + 1]), r=(psk(b), ("c", "pp")), w=(("c", "mod", l),))

        def drain(self, l, b):
            while self.pending(l):
                self.step(b)

    MS = ModStream()

    def modcol(t, l, i, m, c):
        o = ((l * 48) + i * 8 + m) * 2 + c
        return t[:, o:o + 1]

    def modrow(t, l, i, c):
        o = l * 48 + i * 8
        return t[:, :].rearrange("p (x c) -> p x c", c=2)[:, o:o + 8, c]

    def t4(t, l, sub, c):
        o = (l * 2 + sub) * 8
        return t[:, :].rearrange("p (x c) -> p x c", c=2)[:, o:o + 8, c]

    def t4col(t, l, sub, m, c):
        o = ((l * 2 + sub) * 8 + m) * 2 + c
        return t[:, o:o + 1]

    def projbias(l, sub):
        if sub == 1:
            o = PP_OFF["ff_ob"][0] + l * 8
        elif l % 2 == 0:
            o = PP_OFF["hy_ob"][0] + (l // 2) * 8
        else:
            o = PP_OFF["at_ob"][0] + (l // 2) * 8
        return pp[:, o:o + 8]

    o_lng = PP_OFF["lng"][0]
    o_lnb = PP_OFF["lnb"][0]

    def aabb(l, sub):
        nl, nsub = (l, 1) if sub == 0 else (l + 1, 0)
        if nl >= depth:
            return
        lg = pp[:, o_lng + (l * 2 + sub) * 8:o_lng + (l * 2 + sub) * 8 + 8]
        lb = pp[:, o_lnb + (l * 2 + sub) * 8:o_lnb + (l * 2 + sub) * 8 + 8]
        for c in range(2):
            P.op("dve", lambda h, c=c: h.tensor_tensor(out=t4(AAt, l, sub, c), in0=lg, in1=modrow(mod1, nl, 3 * nsub + 1, c), op=ALU.mult),
                 r=(("c", "mod1", nl), ("c", "pp")), w=(("c", "AA", l, sub, c),))
            P.op("dve", lambda h, c=c: h.tensor_tensor(out=t4(BBt, l, sub, c), in0=lb, in1=modrow(mod1, nl, 3 * nsub + 1, c), op=ALU.mult),
                 r=(("c", "mod1", nl), ("c", "pp")), w=(("c", "BB", l, sub, c),))
            P.op("dve", lambda h, c=c: h.tensor_tensor(out=t4(BBt, l, sub, c), in0=t4(BBt, l, sub, c), in1=modrow(mod, nl, 3 * nsub + 0, c), op=ALU.add),
                 r=(("c", "BB", l, sub, c), ("c", "mod", nl)), w=(("c", "BB", l, sub, c),))

    def mod_finish(l):
        P.op("dve", lambda h: h.tensor_scalar(out=mod1[:, l * 96:(l + 1) * 96], in0=mod[:, l * 96:(l + 1) * 96],
                                              scalar1=1.0, scalar2=None, op0=ALU.add),
             r=(("c", "mod", l),), w=(("c", "mod1", l),))
        for sub in range(2):
            for c in range(2):
                P.op("dve", lambda h, sub=sub, c=c: h.tensor_tensor(out=t4(gbt, l, sub, c), in0=modrow(mod1, l, 3 * sub + 2, c),
                                                                    in1=projbias(l, sub), op=ALU.mult),
                     r=(("c", "mod1", l), ("c", "pp")), w=(("c", "gbt", l, sub, c),))
        aabb(l, 0)
        if l >= 1:
            aabb(l - 1, 1)

    P.phase = "prologue"
    MS.add_layer(0)
    for _ in range(16):
        MS.step(bank())
    P.op("dve", lambda h: h.tensor_scalar(out=mod1[:, 0:32], in0=mod[:, 0:32], scalar1=1.0, scalar2=None, op0=ALU.add),
         r=(("c", "mod", 0),), w=(("c", "mod1", 0),))

    def MODK(l):
        return (("c", "mod", l), ("c", "mod1", l))


    def xk(m, t):
        return ("xT", m, t)

    def hk(m, t):
        return ("hT", m, t)

    def tsl(t):
        return slice(t * 512, (t + 1) * 512)

    def tokv(ap2d, g):
        return ap2d.rearrange("p (s t) -> p s t", s=GROUPS[g]["nseq"])

    def psv(b, g):
        if g == 0:
            return ps[b][:, :].rearrange("p (s t) -> p s t", s=2)
        return ps[b][:, :].rearrange("p (s t) -> p s t", s=1)

    def conv_scalars(n, o_w, stride, o_b, o_cb):
        w0, w1, w2 = (pp[:, o_w + i * stride:o_w + i * stride + n] for i in range(3))
        bi, cb = pp[:, o_b:o_b + n], pp[:, o_cb:o_cb + n]
        K = (("dsc",),)
        R_ = (("c", "pp"),)
        P.op("dve", lambda h: h.tensor_tensor(out=dsc[:, 0, 0:n], in0=w0, in1=w1, op=ALU.add), r=R_, w=K)
        P.op("dve", lambda h: h.tensor_tensor(out=dsc[:, 0, 0:n], in0=dsc[:, 0, 0:n], in1=w2, op=ALU.add), r=R_ + K, w=K)
        P.op("dve", lambda h: h.tensor_tensor(out=dsc[:, 0, 0:n], in0=dsc[:, 0, 0:n], in1=bi, op=ALU.mult), r=R_ + K, w=K)
        P.op("dve", lambda h: h.tensor_tensor(out=dsc[:, 0, 0:n], in0=dsc[:, 0, 0:n], in1=cb, op=ALU.add), r=R_ + K, w=K)
        P.op("dve", lambda h: h.scalar_tensor_tensor(out=dsc[:, 1, 0:n], in0=w0, scalar=-1.0, in1=bi, op0=ALU.mult, op1=ALU.mult), r=R_ + K, w=K)
        P.op("dve", lambda h: h.scalar_tensor_tensor(out=dsc[:, 2, 0:n], in0=w2, scalar=-1.0, in1=bi, op0=ALU.mult, op1=ALU.mult), r=R_ + K, w=K)

    def upconv(g, lhs_fn, wkey, w0, w1, w2, ci, out2d, okey, defer=None, after=None):
        cbt, nwb0, nwb2 = dsc[:, 0, ci:ci + 1], dsc[:, 1, ci:ci + 1], dsc[:, 2, ci:ci + 1]
        DK = (("dsc",), ("c", "pp"))
        b0 = bankpair()
        PK = (psk(b0), psk(b0 + 1))

        def mm_t(t):
            b = b0 + t

            def mm(h):
                ins = None
                for k in range(8):
                    ins = h.matmul(ps[b][:, :], lhsT=lhs_fn(k), rhs=hT[:, k, tsl(t)], start=(k == 0), stop=(k == 7))
                return ins
            P.op("pe", mm, r=(wkey,) + tuple(hk(k, t) for k in range(8)), w=(psk(b),))

        def post():
            pv2 = pst[:, b0 * 512:(b0 + 2) * 512]
            if g == 0:
                pv = pv2.rearrange("p (s t) -> p s t", s=4)
                ov = tokv(out2d, 0)
                o_hi, p_lo, o_lo, p_hi = ov[:, :, 1:256], pv[:, :, 0:255], ov[:, :, 0:255], pv[:, :, 1:256]
                e0, e1 = ov[:, :, 0], ov[:, :, 255]
            else:
                pv = pv2
                ov = out2d
                o_hi, p_lo, o_lo, p_hi = ov[:, 1:1024], pv[:, 0:1023], ov[:, 0:1023], pv[:, 1:1024]
                e0, e1 = ov[:, 0:1], ov[:, 1023:1024]
            P.op("act", lambda h: h.activation(out=ov, in_=pv, func=AF.Identity, scale=w1, bias=cbt), r=PK + DK, w=(okey,))
            P.op("dve", lambda h: h.scalar_tensor_tensor(out=o_hi, in0=p_lo, scalar=w0, in1=o_hi, op0=ALU.mult, op1=ALU.add), r=PK + (okey,) + DK, w=(okey,))
            P.op("dve", lambda h: h.scalar_tensor_tensor(out=o_lo, in0=p_hi, scalar=w2, in1=o_lo, op0=ALU.mult, op1=ALU.add), r=PK + (okey,) + DK, w=(okey,))
            P.op("dve", lambda h: h.tensor_scalar(out=e0, in0=e0, scalar1=nwb0, scalar2=None, op0=ALU.add), r=(okey,) + DK, w=(okey,))
            P.op("dve", lambda h: h.tensor_scalar(out=e1, in0=e1, scalar1=nwb2, scalar2=None, op0=ALU.add), r=(okey,) + DK, w=(okey,))
            if after is not None:
                after()

        if defer is not None:
            mm_t(0)
            defer.append((lambda: mm_t(1), post))
        else:
            mm_t(0)
            mm_t(1)
            post()

    def flush_deferred(defer):
        for m1, _ in defer:
            m1()
        for _, po in defer:
            po()
        del defer[:]

    def dbg_dump(tag):
        if dbg is not None and dbg == tag:
            def f(h):
                return [h.dma_start(out=yT[0, m], in_=xT[:, m, :]) for m in range(8)] + \
                       [h.dma_start(out=yT[1, m].bitcast(BF16)[:, 0:1024], in_=hT[:, m, :]) for m in range(8)]
            P.dma("sp", dbg_lane, f, r=tuple(xk(m, t) for m in range(8) for t in range(2)) + tuple(hk(m, t) for m in range(8) for t in range(2)), n=16)
            return True
        return False

    def proj_ln(g, l, sub, nk, src_fn, src_keys_fn, wload_fn, tail_hook=None):
        c = g
        sum_b = [4, 5]
        sq_b = [6, 7]
        st["nbank"] = 4
        pend = []

        def flush(n):
            while len(pend) > n:
                m, t, si = pend.pop(0)
                P.op("pe", lambda h, m=m, t=t, si=si: h.matmul(ps[sum_b[t]][:, :], lhsT=onesf[:, :].bitcast(F32R), rhs=xrt[:, si, :].bitcast(F32R),
                                                             start=(m == 0), stop=(m == 7)),
                     r=(("xr", si), ("c", "onesf")), w=(psk(sum_b[t]),))
                P.op("pe", lambda h, m=m, t=t, si=si: h.matmul(ps[sq_b[t]][:, :], lhsT=onesf[:, :].bitcast(F32R), rhs=sqt[:, si, :].bitcast(F32R),
                                                             start=(m == 0), stop=(m == 7)),
                     r=(("sq", si), ("c", "onesf")), w=(psk(sq_b[t]),))

        has_next = not (sub == 1 and l == depth - 1)
        o_g = o_lng + (l * 2 + sub) * 8
        o_b = o_lnb + (l * 2 + sub) * 8
        cnt = dict(it=0)

        def ln_stats(t):
            P.op("act", lambda h: h.activation(out=lnst[:, 0, :], in_=ps[sum_b[t]][:, :], func=AF.Square),
                 r=(psk(sum_b[t]),), w=(("ln", 0),))
            P.op("dve", lambda h: h.scalar_tensor_tensor(out=lnst[:, 0, :], in0=ps[sq_b[t]][:, :], scalar=LN_EPS, in1=lnst[:, 0, :],
                                                         op0=ALU.add, op1=ALU.subtract),
                 r=(psk(sq_b[t]), ("ln", 0)), w=(("ln", 0),))
            P.op("act", lambda h: h.activation(out=lnst[:, 0, :], in_=lnst[:, 0, :], func=AF.Sqrt),
                 r=(("ln", 0),), w=(("ln", 0),))
            P.op("dve", lambda h: h.reciprocal(out=lnst[:, 0, :], in_=lnst[:, 0, :]), r=(("ln", 0),), w=(("ln", 0),))
            P.op("dve", lambda h: h.tensor_tensor(out=lnst[:, 1, :], in0=ps[sum_b[t]][:, :], in1=lnst[:, 0, :], op=ALU.mult),
                 r=(psk(sum_b[t]), ("ln", 0)), w=(("ln", 1),))

        def ln_apply(m, t, eng="dve"):
            ei = (cnt["it"] % 4) if t == 1 else (2 + cnt["it"] % 2)
            cnt["it"] += 1
            P.op(eng, lambda h: h.tensor_tensor(out=etmp[:, ei, :], in0=xT[:, m, tsl(t)], in1=lnst[:, 0, :], op=ALU.mult),
                 r=(xk(m, t), ("ln", 0)), w=(("et", ei),))
            P.op(eng, lambda h: h.tensor_tensor(out=etmp[:, ei, :], in0=etmp[:, ei, :], in1=lnst[:, 1, :], op=ALU.subtract),
                 r=(("et", ei), ("ln", 1)), w=(("et", ei),))
            if has_next:
                P.op("act", lambda h: h.activation(out=hT[:, m, tsl(t)], in_=etmp[:, ei, :], func=AF.Identity,
                                                   scale=t4col(AAt, l, sub, m, c), bias=t4col(BBt, l, sub, m, c)),
                     r=(("et", ei), ("c", "AA", l, sub, c), ("c", "BB", l, sub, c)), w=(hk(m, t),))
            P.op("act", lambda h: h.activation(out=xT[:, m, tsl(t)], in_=etmp[:, ei, :], func=AF.Identity,
                                               scale=pp[:, o_g + m:o_g + m + 1], bias=pp[:, o_b + m:o_b + m + 1]),
                 r=(("et", ei), ("c", "pp")), w=(xk(m, t),))

        it = 0
        for t in range(2):
            for m in range(8):
                lhs_fn, wkey = wload_fn(m)
                b = bank()

                def mm(h, b=b, t=t, lhs_fn=lhs_fn):
                    ins = None
                    for k in range(nk):
                        ins = h.matmul(ps[b][:, :], lhsT=lhs_fn(k), rhs=src_fn(k, t), start=(k == 0), stop=(k == nk - 1))
                    return ins
                P.op("pe", mm, r=(wkey,) + tuple(src_keys_fn(k, t) for k in range(nk)), w=(psk(b),))
                flush(1)
                ei = it % 2
                si = it % 2
                it += 1
                P.op("act", lambda h, b=b, m=m, ei=ei: h.activation(out=etmp[:, ei, :], in_=ps[b][:, :], func=AF.Identity,
                                                                    scale=modcol(mod1, l, 3 * sub + 2, m, c), bias=t4col(gbt, l, sub, m, c)),
                     r=(psk(b), ("c", "gbt", l, sub, c)) + MODK(l), w=(("et", ei),))
                P.op("dve", lambda h, m=m, t=t, ei=ei: h.scalar_tensor_tensor(out=xT[:, m, tsl(t)], in0=xT[:, m, tsl(t)], scalar=ALPHA,
                                                                            in1=etmp[:, ei, :], op0=ALU.mult, op1=ALU.add),
                     r=(xk(m, t), ("et", ei)), w=(xk(m, t),))
                P.op("act", lambda h, m=m, t=t, si=si: h.activation(out=sqt[:, si, :].bitcast(F32R), in_=xT[:, m, tsl(t)], func=AF.Square),
                     r=(xk(m, t),), w=(("sq", si),))
                P.op("act", lambda h, m=m, t=t, si=si: h.activation(out=xrt[:, si, :].bitcast(F32R), in_=xT[:, m, tsl(t)], func=AF.Identity),
                     r=(xk(m, t),), w=(("xr", si),))
                pend.append((m, t, si))
                if t == 1:
                    ln_apply(m, 0)
            flush(0)
            if t == 0:
                ln_stats(0)
        st["nbank"] = 8
        if tail_hook is not None:
            tail_hook()
        ln_stats(1)
        for m in range(8):
            ln_apply(m, 1)

    def mixer_tail(g, l):
        if not (g == 0 and l + 1 < depth):
            return None

        return None

    def wload_sq(wmat, nk_):
        def wl(m):
            def ld(h, view):
                return [h.dma_start(out=view[:, 0:nk_ * 128].rearrange("p (k n) -> p k n", k=nk_),
                                    in_=wmat.rearrange("(k p) n -> p k n", p=128)[:, :, m * 128:(m + 1) * 128])]
            view, key = load_slab(ld, 1)
            wv = view[:, 0:nk_ * 128].rearrange("p (k n) -> p k n", k=nk_)
            return (lambda k: wv[:, k, :]), key
        return wl

    def ffn(g, l, hook=None, hook_end=None):
        P.phase = "g%d l%d ffn-up" % (g, l)
        P.fence("big")
        act = big[:, 0:22 * 1024].rearrange("p (c t) -> p c t", c=22)
        o_ib = PP_OFF["ff_in_b"][0] + l * 44
        o_cw = PP_OFF["ff_cw"][0] + l * 132
        o_cb = PP_OFF["ff_cb"][0] + l * 44
        conv_scalars(44, o_cw, 44, o_ib, o_cb)
        def ffn_load(cp):
            def ld(h, view, cp=cp):
                dv = view[:, 0:4096].rearrange("p (k a n) -> p k a n", k=8, a=2)
                sv = ff_in_w[l].rearrange("(k p) (a n) -> p k a n", p=128, a=2)
                return [h.dma_start(out=dv[:, :, a, :], in_=sv[:, :, a, cp * 256:(cp + 1) * 256]) for a in range(2)]
            return load_slab(ld, 2)
        PF = 2
        loaded = [ffn_load(i) for i in range(PF)]
        for cp in range(11):
            if hook is not None:
                hook(cp)
            if cp + PF < 11:
                loaded.append(ffn_load(cp + PF))
            view, wkey = loaded[cp]
            wv = view[:, 0:4096].rearrange("p (k a n) -> p k a n", k=8, a=2)
            dq = [] if cp == 0 else None
            for cc in range(2):
                ch = cp * 2 + cc
                cvi = [calloc(), calloc()]

                def tail(ch=ch, cvi=cvi):
                    P.op("act", lambda h: h.activation(out=cvb[:, cvi[0], :], in_=cvb[:, cvi[0], :], func=AF.Gelu_apprx_tanh),
                         r=(("cv", cvi[0]),), w=(("cv", cvi[0]),))
                    P.op(MULT_ENG, lambda h: h.tensor_tensor(out=act[:, ch, :], in0=cvb[:, cvi[0], :], in1=cvb[:, cvi[1], :], op=ALU.mult),
                         r=(("cv", cvi[0]), ("cv", cvi[1])), w=(("big", "act", ch),))
                for a in range(2):
                    ci = a * 22 + ch
                    cv = cvi[a]
                    upconv(g, (lambda k, a=a, cc=cc, wv=wv: wv[:, k, a, cc * 128:(cc + 1) * 128]), wkey,
                           pp[:, o_cw + ci:o_cw + ci + 1], pp[:, o_cw + 44 + ci:o_cw + 44 + ci + 1], pp[:, o_cw + 88 + ci:o_cw + 88 + ci + 1],
                           ci, cvb[:, cv, :], ("cv", cv), defer=dq, after=(tail if a == 1 else None))
            if dq:
                flush_deferred(dq)
        if hook_end is not None:
            hook_end()
        P.phase = "g%d l%d ffn-down+ln" % (g, l)
        proj_ln(g, l, 1, 22, lambda k, t: act[:, k, tsl(t)], lambda k, t: ("big", "act", k), wload_sq(ff_out_w[l], 22))

    def hyena(g, l):
        j = l // 2
        L = GROUPS[g]["L"]
        nseq = GROUPS[g]["nseq"]
        nj = L // 128
        if g == 0 and l + 1 < depth:
            MS.add_layer(l + 1)
        P.phase = "g%d l%d hy-inproj" % (g, l)
        P.fence("big")
        x0 = big[:, 0:8192].rearrange("p (c t) -> p c t", c=8)
        uu = big[:, 8192:16384].rearrange("p (c t) -> p c t", c=8)
        uT = big[:, 16384:18432].rearrange("p (k n) -> p k n", k=8)
        ksd_hi = big[:, 18432:18432 + 2 * nj * 256].rearrange("p (a k n) -> p a k n", a=2, k=nj)
        cvflat = cvb[:, :, :].rearrange("p a b -> p (a b)").bitcast(BF16)
        ks_lo = cvflat[:, 0:nj * 256].rearrange("p (k n) -> p k n", k=nj)
        kd_lo = cvflat[:, 2048:2048 + nj * 256].rearrange("p (k n) -> p k n", k=nj)
        ksd_lo = [ks_lo, kd_lo]
        CVK = tuple(("cv", i) for i in range(4))
        KSK = [CVK, CVK]
        o_ib = PP_OFF["hy_in_b"][0] + j * 24
        o_sw = PP_OFF["hy_sw"][0] + j * 72
        o_sb = PP_OFF["hy_sb"][0] + j * 24
        o_fb = PP_OFF["hy_fb"][0] + j * 8
        o_tn = PP_OFF["tn%d" % L][0]
        o_m0 = PP_OFF["mask0"][0]
        if True:
            P.fence("scr")
            lsb = carve()
            w1s = lsb([33, 64])
            w2s = lsb([64, 2, 64])
            wos = lsb([64, 2, 2, 256])
            fh = lsb([64, 2, 1024])
            ctmp = lsb([128, 4, 256])
            ftmp = ctmp[0:64, :, :].rearrange("p a b -> p (a b)").rearrange("p (a b) -> p a b", a=2)
            deltab = lsb([128, 2, 256])
            dect = lsb([128, 2, 256])
            kft = lsb([128, 2, 512])
            ksb = lsb([128, 2, 512])
            zTs = ksb[0:33, :, :].rearrange("p a b -> p (a b)")
            ZTK = (("scr", "K", 0), ("scr", "K", 1))
            ybuf = lsb([128, 2, 512], BF16)

            def fl(h):
                return [h.dma_start(out=w1s[:], in_=pos_w1[j]),
                        h.dma_start(out=w2s[:], in_=pos_w2[2 * j:2 * j + 2].rearrange("i k n -> k i n")),
]
            P.dma("sp", fin_lane, fl, w=(("scr", "in"),), n=2)
            P.dma("sp", zt_lane, lambda h: [h.dma_start(out=zTs[:, 0:L], in_=zT_d[L][:, :])], w=ZTK, n=1)

            conv_scalars(24, o_sw, 24, o_ib, o_sb)
            def hy_load(c):
                def ld(h, view, c=c):
                    dv = view[:, 0:3072].rearrange("p (k a n) -> p k a n", k=8, a=3)
                    sv = hy_in_w[j].rearrange("(k p) (a n) -> p k a n", p=128, a=3)
                    return [h.dma_start(out=dv[:, :, a, :], in_=sv[:, :, a, c * 128:(c + 1) * 128]) for a in range(3)]
                return load_slab(ld, 3)
            hloaded = [hy_load(0), hy_load(1)]
            for c in range(8):
                if c + 2 < 8:
                    hloaded.append(hy_load(c + 2))
                view, wkey = hloaded[c]
                wv = view[:, 0:3072].rearrange("p (k a n) -> p k a n", k=8, a=3)

                cvi = [calloc(), calloc(), calloc()]
                dq = [] if c == 0 else None

                def tail(c=c, cvi=cvi):
                    P.op("act", lambda h: h.activation(out=x0[:, c, :], in_=cvb[:, cvi[0], :], func=AF.Identity),
                         r=(("cv", cvi[0]),), w=(("big", "x0", c),))
                    P.op(MULT_ENG, lambda h: h.tensor_tensor(out=uu[:, c, :], in0=cvb[:, cvi[1], :], in1=cvb[:, cvi[2], :], op=ALU.mult),
                         r=(("cv", cvi[1]), ("cv", cvi[2])), w=(("big", "u", c),))
                for a in range(3):
                    ci = a * 8 + c
                    cv = cvi[a]
                    upconv(g, (lambda k, a=a, wv=wv: wv[:, k, a, :]), wkey,
                           pp[:, o_sw + ci:o_sw + ci + 1], pp[:, o_sw + 24 + ci:o_sw + 24 + ci + 1], pp[:, o_sw + 48 + ci:o_sw + 48 + ci + 1],
                           ci, cvb[:, cv, :], ("cv", cv), defer=dq, after=(tail if a == 2 else None))
                if dq:
                    flush_deferred(dq)

            P.phase = "g%d l%d hy-filter-mlp" % (g, l)
            FT0 = (("scr", "c", 0), ("scr", "c", 1))
            FT1 = (("scr", "c", 2), ("scr", "c", 3))

            def sin_layer(src_ps_fn, nt, dst, fcol, bcol, rk):
                for t in range(nt):
                    n = min(512, L)
                    sl = slice(t * 512, t * 512 + n)
                    b = src_ps_fn(t)
                    P.op("dve", lambda h, b=b, n=n: h.tensor_scalar(out=ftmp[:, 0, 0:n], in0=ps[b][0:64, 0:n], scalar1=pp[0:64, o_f + j:o_f + j + 1],
                                                                   scalar2=fb1[:, bcol:bcol + 1], op0=ALU.mult, op1=ALU.add),
                         r=(psk(b), ("c", "pp"), ("c", "fb1", bcol)), w=FT0)
                    P.op("dve", lambda h, n=n: h.tensor_scalar(out=ftmp[:, 1, 0:n], in0=ftmp[:, 0, 0:n], scalar1=float(1.0 / TWO_PI), scalar2=MAGIC,
                                                              op0=ALU.mult, op1=ALU.add), r=FT0, w=FT1)
                    P.op("dve", lambda h, n=n: h.tensor_scalar(out=ftmp[:, 1, 0:n], in0=ftmp[:, 1, 0:n], scalar1=MAGIC, scalar2=-TWO_PI,
                                                              op0=ALU.subtract, op1=ALU.mult), r=FT1, w=FT1)
                    P.op("dve", lambda h, n=n: h.tensor_tensor(out=ftmp[:, 0, 0:n], in0=ftmp[:, 0, 0:n], in1=ftmp[:, 1, 0:n], op=ALU.add),
                         r=FT0 + FT1, w=FT0)
                    P.op("act", lambda h, n=n, sl=sl: h.activation(out=dst[:, sl], in_=ftmp[:, 0, 0:n], func=AF.Sin),
                         r=FT0, w=(rk,))
            nt = max(1, L // 512)
            n512 = min(512, L)

            def l1(t):
                b = bank()
                P.op("pe", lambda h, b=b, t=t: h.matmul(ps[b][0:64, 0:n512], lhsT=w1s[:, :], rhs=zTs[:, t * 512:t * 512 + n512], start=True, stop=True),
                     r=(("scr", "in"),) + ZTK, w=(psk(b),))
                return b
            sin_layer(l1, nt, fh[:, 0, :], j, j, ("scr", "h", 0))
            if dbg == ("h1", g, l):
                P.dma("sp", dbg_lane, lambda h: [h.dma_start(out=yT[0, 0][0:64, 0:L], in_=fh[:, 0, 0:L])], r=(("scr", "h", 0),), n=1)
                raise _Stop()
            for i in range(2):
                def l2(t, i=i):
                    b = bank()
                    P.op("pe", lambda h, b=b, t=t, i=i: h.matmul(ps[b][0:64, 0:n512], lhsT=w2s[:, i, :], rhs=fh[:, i, t * 512:t * 512 + n512], start=True, stop=True),
                         r=(("scr", "in"), ("scr", "h", i)), w=(psk(b),))
                    return b
                sin_layer(l2, nt, fh[:, (i + 1) % 2, :], j, 2 + 2 * j + i, ("scr", "h", (i + 1) % 2))
                if dbg == ("h2", g, l) and i == 0:
                    P.dma("sp", dbg_lane, lambda h: [h.dma_start(out=yT[0, 0][0:64, 0:L], in_=fh[:, 1, 0:L]),
                                                      h.dma_start(out=yT[0, 1][0:64, 0:L], in_=fh[:, 0, 0:L])], r=(("scr", "h", 0), ("scr", "h", 1)), n=2)
                    raise _Stop()
            h3 = fh[:, 0, :]
            if dbg == ("h3", g, l):
                P.dma("sp", dbg_lane, lambda h: [h.dma_start(out=yT[0, 0][0:64, 0:L], in_=h3[:, 0:L])], r=(("scr", "h", 0),), n=1)
                raise _Stop()

            ACC = [0, 1, 2, 3]
            P.phase = "g%d l%d hy-dft" % (g, l)
            for ct in range(4):
                c0 = ct * 256
                for tc in range(8):
                    b = 7

                    def tr(h, tc=tc, ct=ct):
                        psb = ps[7][:, :].bitcast(BF16)
                        ins = None
                        for cc in range(2):
                            ins = h.transpose(psb[:, cc * 128:(cc + 1) * 128], uu[:, 2 * ct + cc, tc * 128:(tc + 1) * 128], ident[:, :])
                        return ins
                    P.op("pe", tr, r=(("big", "u", 2 * ct), ("big", "u", 2 * ct + 1), ("c", "ident")), w=(psk(7),))
                    P.op("act", lambda h, tc=tc: h.activation(out=uT[:, tc, :], in_=ps[7][:, :].bitcast(BF16)[:, 0:256], func=AF.Identity),
                         r=(psk(7),), w=(("big", "uT", tc),))
                wb = ct % 2
                P.dma("sp", wo_lanes[wb], lambda h, wb=wb, c0=c0: [
                    h.dma_start(out=wos[:, wb, :, :], in_=pos_wout[j].rearrange("k (a n) -> k a n", a=2)[:, :, c0:c0 + 256]),
                    h.dma_start(out=deltab[:, wb, :], in_=bass.AP(delta_t, c0, [[0, 128], [1, 256]]))], w=(("scr", "wo", wb),), n=2)
                for tc in range(nj):
                    b = 6
                    P.op("pe", lambda h, tc=tc, wb=wb: h.matmul(ps[6][:, :], lhsT=h3[:, tc * 128:(tc + 1) * 128],
                                                             rhs=wos[:, wb, :, :].rearrange("k a n -> k (a n)"), start=True, stop=True),
                         r=(("scr", "h", 0), ("scr", "wo", wb)), w=(psk(6),))
                    di = tc % 2
                    P.op("act", lambda h, tc=tc, di=di, wb=wb: h.activation(out=dect[:, di, :], in_=deltab[:, wb, :], func=AF.Exp,
                                                                          scale=pp[:, o_tn + tc:o_tn + tc + 1]),
                         r=(("scr", "wo", wb), ("c", "pp")), w=(("scr", "dec", di),))
                    P.op("dve", lambda h, di=di: h.tensor_tensor(out=kft[:, di, 0:256], in0=ps[6][:, 0:256], in1=dect[:, di, :], op=ALU.mult),
                         r=(psk(6), ("scr", "dec", di)), w=(("scr", "kf", di),))
                    if tc == 0:
                        P.op("dve", lambda h, di=di: h.tensor_scalar(out=dect[:, di, :], in0=dect[:, di, :], scalar1=pp[:, o_m0:o_m0 + 1], scalar2=None,
                                                                    op0=ALU.mult), r=(("scr", "dec", di), ("c", "pp")), w=(("scr", "dec", di),))
                    P.op("dve", lambda h, di=di: h.tensor_tensor(out=kft[:, di, 256:512], in0=ps[6][:, 256:512], in1=dect[:, di, :], op=ALU.mult),
                         r=(psk(6), ("scr", "dec", di)), w=(("scr", "kb", di),))
                    for a_, op_ in ((0, ALU.add), (1, ALU.subtract)):
                        P.op("dve", lambda h, di=di, a_=a_, op_=op_, tc=tc: h.tensor_tensor(out=ksd_hi[:, a_, tc, :], in0=kft[:, di, 0:256], in1=kft[:, di, 256:512], op=op_),
                             r=(("scr", "kf", di), ("scr", "kb", di)), w=(("big", "khi", a_),))
                if dbg == ("ks", g, l) and ct == 0:
                    def dks(h):
                        return [h.dma_start(out=yT[0, a_].bitcast(BF16)[:, 0:nj * 256], in_=ksd_hi[:, a_].rearrange("p k n -> p (k n)")) for a_ in range(2)]
                    P.dma("sp", dbg_lane, dks, r=(("big", "khi", 0), ("big", "khi", 1)), n=2)
                    raise _Stop()
                yi = 0
                fgi = [0]
                pendI = [None]
                for jf in range(nj):
                    fb_ = fgi[0] % 2
                    fgi[0] += 1
                    cvF = cvb[:, fb_, :].bitcast(BF16)
                    cvG = cvb[:, 2 + fb_, :].bitcast(BF16)
                    fkey, gkey = ("cv", fb_), ("cv", 2 + fb_)
                    P.dma("sp", fg_lanes[fb_], lambda h, jf=jf, cvF=cvF: [h.dma_start(
                        out=cvF[:, 0:nj * 256].rearrange("p (k n) -> p k n", k=nj), in_=F_d[L][jf][:, 0])], w=(fkey,), n=1)
                    P.dma("sp", fg_lanes[2 + fb_], lambda h, jf=jf, cvG=cvG: [h.dma_start(
                        out=cvG[:, 0:2 * L].rearrange("p (a n) -> p a n", a=2), in_=G_d[L][jf][:, 0])], w=(gkey,), n=1)
                    Fv = cvF[:, 0:nj * 256].rearrange("p (k n) -> p k n", k=nj)
                    Gv = cvG[:, 0:2 * L].rearrange("p (a n) -> p a n", a=2)

                    def mmK(h, Fv=Fv):
                        ins = None
                        for a in range(2):
                            for tc in range(nj):
                                ins = h.matmul(ps[6][:, a * 256:(a + 1) * 256], lhsT=Fv[:, tc, a * 128:(a + 1) * 128], rhs=ksd_hi[:, a, tc, :],
                                               start=(tc == 0), stop=(tc == nj - 1))
                        return ins
                    P.op("pe", mmK, r=(fkey, ("big", "khi", 0), ("big", "khi", 1)), w=(psk(6),))
                    kb_ = jf % 2
                    P.op("act", lambda h, kb_=kb_: h.activation(out=ksb[:, kb_, :], in_=ps[6][:, :], func=AF.Identity), r=(psk(6),), w=(("scr", "K", kb_),))
                    for s in range(nseq):
                        if g == 0 and l + 1 < depth:
                            for _ in range(3 if l == 0 else 2):
                                MS.step(7)
                        ub = 4 + (yi % 2)

                        def mmU(h, Fv=Fv, s=s, ub=ub):
                            ins = None
                            for a in range(2):
                                for tc in range(nj):
                                    ins = h.matmul(ps[ub][:, a * 256:(a + 1) * 256], lhsT=Fv[:, tc, a * 128:(a + 1) * 128], rhs=uT[:, s * nj + tc, :],
                                                   start=(tc == 0), stop=(tc == nj - 1))
                            return ins
                        P.op("pe", mmU, r=(fkey,) + tuple(("big", "uT", s * nj + tc) for tc in range(nj)), w=(psk(ub),))
                        prev = pendI.pop(0)
                        if prev is not None:
                            prev()
                        yb = yi % 2
                        yi += 1
                        Ure, Uim = ps[ub][:, 0:256], ps[ub][:, 256:512]
                        Kre, Kim = ksb[:, kb_, 0:256], ksb[:, kb_, 256:512]
                        rk = (psk(ub), ("scr", "K", kb_))
                        P.op("dve", lambda h, Ure=Ure, Kre=Kre: h.tensor_tensor(out=ctmp[:, 0, :], in0=Ure, in1=Kre, op=ALU.mult), r=rk, w=(("scr", "c", 0),))
                        P.op("dve", lambda h, Uim=Uim, Kim=Kim: h.tensor_tensor(out=ctmp[:, 1, :], in0=Uim, in1=Kim, op=ALU.mult), r=rk, w=(("scr", "c", 1),))
                        P.op("dve", lambda h, yb=yb: h.tensor_tensor(out=ybuf[:, yb, 0:256], in0=ctmp[:, 0, :], in1=ctmp[:, 1, :], op=ALU.subtract),
                             r=(("scr", "c", 0), ("scr", "c", 1)), w=(("scr", "yre", yb),))
                        P.op("dve", lambda h, Ure=Ure, Kim=Kim: h.tensor_tensor(out=ctmp[:, 2, :], in0=Ure, in1=Kim, op=ALU.mult), r=rk, w=(("scr", "c", 2),))
                        P.op("dve", lambda h, Uim=Uim, Kre=Kre: h.tensor_tensor(out=ctmp[:, 3, :], in0=Uim, in1=Kre, op=ALU.mult), r=rk, w=(("scr", "c", 3),))
                        P.op("dve", lambda h, yb=yb: h.tensor_tensor(out=ybuf[:, yb, 256:512], in0=ctmp[:, 2, :], in1=ctmp[:, 3, :], op=ALU.add),
                             r=(("scr", "c", 2), ("scr", "c", 3)), w=(("scr", "yim", yb),))

                        def mmI(h, Gv=Gv, s=s, yb=yb, jf=jf):
                            ins = None
                            for cc in range(2):
                                for a in range(2):
                                    lhs = ybuf[:, yb, a * 256 + cc * 128:a * 256 + (cc + 1) * 128]
                                    first = (jf == 0 and a == 0)
                                    last = (jf == nj - 1 and a == 1)
                                    if L == 1024:
                                        for tt in range(2):
                                            ins = h.matmul(ps[ACC[cc * 2 + tt]][:, :], lhsT=lhs, rhs=Gv[:, a, tt * 512:(tt + 1) * 512], start=first, stop=last)
                                    else:
                                        ins = h.matmul(ps[ACC[s]][:, cc * 256:(cc + 1) * 256], lhsT=lhs, rhs=Gv[:, a, :],
                                                       start=(first and cc == 0), stop=(last and cc == 1))
                            return ins
                        nxt = (lambda mmI=mmI, gkey=gkey, yb=yb: P.op("pe", mmI, r=(gkey, ("scr", "yre", yb), ("scr", "yim", yb)), w=tuple(psk(a_) for a_ in ACC)))
                        pendI.append(nxt)
                pendI.pop(0)()
                for cc in range(2):
                    c = 2 * ct + cc
                    for q in range(2 if L == 1024 else 4):
                        if L == 1024:
                            accap = ps[ACC[cc * 2 + q]][:, :]
                            tok = slice(q * 512, (q + 1) * 512)
                            n = 512
                            ab = ACC[cc * 2 + q]
                        else:
                            accap = ps[ACC[q]][:, cc * 256:(cc + 1) * 256]
                            tok = slice(q * 256, (q + 1) * 256)
                            n = 256
                            ab = ACC[q]
                        ei = (cc * 4 + q) % 4
                        P.op("dve", lambda h, c=c, tok=tok, accap=accap, ei=ei, n=n: h.scalar_tensor_tensor(
                            out=etmp[:, ei, 0:n], in0=uu[:, c, tok], scalar=pp[:, o_fb + c:o_fb + c + 1], in1=accap, op0=ALU.mult, op1=ALU.add),
                            r=(("big", "u", c), psk(ab), ("c", "pp")), w=(("et", ei),))
                        P.op("dve", lambda h, c=c, tok=tok, ei=ei, n=n: h.tensor_tensor(out=x0[:, c, tok], in0=etmp[:, ei, 0:n], in1=x0[:, c, tok], op=ALU.mult),
                             r=(("et", ei), ("big", "x0", c)), w=(("big", "x0", c),))
            if g == 0 and l == 0:
                MS.drain(0, 7)
                mod_finish(0)
            if g == 0 and l + 1 < depth:
                MS.drain(l + 1, 7)
                mod_finish(l + 1)
            P.phase = "g%d l%d hy-out+ln" % (g, l)
            proj_ln(g, l, 0, 8, lambda k, t: x0[:, k, tsl(t)], lambda k, t: ("big", "x0", k), wload_sq(hy_out_w[j], 8), tail_hook=mixer_tail(g, l))

    def attention(g, l):
        j = l // 2
        latent = (g == 1)
        if g == 0 and l + 1 < depth:
            MS.add_layer(l + 1)
        P.phase = "g%d l%d at-qkv" % (g, l)
        P.fence("big")
        P.fence("scr")
        qT = big[:, 0:8192].rearrange("p (h t) -> p h t", h=8)
        oT = big[:, 8192:16384].rearrange("p (h t) -> p h t", h=8)
        kT = big[:, 16384:19456].rearrange("p (h t) -> p h t", h=2)
        Vv = big[:, 19456:22528].rearrange("p (k n) -> p k n", k=12)
        lsb = carve()
        qkvb = lsb([128, 1536])
        qgain = lsb([128, 128])
        kgain = lsb([128, 128])
        rope = lsb([128, 8, 2, 64])
        kcb = lsb([128, 4, 256], BF16)
        mark = lsb.off[0]
        NQ = 4
        qf = lsb([128, NQ, 512])
        qsq = lsb([128, 2, 512])
        qst = lsb([128, NQ, 4])
        qr = lsb([128, NQ, 512], BF16)
        rtmp = lsb([128, 4, 256])
        lsb.off[0] = mark
        pT = lsb([128, 3, 512], BF16)
        rdt = lsb([128, 2, 512])
        scale = float(128 ** -0.5)

        def al(h):
            return [h.dma_start(out=qkvb[:], in_=bass.AP(qkvb_t, j * 1536, [[0, 128], [1, 1536]])),
                    h.dma_start(out=qgain[:], in_=bass.AP(qgain_t, j * 128, [[0, 128], [1, 128]])),
                    h.dma_start(out=kgain[:], in_=bass.AP(kgain_t, j * 128, [[0, 128], [1, 128]])),
                    h.dma_start(out=rope[:], in_=rope_d[:, :, :, :])]
        P.dma("sp", ain_lane, al, w=(("scr", "ain"),), n=4)
        if latent:
            P.dma("pool", kc_lane, lambda h: [h.dma_start(out=kcb[:], in_=cache_k[j].rearrange("(c p) n -> p c n", p=128))],
                  w=(("scr", "kc"),), n=1)
            P.dma("pool", vc_lane, lambda h: [h.dma_start(out=Vv[:, 8:12, :], in_=cache_v[j].rearrange("(c p) n -> p c n", p=128))],
                  w=tuple(("big", "V", 8 + i) for i in range(4)), n=1)

        it = 0
        def qkv_load(ct):
            def ld(h, view, ct=ct):
                return [h.dma_start(out=view[:, 0:4096].rearrange("p (k n) -> p k n", k=8),
                                    in_=at_qkv_w[j].rearrange("(k p) n -> p k n", p=128)[:, :, ct * 512:(ct + 1) * 512])]
            return load_slab(ld, 1)
        qloaded = [qkv_load(ct) for ct in range(3)]
        def make_item(ct, tc, it, wv, wkey):
            nh = 4 if ct < 2 else 2
            nw = nh * 128
            gain = qgain if ct < 2 else kgain
            qi = it % NQ
            sqi = it % 2
            kq = ("scr", "qf", qi)
            ks_ = ("scr", "qst", qi)
            kr = ("scr", "qr", qi)
            qv = qf[:, qi, 0:nw].rearrange("p (a b) -> p a b", a=nh)
            st_ = {}

            def s0():
                b = bank()
                while b >= 6:
                    b = bank()
                if g == 0 and l + 1 < depth:
                    MS.step(6)
                    MS.step(6)

                def mm(h):
                    ins = None
                    for k in range(8):
                        ins = h.matmul(ps[b][:, :], lhsT=hT[:, k, tc * 128:(tc + 1) * 128], rhs=wv[:, k, :], start=(k == 0), stop=(k == 7))
                    return ins
                P.op("pe", mm, r=(wkey,) + tuple(hk(k, tc // 4) for k in range(8)), w=(psk(b),))
                P.op("dve", lambda h: h.tensor_tensor(out=qf[:, qi, :], in0=ps[b][:, :], in1=qkvb[:, ct * 512:(ct + 1) * 512], op=ALU.add),
                     r=(psk(b), ("scr", "ain")), w=(kq,))
                P.op("act", lambda h: h.activation(out=qsq[:, sqi, 0:nw], in_=qf[:, qi, 0:nw], func=AF.Square), r=(kq,), w=(("scr", "qsq", sqi),))

            def s1():
                P.op("dve", lambda h: h.tensor_reduce(out=qst[:, qi, 0:nh], in_=qsq[:, sqi, 0:nw].rearrange("p (a b) -> p a b", a=nh),
                                                      axis=AX.X, op=ALU.add), r=(("scr", "qsq", sqi),), w=(ks_,))
                P.op("dve", lambda h: h.tensor_scalar(out=qst[:, qi, 0:nh], in0=qst[:, qi, 0:nh], scalar1=1.0 / 128.0, scalar2=QK_EPS,
                                                      op0=ALU.mult, op1=ALU.add), r=(ks_,), w=(ks_,))
                P.op("act", lambda h: h.activation(out=qst[:, qi, 0:nh], in_=qst[:, qi, 0:nh], func=AF.Sqrt), r=(ks_,), w=(ks_,))

            def s2():
                P.op("dve", lambda h: h.reciprocal(out=qst[:, qi, 0:nh], in_=qst[:, qi, 0:nh]), r=(ks_,), w=(ks_,))
                P.op("dve", lambda h: h.tensor_tensor(out=qv, in0=qv, in1=qst[:, qi, 0:nh].unsqueeze(2).to_broadcast([128, nh, 128]), op=ALU.mult),
                     r=(kq, ks_), w=(kq,))
                P.op("pool" if latent else "dve", lambda h: h.tensor_tensor(out=qv, in0=qv, in1=gain[:, :].unsqueeze(1).to_broadcast([128, nh, 128]),
                                                                            op=ALU.mult), r=(kq, ("scr", "ain")), w=(kq,))

            def s3():
                if latent:
                    q4 = qf[:, qi, 0:nw].rearrange("p (a b two) -> p a b two", a=nh, two=2)
                    r4 = qr[:, qi, 0:nw].rearrange("p (a b two) -> p a b two", a=nh, two=2)
                    ev, od = q4[:, :, :, 0], q4[:, :, :, 1]
                    cosb = rope[:, tc, 0, :].unsqueeze(1).to_broadcast([128, nh, 64])
                    sinb = rope[:, tc, 1, :].unsqueeze(1).to_broadcast([128, nh, 64])
                    nr = nh * 64
                    rv = [rtmp[:, i, 0:nr].rearrange("p (a b) -> p a b", a=nh) for i in range(4)]
                    rr = (kq, ("scr", "ain"))
                    P.op("dve", lambda h: h.tensor_tensor(out=rv[0], in0=ev, in1=cosb, op=ALU.mult), r=rr, w=(("scr", "rt", 0),))
                    P.op("dve", lambda h: h.tensor_tensor(out=rv[1], in0=od, in1=sinb, op=ALU.mult), r=rr, w=(("scr", "rt", 1),))
                    P.op("dve", lambda h: h.tensor_tensor(out=r4[:, :, :, 0], in0=rv[0], in1=rv[1], op=ALU.subtract),
                         r=(("scr", "rt", 0), ("scr", "rt", 1)), w=(kr,))
                    P.op("pool", lambda h: h.tensor_tensor(out=rv[2], in0=ev, in1=sinb, op=ALU.mult), r=rr, w=(("scr", "rt", 2),))
                    P.op("pool", lambda h: h.tensor_tensor(out=rv[3], in0=od, in1=cosb, op=ALU.mult), r=rr, w=(("scr", "rt", 3),))
                    P.op("pool", lambda h: h.tensor_tensor(out=r4[:, :, :, 1], in0=rv[2], in1=rv[3], op=ALU.add),
                         r=(("scr", "rt", 2), ("scr", "rt", 3), kr), w=(kr,))
                else:
                    P.op("act", lambda h: h.activation(out=qr[:, qi, 0:nw], in_=qf[:, qi, 0:nw], func=AF.Identity), r=(kq,), w=(kr,))
                if ct == 2:
                    P.op("act", lambda h: h.activation(out=Vv[:, tc, :], in_=qf[:, qi, 256:512], func=AF.Identity), r=(kq,), w=(("big", "V", tc),))
                    if not latent:
                        s_, r0 = tc // 2, (tc % 2) * 128
                        P.dma("sp", kv_lanes[qi], lambda h: [
                            h.dma_start(out=nk_o[s_, j, r0:r0 + 128, :], in_=qf[:, qi, 0:256]),
                            h.dma_start(out=nv_o[s_, j, r0:r0 + 128, :], in_=qf[:, qi, 256:512])], r=(kq,), n=2)

            def s4():
                def tr(h):
                    psb = ps[7][:, :].bitcast(BF16)
                    ins = None
                    for hh in range(nh):
                        ins = h.transpose(psb[:, hh * 128:(hh + 1) * 128], qr[:, qi, hh * 128:(hh + 1) * 128], ident[:, :])
                    return ins
                P.op("pe", tr, r=(kr, ("c", "ident")), w=(psk(7),))
                if ct < 2:
                    P.op("act", lambda h: h.activation(out=qT[:, ct * 4:ct * 4 + 4, tc * 128:(tc + 1) * 128],
                                                       in_=ps[7][:, :].bitcast(BF16)[:, 0:512].rearrange("p (a b) -> p a b", a=4), func=AF.Identity),
                         r=(psk(7),), w=tuple(("big", "qT", ct * 4 + hh) for hh in range(4)))
                else:
                    P.op("act", lambda h: h.activation(out=kT[:, :, tc * 128:(tc + 1) * 128],
                                                       in_=ps[7][:, :].bitcast(BF16)[:, 0:256].rearrange("p (a b) -> p a b", a=2), func=AF.Identity),
                         r=(psk(7),), w=(("big", "kT", 0), ("big", "kT", 1)))
            return [s0, s1, s2, s3, s4]

        items = []
        for ct in range(3):
            view, wkey = qloaded[ct]
            wv = view[:, 0:4096].rearrange("p (k n) -> p k n", k=8)
            for tc in range(8):
                items.append(make_item(ct, tc, len(items), wv, wkey))
        NST = 5
        for n in range(len(items) + NST - 1):
            for sidx in range(NST):
                i = n - sidx
                if 0 <= i < len(items):
                    items[i][sidx]()
        if latent:
            for kc in range(4):
                def trc(h, kc=kc):
                    psb = ps[7][:, :].bitcast(BF16)
                    ins = None
                    for kvh in range(2):
                        ins = h.transpose(psb[:, kvh * 128:(kvh + 1) * 128], kcb[:, kc, kvh * 128:(kvh + 1) * 128], ident[:, :])
                    return ins
                P.op("pe", trc, r=(("scr", "kc"), ("c", "ident")), w=(psk(7),))
                P.op("act", lambda h, kc=kc: h.activation(out=kT[:, :, 1024 + kc * 128:1024 + (kc + 1) * 128],
                                                        in_=ps[7][:, :].bitcast(BF16)[:, 0:256].rearrange("p (a b) -> p a b", a=2), func=AF.Identity),
                     r=(psk(7),), w=(("big", "kT", 0), ("big", "kT", 1)))

        if g == 0 and l + 1 < depth:
            MS.drain(l + 1, 6)
            mod_finish(l + 1)
        P.fence("scr")
        P.phase = "g%d l%d at-core" % (g, l)
        units = []
        if latent:
            for qt in range(2):
                for kvh in range(2):
                    for hh in range(4):
                        hd_ = kvh * 4 + hh
                        units.append(dict(kvh=kvh, rhs=qT[:, hd_, qt * 512:(qt + 1) * 512], rk=(("big", "qT", hd_),), kcs=list(range(12)),
                                          out=oT[:, hd_, qt * 512:(qt + 1) * 512], ok=(("big", "oT", hd_),), v3=False))
        else:
            for s_ in range(4):
                for kvh in range(2):
                    for hp in range(2):
                        h0 = kvh * 4 + hp * 2
                        units.append(dict(kvh=kvh, rhs=qT[:, h0:h0 + 2, s_ * 256:(s_ + 1) * 256], rk=(("big", "qT", h0), ("big", "qT", h0 + 1)),
                                          kcs=[2 * s_, 2 * s_ + 1], out=oT[:, h0:h0 + 2, s_ * 256:(s_ + 1) * 256],
                                          ok=(("big", "oT", h0), ("big", "oT", h0 + 1)), v3=True))
        srot = 0
        for ui, u in enumerate(units):
            ob = 3 + ui % 2
            db = 5 + ui % 2
            kvh = u["kvh"]
            pend = None
            nk_ = len(u["kcs"])

            def od(idx, kc, pi, u=u, ob=ob, db=db, kvh=kvh, nk_=nk_):
                def f(h):
                    h.matmul(ps[ob][:, :], lhsT=Vv[:, kc, kvh * 128:(kvh + 1) * 128], rhs=pT[:, pi, :], start=(idx == 0), stop=(idx == nk_ - 1))
                    return h.matmul(ps[db][:, :], lhsT=onesb[:, :], rhs=pT[:, pi, :], start=(idx == 0), stop=(idx == nk_ - 1))
                P.op("pe", f, r=(("big", "V", kc), ("scr", "pT", pi), ("c", "onesb")), w=(psk(ob), psk(db)))
            for idx, kc in enumerate(u["kcs"]):
                sb_ = srot % 3
                pi = srot % 3
                srot += 1
                P.op("pe", lambda h, sb_=sb_, kc=kc, u=u, kvh=kvh: h.matmul(ps[sb_][:, :], lhsT=kT[:, kvh, kc * 128:(kc + 1) * 128], rhs=u["rhs"],
                                                                         start=True, stop=True),
                     r=(("big", "kT", kvh),) + u["rk"], w=(psk(sb_),))
                P.op("act", lambda h, sb_=sb_, pi=pi: h.activation(out=pT[:, pi, :], in_=ps[sb_][:, :], func=AF.Exp, scale=scale),
                     r=(psk(sb_),), w=(("scr", "pT", pi),))
                if pend is not None:
                    od(*pend)
                pend = (idx, kc, pi)
            od(*pend)
            ri = ui % 2
            P.op("dve", lambda h, db=db, ri=ri: h.reciprocal(out=rdt[:, ri, :], in_=ps[db][:, :]), r=(psk(db),), w=(("scr", "rd", ri),))
            if u["v3"]:
                P.op("dve", lambda h, ob=ob, ri=ri, u=u: h.tensor_tensor(out=u["out"], in0=ps[ob][:, :].rearrange("p (a b) -> p a b", a=2),
                                                                      in1=rdt[:, ri, :].rearrange("p (a b) -> p a b", a=2), op=ALU.mult),
                     r=(psk(ob), ("scr", "rd", ri)), w=u["ok"])
            else:
                P.op("dve", lambda h, ob=ob, ri=ri, u=u: h.tensor_tensor(out=u["out"], in0=ps[ob][:, :], in1=rdt[:, ri, :], op=ALU.mult),
                     r=(psk(ob), ("scr", "rd", ri)), w=u["ok"])
        P.phase = "g%d l%d at-out+ln" % (g, l)
        proj_ln(g, l, 0, 8, lambda k, t: oT[:, k, tsl(t)], lambda k, t: ("big", "oT", k), wload_sq(at_o_w[j], 8), tail_hook=mixer_tail(g, l))

    allx = tuple(xk(m, t) for m in range(8) for t in range(2))
    stop = False
    for g in range(ngroups):
        c = g
        P.dma("sp", x_lane, lambda h, g=g: [h.dma_start(out=xT[:, m, :], in_=xin[g, m]) for m in range(8)], w=allx, n=8)
        for m in range(8):
            for t in range(2):
                P.op("dve", lambda h, m=m, t=t, c=c: h.tensor_scalar(out=hT[:, m, tsl(t)], in0=xT[:, m, tsl(t)], scalar1=modcol(mod1, 0, 1, m, c),
                                                                   scalar2=modcol(mod, 0, 0, m, c), op0=ALU.mult, op1=ALU.add),
                     r=(xk(m, t),) + MODK(0), w=(hk(m, t),))
        for l in range(depth):
            try:
                if l % 2 == 0:
                    hyena(g, l)
                else:
                    attention(g, l)
            except _Stop:
                stop = True
                break
            if dbg_dump((g, l, 0)):
                stop = True
                break
            ffn(g, l)
            if dbg_dump((g, l, 1)):
                stop = True
                break
        if stop:
            break
        P.dma("sp", y_lane, lambda h, g=g: [h.dma_start(out=yT[g, m], in_=xT[:, m, :]) for m in range(8)], r=allx, n=8)

    P.finalize()
    for ln in P.dma_lanes:
        ln.sem = semaphore("d_" + ln.name)
    with nc.Block() as block:
        @block.tensor
        def _(h):
            P.emit("pe", h)

        @block.scalar
        def _(h):
            P.emit("act", h)

        @block.vector
        def _(h):
            P.emit("dve", h)

        @block.gpsimd
        def _(h):
            P.emit("pool", h)

        @block.sync
        def _(h):
            P.emit("sp", h)
    es.close()
    nc._prog_stats = {e: len(P.eng_ops[e]) for e in P.ENGS}
    nc._pe_log = P.pe_log
    return nc


_NC_CACHE = {}


def make_in_maps(inp):
    f32 = lambda a: np.ascontiguousarray(np.asarray(a, np.float32))
    consts = make_consts()
    pp = pack_pp(inp)
    shared = {
        "pp": pp,
        "w_mod": f32(inp["w_mod"]), "hy_in_w": f32(inp["hy_in_w"]), "hy_out_w": f32(inp["hy_out_w"]),
        "at_qkv_w": f32(inp["at_qkv_w"]), "at_o_w": f32(inp["at_o_w"]), "ff_in_w": f32(inp["ff_in_w"]), "ff_out_w": f32(inp["ff_out_w"]),
        "pos_w1": f32(inp["hy_pos_w1"]), "pos_w2": f32(inp["hy_pos_w2"]).reshape(4, 64, 64), "pos_wout": f32(inp["hy_pos_wout"]),
        "qkvb": f32(inp["at_qkv_b"]), "qgain": f32(inp["at_q_gain"]), "kgain": f32(inp["at_k_gain"]),
        "ident": consts["ident"], "zT256": consts["zT256"], "zT1024": consts["zT1024"],
        "F256": consts["F256"], "F1024": consts["F1024"], "G256": consts["G256"], "G1024": consts["G1024"],
        "delta": consts["delta"], "rope": consts["rope"],
    }
    xp = f32(inp["x_prompt"])
    xs = f32(inp["x_sample"])
    ck = f32(inp["cache_k"])
    cv = f32(inp["cache_v"])
    cc = f32(inp["c"])
    cctx = f32(inp["c_ctx"])
    maps = []
    for core in range(8):
        x0 = xp[4 * core:4 * core + 4].reshape(1024, 1024)
        x1 = xs[core]
        xin = np.stack([x0.T.reshape(8, 128, 1024), x1.T.reshape(8, 128, 1024)], 0)
        cond = np.stack([cctx, cc[core]], 0).reshape(2, 8, 128).transpose(2, 1, 0)
        m = dict(shared)
        m["xin"] = np.ascontiguousarray(xin)
        m["cond"] = np.ascontiguousarray(cond)
        m["cache_k"] = np.ascontiguousarray(ck[core].reshape(2, 512, 256))
        m["cache_v"] = np.ascontiguousarray(cv[core].reshape(2, 512, 256))
        maps.append(m)
    return maps


def kernel(**inp):
    if "nc" not in _NC_CACHE:
        _NC_CACHE["nc"] = build_nc()
    nc = _NC_CACHE["nc"]
    maps = make_in_maps(inp)
    res = run_bass_kernel_spmd(nc, maps, core_ids=list(range(8)))
    y_prompt = np.zeros((32, 256, 1024), np.float32)
    y_sample = np.zeros((8, 1024, 1024), np.float32)
    nk = np.zeros((32, 2, 256, 2, 128), np.float32)
    nv = np.zeros((32, 2, 256, 2, 128), np.float32)
    for core in range(8):
        r = res.results[core]
        yT = np.asarray(r["yT"], np.float32)
        y_prompt[4 * core:4 * core + 4] = yT[0].reshape(1024, 1024).T.reshape(4, 256, 1024)
        y_sample[core] = yT[1].reshape(1024, 1024).T
        nk[4 * core:4 * core + 4] = np.asarray(r["nk"], np.float32).reshape(4, 2, 256, 2, 128)
        nv[4 * core:4 * core + 4] = np.asarray(r["nv"], np.float32).reshape(4, 2, 256, 2, 128)
    return (y_prompt, y_sample, nk, nv)
```

```python
import contextlib
import numpy as np
import ml_dtypes
import concourse.bass as bass
import concourse.mybir as mybir
from concourse.bass_utils import run_bass_kernel_spmd

F32 = mybir.dt.float32
BF16 = mybir.dt.bfloat16
F32R = mybir.dt.float32r
AF = mybir.ActivationFunctionType
ALU = mybir.AluOpType
AX = mybir.AxisListType

D = 1024
NM = 8
TOK = 1024
DEPTH = 4
DFF = 2816
NFC = 22
LN_EPS = 1e-5
QK_EPS = 1e-6
ALPHA = float((2 * DEPTH) ** 0.25)
GROUPS = [dict(nseq=4, L=256), dict(nseq=1, L=1024)]
MAGIC = 12582912.0
TWO_PI = float(2 * np.pi)
NSLOT = 3
MULT_ENG = "pool"
SLOT_EL = 4096

DBG = None


class _Stop(Exception):
    pass


class _CountProxy:
    def __init__(self, h):
        self.h = h
        self.n = 0

    def matmul(self, *a, **k):
        self.n += 1
        return self.h.matmul(*a, **k)

    def transpose(self, *a, **k):
        self.n += 1
        return self.h.transpose(*a, **k)

    def __getattr__(self, name):
        return getattr(self.h, name)


class Lane:
    def __init__(self, name, step):
        self.name = name
        self.step = step
        self.ops = []
        self.sem = None
        self.cum = None

    def count_at(self, seq):
        return self.cum[seq]


class Op:
    __slots__ = ("eng", "lane", "seq", "fn", "deps", "need", "vc", "isdma", "waits", "ninc", "phase")


class Prog:
    ENGS = ("pe", "act", "dve", "pool", "sp")

    def __init__(self):
        self.lanes = {e: Lane(e, 1) for e in ("pe", "act", "dve", "pool")}
        self.dma_lanes = []
        self.eng_ops = {e: [] for e in self.ENGS}
        self.all_ops = []
        self.last_w = {}
        self.readers = {}
        self.fences = {}
        self.store_lanes = []
        self.gfence = {}
        self.phase = ""
        self.pe_log = []

    def dma_lane(self, name, store=False):
        ln = Lane(name, 16)
        self.dma_lanes.append(ln)
        if store:
            self.store_lanes.append(ln)
        return ln

    def _add(self, eng, lane, fn, reads, writes, isdma, ninc=1):
        op = Op()
        op.eng, op.lane, op.seq, op.fn, op.isdma, op.need, op.ninc = eng, lane, len(lane.ops), fn, isdma, isdma, ninc
        op.phase = self.phase
        deps = {}
        own = self.lanes.get(eng)

        def dep(ln, sq, raw):
            if (not isdma) and (ln is own) and (not raw) and eng == "pe":
                return
            cur = deps.get(ln)
            if cur is None or sq > cur:
                deps[ln] = sq

        for ln, sq in self.gfence.items():
            dep(ln, sq, True)
        for k in reads:
            f = self.fences.get(k[0])
            if f:
                for ln, sq in f.items():
                    dep(ln, sq, True)
            w = self.last_w.get(k)
            if w:
                dep(w[0], w[1], True)
        for k in writes:
            f = self.fences.get(k[0])
            if f:
                for ln, sq in f.items():
                    dep(ln, sq, True)
            w = self.last_w.get(k)
            if w:
                dep(w[0], w[1], False)
            for ln, sq in self.readers.get(k, {}).items():
                dep(ln, sq, False)
        for ln in list(deps):
            if ln.step == 16:
                deps[ln] = len(ln.ops) - 1
        if lane in deps and deps[lane] >= op.seq:
            deps[lane] = op.seq - 1
        op.deps = deps
        lane.ops.append(op)
        self.eng_ops[eng].append(op)
        self.all_ops.append(op)
        for k in reads:
            self.readers.setdefault(k, {})[lane] = op.seq
        for k in writes:
            self.last_w[k] = (lane, op.seq)
            self.readers[k] = {}
        return op

    def op(self, eng, fn, r=(), w=()):
        return self._add(eng, self.lanes[eng], fn, r, w, False)

    def dma(self, eng, lane, fn, r=(), w=(), n=1):
        return self._add(eng, lane, fn, r, w, True, n)

    def barrier(self):
        for ln in list(self.lanes.values()) + self.dma_lanes:
            if ln.ops:
                self.gfence[ln] = len(ln.ops) - 1

    def fence(self, region):
        f = dict(self.fences.get(region, {}))

        def upd(ln, sq):
            if f.get(ln, -1) < sq:
                f[ln] = sq

        for k in [k for k in self.last_w if k[0] == region]:
            ln, sq = self.last_w.pop(k)
            upd(ln, sq)
        for k in [k for k in self.readers if k[0] == region]:
            for ln, sq in self.readers.pop(k).items():
                upd(ln, sq)
        self.fences[region] = f

    def finalize(self):
        know = {e: {} for e in self.ENGS}
        for op in self.all_ops:
            K = know[op.eng]
            waits = []
            for ln, sq in op.deps.items():
                if sq < 0 or K.get(ln, -1) >= sq:
                    continue
                waits.append((ln, sq))
                tgt = ln.ops[sq]
                tgt.need = True
                for l2, s2 in tgt.vc.items():
                    if K.get(l2, -1) < s2:
                        K[l2] = s2
                if K.get(ln, -1) < sq:
                    K[ln] = sq
            op.waits = waits
            vc = dict(K)
            if vc.get(op.lane, -1) < op.seq:
                vc[op.lane] = op.seq
            op.vc = vc
        for ln in list(self.lanes.values()) + self.dma_lanes:
            c = 0
            ln.cum = []
            for o in ln.ops:
                if ln.step == 16:
                    c += 16 * o.ninc
                elif o.need:
                    c += 1
                ln.cum.append(c)

    def emit(self, eng, h):
        if eng == "pe":
            h = _CountProxy(h)
        for op in self.eng_ops[eng]:
            for ln, sq in op.waits:
                h.wait_ge(ln.sem, ln.count_at(sq))
            if eng == "pe":
                n0 = h.n
            ins = op.fn(h)
            if eng == "pe":
                self.pe_log.append((op.phase, h.n - n0))
            if op.isdma:
                if not isinstance(ins, (list, tuple)):
                    ins = [ins]
                assert len(ins) == op.ninc
                for i in ins:
                    i.then_inc(op.lane.sem, 16)
            elif op.need:
                ins.then_inc(op.lane.sem, 1)
        if eng == "sp":
            for ln in self.store_lanes:
                if ln.ops:
                    h.wait_ge(ln.sem, ln.cum[-1])


def _pp_layout():
    items = [
        ("bmod", 192), ("lng", 64), ("lnb", 64),
        ("hy_in_b", 48), ("hy_sw", 144), ("hy_sb", 48), ("hy_fb", 16), ("hy_ob", 16),
        ("at_ob", 16), ("ff_in_b", 176), ("ff_cw", 528), ("ff_cb", 176), ("ff_ob", 32),
        ("freq", 2), ("pb1", 2), ("pb2", 4), ("tn256", 2), ("tn1024", 8), ("mask0", 1),
    ]
    off = {}
    o = 0
    for n, w in items:
        off[n] = (o, w)
        o += w
    return off, o


PP_OFF, NPP = _pp_layout()


def _cp(v):
    v = np.asarray(v, np.float32)
    sh = v.shape
    c = sh[-1] // 128
    v = v.reshape(sh[:-1] + (c, 128))
    v = np.moveaxis(v, -1, 0)
    return np.ascontiguousarray(v).reshape(128, -1)


def pack_pp(inp):
    pp = np.zeros((128, NPP), np.float32)

    def put(name, arr):
        o, w = PP_OFF[name]
        assert arr.shape == (128, w), (name, arr.shape, w)
        pp[:, o:o + w] = arr

    put("bmod", _cp(inp["b_mod"]))
    put("lng", _cp(inp["ln_g"]))
    put("lnb", _cp(inp["ln_b"]))
    put("hy_in_b", _cp(inp["hy_in_b"]))
    put("hy_sw", _cp(inp["hy_short_w"]))
    put("hy_sb", _cp(inp["hy_short_b"]))
    put("hy_fb", _cp(inp["hy_filt_bias"]))
    put("hy_ob", _cp(inp["hy_out_b"]))
    put("at_ob", _cp(inp["at_o_b"]))
    put("ff_in_b", _cp(inp["ff_in_b"]))
    put("ff_cw", _cp(inp["ff_conv_w"]))
    put("ff_cb", _cp(inp["ff_conv_b"]))
    put("ff_ob", _cp(inp["ff_out_b"]))
    z = np.zeros((128, 2), np.float32)
    z[:64] = np.asarray(inp["hy_freq"], np.float32).T
    put("freq", z)
    z = np.zeros((128, 2), np.float32)
    z[:64] = np.asarray(inp["hy_pos_b1"], np.float32).T
    put("pb1", z)
    z = np.zeros((128, 4), np.float32)
    z[:64] = np.asarray(inp["hy_pos_b2"], np.float32).reshape(4, 64).T
    put("pb2", z)
    for L in (256, 1024):
        t = np.linspace(0.0, 1.0, L, dtype=np.float32)
        put("tn%d" % L, np.ascontiguousarray(-t.reshape(L // 128, 128).T))
    m = np.ones((128, 1), np.float32)
    m[0, 0] = 0.0
    put("mask0", m)
    return pp


def _round_f32r(a):
    u = np.ascontiguousarray(a, np.float32).view(np.uint32).astype(np.uint64)
    u = ((u + 0x800) & 0xFFFFF000).astype(np.uint32)
    return u.view(np.float32)


def make_consts():
    c = {}
    c["ident"] = np.eye(128).astype(ml_dtypes.bfloat16)
    for L in (256, 1024):
        N = 2 * L
        t = np.arange(L, dtype=np.float64)
        k = np.arange(L, dtype=np.float64)
        th = 2.0 * np.pi * np.outer(t, k + 0.5) / N
        C = np.cos(th)
        S = -np.sin(th)
        nj = L // 128
        Fm = np.zeros((nj, 128, nj, 256), np.float64)
        Gm = np.zeros((nj, 128, 2, L), np.float64)
        for j in range(nj):
            cr = C[:, j * 128:(j + 1) * 128].reshape(nj, 128, 128)
            ci = S[:, j * 128:(j + 1) * 128].reshape(nj, 128, 128)
            Fm[j, :, :, 0:128] = cr.transpose(1, 0, 2)
            Fm[j, :, :, 128:256] = ci.transpose(1, 0, 2)
            Gm[j, :, 0, :] = (2.0 / N) * C[:, j * 128:(j + 1) * 128].T
            Gm[j, :, 1, :] = (2.0 / N) * S[:, j * 128:(j + 1) * 128].T
        Fh = Fm.astype(np.float32).astype(ml_dtypes.bfloat16)
        Fl = (Fm - Fh.astype(np.float64)).astype(np.float32).astype(ml_dtypes.bfloat16)
        c["F%d" % L] = np.ascontiguousarray(np.stack([Fh, Fl], 2))
        Gh = Gm.astype(np.float32).astype(ml_dtypes.bfloat16)
        Gl = (Gm - Gh.astype(np.float64)).astype(np.float32).astype(ml_dtypes.bfloat16)
        c["G%d" % L] = np.ascontiguousarray(np.stack([Gh, Gl], 2))
        tl = np.linspace(0.0, 1.0, L, dtype=np.float32)[:, None]
        w = (2.0 * np.pi * np.arange(L, dtype=np.float32) / L).astype(np.float32)
        f = np.linspace(1e-4, 15, 16, dtype=np.float32)
        ang = (w[:, None] * f[None, :]).astype(np.float32)
        z = np.concatenate([tl, np.cos(ang), -np.sin(ang)], -1).astype(np.float32)
        c["zT%d" % L] = np.ascontiguousarray(z.T)
    max_decay = np.log(1e-2) / 0.3
    min_decay = np.log(1e-2) / 1.5
    c["delta"] = np.abs(np.linspace(min_decay, max_decay, D, dtype=np.float32)).astype(np.float32)
    rows = np.repeat(np.arange(16), 64).astype(np.float32)
    cols = np.tile(np.arange(64), 16).astype(np.float32)
    inv = (10000.0 ** (-np.arange(0, 64, 2, dtype=np.float32) / 64)).astype(np.float32)
    ang = np.concatenate([rows[:, None] * inv, cols[:, None] * inv], -1).astype(np.float32)
    cs = np.stack([np.cos(ang), np.sin(ang)], 1).astype(np.float32)
    c["rope"] = np.ascontiguousarray(cs.reshape(8, 128, 2, 64).transpose(1, 0, 2, 3))
    return c


def build_nc(dbg=None, ngroups=2, depth=DEPTH):
    nc = bass.Bass("TRN2", target_bir_lowering=False)
    P = Prog()
    es = contextlib.ExitStack()

    def din(name, shape, dt=F32):
        return nc.dram_tensor(name, list(shape), dt, kind="ExternalInput")

    xin = din("xin", [2, 8, 128, 1024]).ap()
    cond = din("cond", [128, 8, 2]).ap()
    cache_k = din("cache_k", [2, 512, 256]).ap()
    cache_v = din("cache_v", [2, 512, 256]).ap()
    ppd = din("pp", [128, NPP]).ap()
    w_mod = din("w_mod", [4, 1024, 6144]).ap()
    hy_in_w = din("hy_in_w", [2, 1024, 3072]).ap()
    hy_out_w = din("hy_out_w", [2, 1024, 1024]).ap()
    at_qkv_w = din("at_qkv_w", [2, 1024, 1536]).ap()
    at_o_w = din("at_o_w", [2, 1024, 1024]).ap()
    ff_in_w = din("ff_in_w", [4, 1024, 5632]).ap()
    ff_out_w = din("ff_out_w", [4, 2816, 1024]).ap()
    pos_w1 = din("pos_w1", [2, 33, 64]).ap()
    pos_w2 = din("pos_w2", [4, 64, 64]).ap()
    pos_wout = din("pos_wout", [2, 64, 2048]).ap()
    qkvb_t = din("qkvb", [2, 1536])
    qgain_t = din("qgain", [2, 128])
    kgain_t = din("kgain", [2, 128])
    ident_d = din("ident", [128, 128], BF16).ap()
    zT_d = {L: din("zT%d" % L, [33, L]).ap() for L in (256, 1024)}
    F_d = {L: din("F%d" % L, [L // 128, 128, 2, L // 128, 256], BF16).ap() for L in (256, 1024)}
    G_d = {L: din("G%d" % L, [L // 128, 128, 2, 2, L], BF16).ap() for L in (256, 1024)}
    delta_t = din("delta", [1024])
    rope_d = din("rope", [128, 8, 2, 64]).ap()

    yT = nc.dram_tensor("yT", [2, 8, 128, 1024], F32, kind="ExternalOutput").ap()
    nk_o = nc.dram_tensor("nk", [4, 2, 256, 256], F32, kind="ExternalOutput").ap()
    nv_o = nc.dram_tensor("nv", [4, 2, 256, 256], F32, kind="ExternalOutput").ap()

    def sb(name, shape, dt=F32):
        return es.enter_context(nc.sbuf_tensor(name, list(shape), dt))

    def semaphore(name):
        return es.enter_context(nc.semaphore(name))

    xT = sb("xT", [128, 8, 1024])
    hT = sb("hT", [128, 8, 1024], BF16)
    big = sb("big", [128, 22528], BF16)
    cvb = sb("cvb", [128, 4, 1024])
    dsc = sb("dsc", [128, 3, 44])
    slots = sb("slots", [128, NSLOT, SLOT_EL], BF16)
    pp = sb("ppsb", [128, NPP])
    mod = sb("mod", [128, 4 * 48 * 2])
    mod1 = sb("mod1", [128, 4 * 48 * 2])
    gbt = sb("gbt", [128, 4 * 2 * 8 * 2])
    AAt = sb("AAt", [128, 4 * 2 * 8 * 2])
    BBt = sb("BBt", [128, 4 * 2 * 8 * 2])
    condT = sb("condT", [128, 16])
    condb = sb("condb", [128, 16], BF16)
    ident = sb("identsb", [128, 128], BF16)
    onesb = sb("onesb", [128, 128], BF16)
    onesf = sb("onesf", [128, 128])
    etmp = sb("etmp", [128, 4, 512])
    NMW = 4
    modw = sb("modw", [128, NMW, 1024], BF16)
    lnst = sb("lnst", [128, 2, 512])
    xrt = sb("xrt", [128, 2, 512])
    sqt = sb("sqt", [128, 2, 512])
    fb1 = sb("fb1", [64, 8])

    SCRW = 8880
    scr = sb("scr", [128, SCRW])

    def carve():
        off = [0]

        def alloc(shape, dt=F32):
            n = int(np.prod(shape[1:]))
            nw = n if dt == F32 else (n + 1) // 2
            assert off[0] + nw <= SCRW, (off[0], nw)
            v = scr[0:shape[0], off[0]:off[0] + nw]
            off[0] += nw
            if dt == BF16:
                v = v.bitcast(BF16)
            if len(shape) == 3:
                v = v.rearrange("p (a b) -> p a b", a=shape[1])
            elif len(shape) == 4:
                v = v.rearrange("p (a b c) -> p a b c", a=shape[1], b=shape[2])
            return v
        alloc.off = off
        return alloc

    pst = es.enter_context(nc.psum_tensor("pst", [128, 4096], F32))
    ps = [pst[:, i * 512:(i + 1) * 512] for i in range(8)]

    for ln in P.lanes.values():
        ln.sem = semaphore("c_" + ln.name)
    slot_lanes = [P.dma_lane("slot%d" % i) for i in range(NSLOT)]
    misc_lane = P.dma_lane("misc")
    x_lane = P.dma_lane("xload")
    modw_lanes = [P.dma_lane("modw%d" % i) for i in range(4)]
    fg_lanes = [P.dma_lane("fg%d" % i) for i in range(4)]
    fin_lane = P.dma_lane("fin")
    zt_lane = P.dma_lane("zt")
    wo_lanes = [P.dma_lane("wo%d" % i) for i in range(2)]
    ain_lane = P.dma_lane("ain")
    kc_lane = P.dma_lane("kc")
    vc_lane = P.dma_lane("vc")
    y_lane = P.dma_lane("ystore", store=True)
    kv_lanes = [P.dma_lane("kvst%d" % i, store=True) for i in range(5)]
    dbg_lane = P.dma_lane("dbg", store=True)

    def PPv(name, *idx):
        o, w = PP_OFF[name]
        return o, w

    def ppcol(name, i):
        o, w = PP_OFF[name]
        assert 0 <= i < w
        return pp[:, o + i:o + i + 1]

    st = dict(slot=0, bank=0, zb=0, cv=0, nbank=8, pair=0)

    def load_slab(fn_list_builder, n, eng="pool"):
        i = st["slot"] % NSLOT
        st["slot"] += 1
        key = ("slot", i)
        view = slots[:, i, :]

        def fn(h, view=view):
            return fn_list_builder(h, view)
        P.dma(eng, slot_lanes[i], fn, r=(), w=(key,), n=n)
        return view, key

    def bank():
        b = st["bank"] % st["nbank"]
        st["bank"] += 1
        return b

    def bankpair():
        b = 2 * (st["pair"] % 4)
        st["pair"] += 1
        return b

    def zalloc():
        i = st["zb"] % 4
        st["zb"] += 1
        return i

    def calloc():
        i = st["cv"] % 4
        st["cv"] += 1
        return i

    def psk(b):
        return ("ps", b)

    def pro_loads(h):
        return [
            h.dma_start(out=pp[:], in_=ppd[:, :]),
            h.dma_start(out=condT[:], in_=cond.rearrange("p k c -> p (k c)")),
            h.dma_start(out=ident[:], in_=ident_d[:, :]),
        ]
    P.dma("sp", misc_lane, pro_loads, w=(("c", "pp"), ("c", "cond"), ("c", "ident")), n=3)
    P.op("dve", lambda h: h.memset(onesb[:], 1.0), w=(("c", "onesb"),))
    P.op("dve", lambda h: h.memset(etmp[:, 0, 0:128], 1.0 / 1024.0), w=(("et", 0),))
    P.op("act", lambda h: h.activation(out=onesf[:].bitcast(F32R), in_=etmp[:, 0, 0:128], func=AF.Identity), r=(("et", 0),), w=(("c", "onesf"),))
    P.op("act", lambda h: h.activation(out=condb[:], in_=condT[:], func=AF.Silu), r=(("c", "cond"),), w=(("c", "condb"),))
    o_f = PP_OFF["freq"][0]
    o_b1 = PP_OFF["pb1"][0]
    o_b2 = PP_OFF["pb2"][0]
    for j in range(2):
        P.op("dve", lambda h, j=j: h.tensor_tensor(out=fb1[:, j:j + 1], in0=pp[0:64, o_f + j:o_f + j + 1],
                                                  in1=pp[0:64, o_b1 + j:o_b1 + j + 1], op=ALU.mult),
             r=(("c", "pp"),), w=(("c", "fb1", j),))
        for i in range(2):
            P.op("dve", lambda h, j=j, i=i: h.tensor_tensor(out=fb1[:, 2 + 2 * j + i:3 + 2 * j + i], in0=pp[0:64, o_f + j:o_f + j + 1],
                                                          in1=pp[0:64, o_b2 + 2 * j + i:o_b2 + 2 * j + i + 1], op=ALU.mult),
                 r=(("c", "pp"),), w=(("c", "fb1", 2 + 2 * j + i),))

    o_bm = PP_OFF["bmod"][0]

    class ModStream:
        def __init__(self):
            self.tasks = []
            self.loaded = 0
            self.done = 0
            self.batch = []
            self.bbank = None

        def add_layer(self, l):
            self.tasks += [(l, f) for f in range(48)]

        def _load(self):
            l, f = self.tasks[self.loaded]
            i = self.loaded % 4
            self.loaded += 1
            P.dma("pool", modw_lanes[i], lambda h, l=l, f=f, i=i: [h.dma_start(
                out=modw[:, i, :].rearrange("p (k n) -> p k n", k=8),
                in_=w_mod[l].rearrange("(k p) n -> p k n", p=128)[:, :, f * 128:(f + 1) * 128])], w=(("modw", i),), n=1)

        def pending(self, l):
            return any(t[0] == l for t in self.tasks[self.done:])

        def step(self, b):
            if self.done >= len(self.tasks):
                return
            while self.loaded < len(self.tasks) and self.loaded < self.done + 3:
                self._load()
            l, f = self.tasks[self.done]
            i = self.done % 4
            self.done += 1
            if self.loaded < len(self.tasks) and self.loaded < self.done + 3:
                self._load()
            if self.batch and self.bbank != b:
                self.flush()
            self.bbank = b
            slot = len(self.batch)
            self.batch.append((l, f))
            c0 = 496 + 2 * slot

            def mm(h, i=i, b=b, c0=c0):
                ins = None
                wv = modw[:, i, :].rearrange("p (k n) -> p k n", k=8)
                for k in range(8):
                    ins = h.matmul(ps[b][:, c0:c0 + 2], lhsT=wv[:, k, :], rhs=condb[:, 2 * k:2 * k + 2], start=(k == 0), stop=(k == 7))
                return ins
            P.op("pe", mm, r=(("modw", i), ("c", "condb")), w=(("psm", b), psk(b)))
            if len(self.batch) == 8:
                self.flush()

        def flush(self):
            if not self.batch:
                return
            n = len(self.batch)
            l0, f0 = self.batch[0]
            b = self.bbank
            F0 = l0 * 48 + f0
            self.batch = []
            P.op("dve", lambda h: h.tensor_tensor(out=mod[:, 2 * F0:2 * F0 + 2 * n].rearrange("p (f c) -> p f c", c=2),
                                                  in0=ps[b][:, 496:496 + 2 * n].rearrange("p (f c) -> p f c", c=2),
                                                  in1=pp[:, o_bm + F0:o_bm + F0 + n].unsqueeze(2).to_broadcast([128, n, 2]), op=ALU.add),
                 r=(("psm", b), psk(b), ("c", "pp")), w=(("c", "mod", l0),))

        def drain(self, l, b):
            while self.pending(l):
                self.step(b)
            self.flush()

    MS = ModStream()

    def modcol(t, l, i, m, c):
        o = ((l * 48) + i * 8 + m) * 2 + c
        return t[:, o:o + 1]

    def modrow(t, l, i, c):
        o = l * 48 + i * 8
        return t[:, :].rearrange("p (x c) -> p x c", c=2)[:, o:o + 8, c]

    def t4(t, l, sub, c):
        o = (l * 2 + sub) * 8
        return t[:, :].rearrange("p (x c) -> p x c", c=2)[:, o:o + 8, c]

    def t4col(t, l, sub, m, c):
        o = ((l * 2 + sub) * 8 + m) * 2 + c
        return t[:, o:o + 1]

    def projbias(l, sub):
        if sub == 1:
            o = PP_OFF["ff_ob"][0] + l * 8
        elif l % 2 == 0:
            o = PP_OFF["hy_ob"][0] + (l // 2) * 8
        else:
            o = PP_OFF["at_ob"][0] + (l // 2) * 8
        return pp[:, o:o + 8]

    o_lng = PP_OFF["lng"][0]
    o_lnb = PP_OFF["lnb"][0]

    def aabb(l, sub):
        nl, nsub = (l, 1) if sub == 0 else (l + 1, 0)
        if nl >= depth:
            return
        lg = pp[:, o_lng + (l * 2 + sub) * 8:o_lng + (l * 2 + sub) * 8 + 8]
        lb = pp[:, o_lnb + (l * 2 + sub) * 8:o_lnb + (l * 2 + sub) * 8 + 8]
        for c in range(2):
            P.op("dve", lambda h, c=c: h.tensor_tensor(out=t4(AAt, l, sub, c), in0=lg, in1=modrow(mod1, nl, 3 * nsub + 1, c), op=ALU.mult),
                 r=(("c", "mod1", nl), ("c", "pp")), w=(("c", "AA", l, sub, c),))
            P.op("dve", lambda h, c=c: h.tensor_tensor(out=t4(BBt, l, sub, c), in0=lb, in1=modrow(mod1, nl, 3 * nsub + 1, c), op=ALU.mult),
                 r=(("c", "mod1", nl), ("c", "pp")), w=(("c", "BB", l, sub, c),))
            P.op("dve", lambda h, c=c: h.tensor_tensor(out=t4(BBt, l, sub, c), in0=t4(BBt, l, sub, c), in1=modrow(mod, nl, 3 * nsub + 0, c), op=ALU.add),
                 r=(("c", "BB", l, sub, c), ("c", "mod", nl)), w=(("c", "BB", l, sub, c),))

    def mod_finish(l):
        P.op("dve", lambda h: h.tensor_scalar(out=mod1[:, l * 96:(l + 1) * 96], in0=mod[:, l * 96:(l + 1) * 96],
                                              scalar1=1.0, scalar2=None, op0=ALU.add),
             r=(("c", "mod", l),), w=(("c", "mod1", l),))
        for sub in range(2):
            for c in range(2):
                P.op("dve", lambda h, sub=sub, c=c: h.tensor_tensor(out=t4(gbt, l, sub, c), in0=modrow(mod1, l, 3 * sub + 2, c),
                                                                    in1=projbias(l, sub), op=ALU.mult),
                     r=(("c", "mod1", l), ("c", "pp")), w=(("c", "gbt", l, sub, c),))
        aabb(l, 0)
        if l >= 1:
            aabb(l - 1, 1)

    P.phase = "prologue"
    MS.add_layer(0)
    for _ in range(16):
        MS.step(0)
    MS.flush()
    P.op("dve", lambda h: h.tensor_scalar(out=mod1[:, 0:32], in0=mod[:, 0:32], scalar1=1.0, scalar2=None, op0=ALU.add),
         r=(("c", "mod", 0),), w=(("c", "mod1", 0),))

    def MODK(l):
        return (("c", "mod", l), ("c", "mod1", l))


    def xk(m, t):
        return ("xT", m, t)

    def hk(m, t):
        return ("hT", m, t)

    def tsl(t):
        return slice(t * 512, (t + 1) * 512)

    def tokv(ap2d, g):
        return ap2d.rearrange("p (s t) -> p s t", s=GROUPS[g]["nseq"])

    def psv(b, g):
        if g == 0:
            return ps[b][:, :].rearrange("p (s t) -> p s t", s=2)
        return ps[b][:, :].rearrange("p (s t) -> p s t", s=1)

    def conv_scalars(n, o_w, stride, o_b, o_cb):
        w0, w1, w2 = (pp[:, o_w + i * stride:o_w + i * stride + n] for i in range(3))
        bi, cb = pp[:, o_b:o_b + n], pp[:, o_cb:o_cb + n]
        K = (("dsc",),)
        R_ = (("c", "pp"),)
        P.op("dve", lambda h: h.tensor_tensor(out=dsc[:, 0, 0:n], in0=w0, in1=w1, op=ALU.add), r=R_, w=K)
        P.op("dve", lambda h: h.tensor_tensor(out=dsc[:, 0, 0:n], in0=dsc[:, 0, 0:n], in1=w2, op=ALU.add), r=R_ + K, w=K)
        P.op("dve", lambda h: h.tensor_tensor(out=dsc[:, 0, 0:n], in0=dsc[:, 0, 0:n], in1=bi, op=ALU.mult), r=R_ + K, w=K)
        P.op("dve", lambda h: h.tensor_tensor(out=dsc[:, 0, 0:n], in0=dsc[:, 0, 0:n], in1=cb, op=ALU.add), r=R_ + K, w=K)
        P.op("dve", lambda h: h.scalar_tensor_tensor(out=dsc[:, 1, 0:n], in0=w0, scalar=-1.0, in1=bi, op0=ALU.mult, op1=ALU.mult), r=R_ + K, w=K)
        P.op("dve", lambda h: h.scalar_tensor_tensor(out=dsc[:, 2, 0:n], in0=w2, scalar=-1.0, in1=bi, op0=ALU.mult, op1=ALU.mult), r=R_ + K, w=K)

    def upconv(g, lhs_fn, wkey, w0, w1, w2, ci, out2d, okey, defer=None, after=None):
        cbt, nwb0, nwb2 = dsc[:, 0, ci:ci + 1], dsc[:, 1, ci:ci + 1], dsc[:, 2, ci:ci + 1]
        DK = (("dsc",), ("c", "pp"))
        b0 = bankpair()
        PK = (psk(b0), psk(b0 + 1))

        def mm_t(t):
            b = b0 + t

            def mm(h):
                ins = None
                for k in range(8):
                    ins = h.matmul(ps[b][:, :], lhsT=lhs_fn(k), rhs=hT[:, k, tsl(t)], start=(k == 0), stop=(k == 7))
                return ins
            P.op("pe", mm, r=(wkey,) + tuple(hk(k, t) for k in range(8)), w=(psk(b),))

        def post():
            pv2 = pst[:, b0 * 512:(b0 + 2) * 512]
            if g == 0:
                pv = pv2.rearrange("p (s t) -> p s t", s=4)
                ov = tokv(out2d, 0)
                o_hi, p_lo, o_lo, p_hi = ov[:, :, 1:256], pv[:, :, 0:255], ov[:, :, 0:255], pv[:, :, 1:256]
                e0, e1 = ov[:, :, 0], ov[:, :, 255]
            else:
                pv = pv2
                ov = out2d
                o_hi, p_lo, o_lo, p_hi = ov[:, 1:1024], pv[:, 0:1023], ov[:, 0:1023], pv[:, 1:1024]
                e0, e1 = ov[:, 0:1], ov[:, 1023:1024]
            P.op("act", lambda h: h.activation(out=ov, in_=pv, func=AF.Identity, scale=w1, bias=cbt), r=PK + DK, w=(okey,))
            P.op("dve", lambda h: h.scalar_tensor_tensor(out=o_hi, in0=p_lo, scalar=w0, in1=o_hi, op0=ALU.mult, op1=ALU.add), r=PK + (okey,) + DK, w=(okey,))
            P.op("dve", lambda h: h.scalar_tensor_tensor(out=o_lo, in0=p_hi, scalar=w2, in1=o_lo, op0=ALU.mult, op1=ALU.add), r=PK + (okey,) + DK, w=(okey,))
            P.op("dve", lambda h: h.tensor_scalar(out=e0, in0=e0, scalar1=nwb0, scalar2=None, op0=ALU.add), r=(okey,) + DK, w=(okey,))
            P.op("dve", lambda h: h.tensor_scalar(out=e1, in0=e1, scalar1=nwb2, scalar2=None, op0=ALU.add), r=(okey,) + DK, w=(okey,))
            if after is not None:
                after()

        if defer is not None:
            mm_t(0)
            defer.append((lambda: mm_t(1), post))
        else:
            mm_t(0)
            mm_t(1)
            post()

    def flush_deferred(defer):
        for m1, _ in defer:
            m1()
        for _, po in defer:
            po()
        del defer[:]

    def dbg_dump(tag):
        if dbg is not None and dbg == tag:
            def f(h):
                return [h.dma_start(out=yT[0, m], in_=xT[:, m, :]) for m in range(8)] + \
                       [h.dma_start(out=yT[1, m].bitcast(BF16)[:, 0:1024], in_=hT[:, m, :]) for m in range(8)]
            P.dma("sp", dbg_lane, f, r=tuple(xk(m, t) for m in range(8) for t in range(2)) + tuple(hk(m, t) for m in range(8) for t in range(2)), n=16)
            return True
        return False

    def proj_ln(g, l, sub, nk, src_fn, src_keys_fn, wload_fn, tail_hook=None):
        c = g
        sum_b = [4, 5]
        sq_b = [6, 7]
        st["nbank"] = 4
        pend = []

        def flush(n):
            while len(pend) > n:
                m, t, si = pend.pop(0)
                P.op("pe", lambda h, m=m, t=t, si=si: h.matmul(ps[sum_b[t]][:, :], lhsT=onesf[:, :].bitcast(F32R), rhs=xrt[:, si, :].bitcast(F32R),
                                                             start=(m == 0), stop=(m == 7)),
                     r=(("xr", si), ("c", "onesf")), w=(psk(sum_b[t]),))
                P.op("pe", lambda h, m=m, t=t, si=si: h.matmul(ps[sq_b[t]][:, :], lhsT=onesf[:, :].bitcast(F32R), rhs=sqt[:, si, :].bitcast(F32R),
                                                             start=(m == 0), stop=(m == 7)),
                     r=(("sq", si), ("c", "onesf")), w=(psk(sq_b[t]),))

        has_next = not (sub == 1 and l == depth - 1)
        o_g = o_lng + (l * 2 + sub) * 8
        o_b = o_lnb + (l * 2 + sub) * 8
        cnt = dict(it=0)

        def ln_stats(t):
            P.op("act", lambda h: h.activation(out=lnst[:, 0, :], in_=ps[sum_b[t]][:, :], func=AF.Square),
                 r=(psk(sum_b[t]),), w=(("ln", 0),))
            P.op("dve", lambda h: h.scalar_tensor_tensor(out=lnst[:, 0, :], in0=ps[sq_b[t]][:, :], scalar=LN_EPS, in1=lnst[:, 0, :],
                                                         op0=ALU.add, op1=ALU.subtract),
                 r=(psk(sq_b[t]), ("ln", 0)), w=(("ln", 0),))
            P.op("act", lambda h: h.activation(out=lnst[:, 0, :], in_=lnst[:, 0, :], func=AF.Sqrt),
                 r=(("ln", 0),), w=(("ln", 0),))
            P.op("dve", lambda h: h.reciprocal(out=lnst[:, 0, :], in_=lnst[:, 0, :]), r=(("ln", 0),), w=(("ln", 0),))
            P.op("dve", lambda h: h.tensor_tensor(out=lnst[:, 1, :], in0=ps[sum_b[t]][:, :], in1=lnst[:, 0, :], op=ALU.mult),
                 r=(psk(sum_b[t]), ("ln", 0)), w=(("ln", 1),))

        def ln_apply(m, t, eng="dve"):
            ei = (cnt["it"] % 4) if t == 1 else (2 + cnt["it"] % 2)
            cnt["it"] += 1
            P.op(eng, lambda h: h.tensor_tensor(out=etmp[:, ei, :], in0=xT[:, m, tsl(t)], in1=lnst[:, 0, :], op=ALU.mult),
                 r=(xk(m, t), ("ln", 0)), w=(("et", ei),))
            P.op(eng, lambda h: h.tensor_tensor(out=etmp[:, ei, :], in0=etmp[:, ei, :], in1=lnst[:, 1, :], op=ALU.subtract),
                 r=(("et", ei), ("ln", 1)), w=(("et", ei),))
            if has_next:
                P.op("act", lambda h: h.activation(out=hT[:, m, tsl(t)], in_=etmp[:, ei, :], func=AF.Identity,
                                                   scale=t4col(AAt, l, sub, m, c), bias=t4col(BBt, l, sub, m, c)),
                     r=(("et", ei), ("c", "AA", l, sub, c), ("c", "BB", l, sub, c)), w=(hk(m, t),))
            P.op("act", lambda h: h.activation(out=xT[:, m, tsl(t)], in_=etmp[:, ei, :], func=AF.Identity,
                                               scale=pp[:, o_g + m:o_g + m + 1], bias=pp[:, o_b + m:o_b + m + 1]),
                 r=(("et", ei), ("c", "pp")), w=(xk(m, t),))

        it = 0
        for t in range(2):
            for m in range(8):
                lhs_fn, wkey = wload_fn(m)
                b = bank()

                def mm(h, b=b, t=t, lhs_fn=lhs_fn):
                    ins = None
                    for k in range(nk):
                        ins = h.matmul(ps[b][:, :], lhsT=lhs_fn(k), rhs=src_fn(k, t), start=(k == 0), stop=(k == nk - 1))
                    return ins
                P.op("pe", mm, r=(wkey,) + tuple(src_keys_fn(k, t) for k in range(nk)), w=(psk(b),))
                flush(1)
                ei = it % 2
                si = it % 2
                it += 1
                P.op("act", lambda h, b=b, m=m, ei=ei: h.activation(out=etmp[:, ei, :], in_=ps[b][:, :], func=AF.Identity,
                                                                    scale=modcol(mod1, l, 3 * sub + 2, m, c), bias=t4col(gbt, l, sub, m, c)),
                     r=(psk(b), ("c", "gbt", l, sub, c)) + MODK(l), w=(("et", ei),))
                P.op("dve", lambda h, m=m, t=t, ei=ei: h.scalar_tensor_tensor(out=xT[:, m, tsl(t)], in0=xT[:, m, tsl(t)], scalar=ALPHA,
                                                                            in1=etmp[:, ei, :], op0=ALU.mult, op1=ALU.add),
                     r=(xk(m, t), ("et", ei)), w=(xk(m, t),))
                P.op("act", lambda h, m=m, t=t, si=si: h.activation(out=sqt[:, si, :].bitcast(F32R), in_=xT[:, m, tsl(t)], func=AF.Square),
                     r=(xk(m, t),), w=(("sq", si),))
                P.op("act", lambda h, m=m, t=t, si=si: h.activation(out=xrt[:, si, :].bitcast(F32R), in_=xT[:, m, tsl(t)], func=AF.Identity),
                     r=(xk(m, t),), w=(("xr", si),))
                pend.append((m, t, si))
                if t == 1:
                    ln_apply(m, 0)
            flush(0)
            if t == 0:
                ln_stats(0)
        st["nbank"] = 8
        if tail_hook is not None:
            tail_hook()
        ln_stats(1)
        for m in range(8):
            ln_apply(m, 1)

    def mixer_tail(g, l):
        if not (g == 0 and l + 1 < depth):
            return None

        return None

    def wload_sq(wmat, nk_):
        def wl(m):
            def ld(h, view):
                return [h.dma_start(out=view[:, 0:nk_ * 128].rearrange("p (k n) -> p k n", k=nk_),
                                    in_=wmat.rearrange("(k p) n -> p k n", p=128)[:, :, m * 128:(m + 1) * 128])]
            view, key = load_slab(ld, 1)
            wv = view[:, 0:nk_ * 128].rearrange("p (k n) -> p k n", k=nk_)
            return (lambda k: wv[:, k, :]), key
        return wl

    def ffn(g, l, hook=None, hook_end=None):
        P.phase = "g%d l%d ffn-up" % (g, l)
        P.fence("big")
        act = big[:, 0:22 * 1024].rearrange("p (c t) -> p c t", c=22)
        o_ib = PP_OFF["ff_in_b"][0] + l * 44
        o_cw = PP_OFF["ff_cw"][0] + l * 132
        o_cb = PP_OFF["ff_cb"][0] + l * 44
        conv_scalars(44, o_cw, 44, o_ib, o_cb)
        def ffn_load(cp):
            def ld(h, view, cp=cp):
                dv = view[:, 0:4096].rearrange("p (k a n) -> p k a n", k=8, a=2)
                sv = ff_in_w[l].rearrange("(k p) (a n) -> p k a n", p=128, a=2)
                return [h.dma_start(out=dv[:, :, a, :], in_=sv[:, :, a, cp * 256:(cp + 1) * 256]) for a in range(2)]
            return load_slab(ld, 2)
        PF = 2
        loaded = [ffn_load(i) for i in range(PF)]
        for cp in range(11):
            if hook is not None:
                hook(cp)
            if cp + PF < 11:
                loaded.append(ffn_load(cp + PF))
            view, wkey = loaded[cp]
            wv = view[:, 0:4096].rearrange("p (k a n) -> p k a n", k=8, a=2)
            dq = [] if cp == 0 else None
            for cc in range(2):
                ch = cp * 2 + cc
                cvi = [calloc(), calloc()]

                def tail(ch=ch, cvi=cvi):
                    P.op("act", lambda h: h.activation(out=cvb[:, cvi[0], :], in_=cvb[:, cvi[0], :], func=AF.Gelu_apprx_tanh),
                         r=(("cv", cvi[0]),), w=(("cv", cvi[0]),))
                    P.op(MULT_ENG, lambda h: h.tensor_tensor(out=act[:, ch, :], in0=cvb[:, cvi[0], :], in1=cvb[:, cvi[1], :], op=ALU.mult),
                         r=(("cv", cvi[0]), ("cv", cvi[1])), w=(("big", "act", ch),))
                for a in range(2):
                    ci = a * 22 + ch
                    cv = cvi[a]
                    upconv(g, (lambda k, a=a, cc=cc, wv=wv: wv[:, k, a, cc * 128:(cc + 1) * 128]), wkey,
                           pp[:, o_cw + ci:o_cw + ci + 1], pp[:, o_cw + 44 + ci:o_cw + 44 + ci + 1], pp[:, o_cw + 88 + ci:o_cw + 88 + ci + 1],
                           ci, cvb[:, cv, :], ("cv", cv), defer=dq, after=(tail if a == 1 else None))
            if dq:
                flush_deferred(dq)
        if hook_end is not None:
            hook_end()
        P.phase = "g%d l%d ffn-down+ln" % (g, l)
        proj_ln(g, l, 1, 22, lambda k, t: act[:, k, tsl(t)], lambda k, t: ("big", "act", k), wload_sq(ff_out_w[l], 22))

    def hyena(g, l):
        j = l // 2
        L = GROUPS[g]["L"]
        nseq = GROUPS[g]["nseq"]
        nj = L // 128
        if g == 0 and l + 1 < depth:
            MS.add_layer(l + 1)
        P.phase = "g%d l%d hy-inproj" % (g, l)
        P.fence("big")
        x0 = big[:, 0:8192].rearrange("p (c t) -> p c t", c=8)
        uu = big[:, 8192:16384].rearrange("p (c t) -> p c t", c=8)
        uT = big[:, 16384:18432].rearrange("p (k n) -> p k n", k=8)
        ksd_hi = big[:, 18432:18432 + 2 * nj * 256].rearrange("p (a k n) -> p a k n", a=2, k=nj)
        cvflat = cvb[:, :, :].rearrange("p a b -> p (a b)").bitcast(BF16)
        ks_lo = cvflat[:, 0:nj * 256].rearrange("p (k n) -> p k n", k=nj)
        kd_lo = cvflat[:, 2048:2048 + nj * 256].rearrange("p (k n) -> p k n", k=nj)
        ksd_lo = [ks_lo, kd_lo]
        CVK = tuple(("cv", i) for i in range(4))
        KSK = [CVK, CVK]
        o_ib = PP_OFF["hy_in_b"][0] + j * 24
        o_sw = PP_OFF["hy_sw"][0] + j * 72
        o_sb = PP_OFF["hy_sb"][0] + j * 24
        o_fb = PP_OFF["hy_fb"][0] + j * 8
        o_tn = PP_OFF["tn%d" % L][0]
        o_m0 = PP_OFF["mask0"][0]
        if True:
            P.fence("scr")
            lsb = carve()
            w1s = lsb([33, 64])
            w2s = lsb([64, 2, 64])
            wos = lsb([64, 2, 2, 256])
            fh = lsb([64, 2, 1024])
            ctmp = lsb([128, 4, 256])
            ftmp = ctmp[0:64, :, :].rearrange("p a b -> p (a b)").rearrange("p (a b) -> p a b", a=2)
            deltab = lsb([128, 2, 256])
            dect = lsb([128, 2, 256])
            kft = lsb([128, 2, 512])
            ksb = lsb([128, 2, 512])
            zTs = ksb[0:33, :, :].rearrange("p a b -> p (a b)")
            ZTK = (("scr", "K", 0), ("scr", "K", 1))
            ybuf = lsb([128, 2, 512], BF16)

            def fl(h):
                return [h.dma_start(out=w1s[:], in_=pos_w1[j]),
                        h.dma_start(out=w2s[:], in_=pos_w2[2 * j:2 * j + 2].rearrange("i k n -> k i n")),
]
            P.dma("sp", fin_lane, fl, w=(("scr", "in"),), n=2)
            P.dma("sp", zt_lane, lambda h: [h.dma_start(out=zTs[:, 0:L], in_=zT_d[L][:, :])], w=ZTK, n=1)

            conv_scalars(24, o_sw, 24, o_ib, o_sb)
            def hy_load(c):
                def ld(h, view, c=c):
                    dv = view[:, 0:3072].rearrange("p (k a n) -> p k a n", k=8, a=3)
                    sv = hy_in_w[j].rearrange("(k p) (a n) -> p k a n", p=128, a=3)
                    return [h.dma_start(out=dv[:, :, a, :], in_=sv[:, :, a, c * 128:(c + 1) * 128]) for a in range(3)]
                return load_slab(ld, 3)
            hloaded = [hy_load(0), hy_load(1)]
            for c in range(8):
                if c + 2 < 8:
                    hloaded.append(hy_load(c + 2))
                view, wkey = hloaded[c]
                wv = view[:, 0:3072].rearrange("p (k a n) -> p k a n", k=8, a=3)

                cvi = [calloc(), calloc(), calloc()]
                dq = [] if c == 0 else None

                def tail(c=c, cvi=cvi):
                    P.op("act", lambda h: h.activation(out=x0[:, c, :], in_=cvb[:, cvi[0], :], func=AF.Identity),
                         r=(("cv", cvi[0]),), w=(("big", "x0", c),))
                    P.op(MULT_ENG, lambda h: h.tensor_tensor(out=uu[:, c, :], in0=cvb[:, cvi[1], :], in1=cvb[:, cvi[2], :], op=ALU.mult),
                         r=(("cv", cvi[1]), ("cv", cvi[2])), w=(("big", "u", c),))
                for a in range(3):
                    ci = a * 8 + c
                    cv = cvi[a]
                    upconv(g, (lambda k, a=a, wv=wv: wv[:, k, a, :]), wkey,
                           pp[:, o_sw + ci:o_sw + ci + 1], pp[:, o_sw + 24 + ci:o_sw + 24 + ci + 1], pp[:, o_sw + 48 + ci:o_sw + 48 + ci + 1],
                           ci, cvb[:, cv, :], ("cv", cv), defer=dq, after=(tail if a == 2 else None))
                if dq:
                    flush_deferred(dq)

            P.phase = "g%d l%d hy-filter-mlp" % (g, l)
            FT0 = (("scr", "c", 0), ("scr", "c", 1))
            FT1 = (("scr", "c", 2), ("scr", "c", 3))

            def sin_layer(src_ps_fn, nt, dst, fcol, bcol, rk):
                for t in range(nt):
                    n = min(512, L)
                    sl = slice(t * 512, t * 512 + n)
                    b = src_ps_fn(t)
                    P.op("dve", lambda h, b=b, n=n: h.tensor_scalar(out=ftmp[:, 0, 0:n], in0=ps[b][0:64, 0:n], scalar1=pp[0:64, o_f + j:o_f + j + 1],
                                                                   scalar2=fb1[:, bcol:bcol + 1], op0=ALU.mult, op1=ALU.add),
                         r=(psk(b), ("c", "pp"), ("c", "fb1", bcol)), w=FT0)
                    P.op("dve", lambda h, n=n: h.tensor_scalar(out=ftmp[:, 1, 0:n], in0=ftmp[:, 0, 0:n], scalar1=float(1.0 / TWO_PI), scalar2=MAGIC,
                                                              op0=ALU.mult, op1=ALU.add), r=FT0, w=FT1)
                    P.op("dve", lambda h, n=n: h.tensor_scalar(out=ftmp[:, 1, 0:n], in0=ftmp[:, 1, 0:n], scalar1=MAGIC, scalar2=-TWO_PI,
                                                              op0=ALU.subtract, op1=ALU.mult), r=FT1, w=FT1)
                    P.op("dve", lambda h, n=n: h.tensor_tensor(out=ftmp[:, 0, 0:n], in0=ftmp[:, 0, 0:n], in1=ftmp[:, 1, 0:n], op=ALU.add),
                         r=FT0 + FT1, w=FT0)
                    P.op("act", lambda h, n=n, sl=sl: h.activation(out=dst[:, sl], in_=ftmp[:, 0, 0:n], func=AF.Sin),
                         r=FT0, w=(rk,))
            nt = max(1, L // 512)
            n512 = min(512, L)

            def l1(t):
                b = bank()
                P.op("pe", lambda h, b=b, t=t: h.matmul(ps[b][0:64, 0:n512], lhsT=w1s[:, :], rhs=zTs[:, t * 512:t * 512 + n512], start=True, stop=True),
                     r=(("scr", "in"),) + ZTK, w=(psk(b),))
                return b
            sin_layer(l1, nt, fh[:, 0, :], j, j, ("scr", "h", 0))
            if dbg == ("h1", g, l):
                P.dma("sp", dbg_lane, lambda h: [h.dma_start(out=yT[0, 0][0:64, 0:L], in_=fh[:, 0, 0:L])], r=(("scr", "h", 0),), n=1)
                raise _Stop()
            for i in range(2):
                def l2(t, i=i):
                    b = bank()
                    P.op("pe", lambda h, b=b, t=t, i=i: h.matmul(ps[b][0:64, 0:n512], lhsT=w2s[:, i, :], rhs=fh[:, i, t * 512:t * 512 + n512], start=True, stop=True),
                         r=(("scr", "in"), ("scr", "h", i)), w=(psk(b),))
                    return b
                sin_layer(l2, nt, fh[:, (i + 1) % 2, :], j, 2 + 2 * j + i, ("scr", "h", (i + 1) % 2))
                if dbg == ("h2", g, l) and i == 0:
                    P.dma("sp", dbg_lane, lambda h: [h.dma_start(out=yT[0, 0][0:64, 0:L], in_=fh[:, 1, 0:L]),
                                                      h.dma_start(out=yT[0, 1][0:64, 0:L], in_=fh[:, 0, 0:L])], r=(("scr", "h", 0), ("scr", "h", 1)), n=2)
                    raise _Stop()
            h3 = fh[:, 0, :]
            if dbg == ("h3", g, l):
                P.dma("sp", dbg_lane, lambda h: [h.dma_start(out=yT[0, 0][0:64, 0:L], in_=h3[:, 0:L])], r=(("scr", "h", 0),), n=1)
                raise _Stop()

            ACC = [0, 1, 2, 3]
            P.phase = "g%d l%d hy-dft" % (g, l)
            for ct in range(4):
                c0 = ct * 256
                for tc in range(8):
                    b = 7

                    def tr(h, tc=tc, ct=ct):
                        psb = ps[7][:, :].bitcast(BF16)
                        ins = None
                        for cc in range(2):
                            ins = h.transpose(psb[:, cc * 128:(cc + 1) * 128], uu[:, 2 * ct + cc, tc * 128:(tc + 1) * 128], ident[:, :])
                        return ins
                    P.op("pe", tr, r=(("big", "u", 2 * ct), ("big", "u", 2 * ct + 1), ("c", "ident")), w=(psk(7),))
                    P.op("act", lambda h, tc=tc: h.activation(out=uT[:, tc, :], in_=ps[7][:, :].bitcast(BF16)[:, 0:256], func=AF.Identity),
                         r=(psk(7),), w=(("big", "uT", tc),))
                wb = ct % 2
                P.dma("sp", wo_lanes[wb], lambda h, wb=wb, c0=c0: [
                    h.dma_start(out=wos[:, wb, :, :], in_=pos_wout[j].rearrange("k (a n) -> k a n", a=2)[:, :, c0:c0 + 256]),
                    h.dma_start(out=deltab[:, wb, :], in_=bass.AP(delta_t, c0, [[0, 128], [1, 256]]))], w=(("scr", "wo", wb),), n=2)
                for tc in range(nj):
                    b = 6
                    P.op("pe", lambda h, tc=tc, wb=wb: h.matmul(ps[6][:, :], lhsT=h3[:, tc * 128:(tc + 1) * 128],
                                                             rhs=wos[:, wb, :, :].rearrange("k a n -> k (a n)"), start=True, stop=True),
                         r=(("scr", "h", 0), ("scr", "wo", wb)), w=(psk(6),))
                    di = tc % 2
                    P.op("act", lambda h, tc=tc, di=di, wb=wb: h.activation(out=dect[:, di, :], in_=deltab[:, wb, :], func=AF.Exp,
                                                                          scale=pp[:, o_tn + tc:o_tn + tc + 1]),
                         r=(("scr", "wo", wb), ("c", "pp")), w=(("scr", "dec", di),))
                    P.op("dve", lambda h, di=di: h.tensor_tensor(out=kft[:, di, 0:256], in0=ps[6][:, 0:256], in1=dect[:, di, :], op=ALU.mult),
                         r=(psk(6), ("scr", "dec", di)), w=(("scr", "kf", di),))
                    if tc == 0:
                        P.op("dve", lambda h, di=di: h.tensor_scalar(out=dect[:, di, :], in0=dect[:, di, :], scalar1=pp[:, o_m0:o_m0 + 1], scalar2=None,
                                                                    op0=ALU.mult), r=(("scr", "dec", di), ("c", "pp")), w=(("scr", "dec", di),))
                    P.op("dve", lambda h, di=di: h.tensor_tensor(out=kft[:, di, 256:512], in0=ps[6][:, 256:512], in1=dect[:, di, :], op=ALU.mult),
                         r=(psk(6), ("scr", "dec", di)), w=(("scr", "kb", di),))
                    for a_, op_ in ((0, ALU.add), (1, ALU.subtract)):
                        P.op("dve", lambda h, di=di, a_=a_, op_=op_, tc=tc: h.tensor_tensor(out=ksd_hi[:, a_, tc, :], in0=kft[:, di, 0:256], in1=kft[:, di, 256:512], op=op_),
                             r=(("scr", "kf", di), ("scr", "kb", di)), w=(("big", "khi", a_),))
                if dbg == ("ks", g, l) and ct == 0:
                    def dks(h):
                        return [h.dma_start(out=yT[0, a_].bitcast(BF16)[:, 0:nj * 256], in_=ksd_hi[:, a_].rearrange("p k n -> p (k n)")) for a_ in range(2)]
                    P.dma("sp", dbg_lane, dks, r=(("big", "khi", 0), ("big", "khi", 1)), n=2)
                    raise _Stop()
                yi = 0
                fgi = [0]
                pendI = [None]
                for jf in range(nj):
                    fb_ = fgi[0] % 2
                    fgi[0] += 1
                    cvF = cvb[:, fb_, :].bitcast(BF16)
                    cvG = cvb[:, 2 + fb_, :].bitcast(BF16)
                    fkey, gkey = ("cv", fb_), ("cv", 2 + fb_)
                    P.dma("sp", fg_lanes[fb_], lambda h, jf=jf, cvF=cvF: [h.dma_start(
                        out=cvF[:, 0:nj * 256].rearrange("p (k n) -> p k n", k=nj), in_=F_d[L][jf][:, 0])], w=(fkey,), n=1)
                    P.dma("sp", fg_lanes[2 + fb_], lambda h, jf=jf, cvG=cvG: [h.dma_start(
                        out=cvG[:, 0:2 * L].rearrange("p (a n) -> p a n", a=2), in_=G_d[L][jf][:, 0])], w=(gkey,), n=1)
                    Fv = cvF[:, 0:nj * 256].rearrange("p (k n) -> p k n", k=nj)
                    Gv = cvG[:, 0:2 * L].rearrange("p (a n) -> p a n", a=2)

                    def mmK(h, Fv=Fv):
                        ins = None
                        for a in range(2):
                            for tc in range(nj):
                                ins = h.matmul(ps[6][:, a * 256:(a + 1) * 256], lhsT=Fv[:, tc, a * 128:(a + 1) * 128], rhs=ksd_hi[:, a, tc, :],
                                               start=(tc == 0), stop=(tc == nj - 1))
                        return ins
                    P.op("pe", mmK, r=(fkey, ("big", "khi", 0), ("big", "khi", 1)), w=(psk(6),))
                    kb_ = jf % 2
                    P.op("act", lambda h, kb_=kb_: h.activation(out=ksb[:, kb_, :], in_=ps[6][:, :], func=AF.Identity), r=(psk(6),), w=(("scr", "K", kb_),))
                    for s in range(nseq):
                        if g == 0 and l + 1 < depth:
                            for _ in range(3 if l == 0 else 2):
                                MS.step(7)
                        ub = 4 + (yi % 2)

                        def mmU(h, Fv=Fv, s=s, ub=ub):
                            ins = None
                            for a in range(2):
                                for tc in range(nj):
                                    ins = h.matmul(ps[ub][:, a * 256:(a + 1) * 256], lhsT=Fv[:, tc, a * 128:(a + 1) * 128], rhs=uT[:, s * nj + tc, :],
                                                   start=(tc == 0), stop=(tc == nj - 1))
                            return ins
                        P.op("pe", mmU, r=(fkey,) + tuple(("big", "uT", s * nj + tc) for tc in range(nj)), w=(psk(ub),))
                        prev = pendI.pop(0)
                        if prev is not None:
                            prev()
                        yb = yi % 2
                        yi += 1
                        Ure, Uim = ps[ub][:, 0:256], ps[ub][:, 256:512]
                        Kre, Kim = ksb[:, kb_, 0:256], ksb[:, kb_, 256:512]
                        rk = (psk(ub), ("scr", "K", kb_))
                        P.op("dve", lambda h, Ure=Ure, Kre=Kre: h.tensor_tensor(out=ctmp[:, 0, :], in0=Ure, in1=Kre, op=ALU.mult), r=rk, w=(("scr", "c", 0),))
                        P.op("dve", lambda h, Uim=Uim, Kim=Kim: h.tensor_tensor(out=ctmp[:, 1, :], in0=Uim, in1=Kim, op=ALU.mult), r=rk, w=(("scr", "c", 1),))
                        P.op("dve", lambda h, yb=yb: h.tensor_tensor(out=ybuf[:, yb, 0:256], in0=ctmp[:, 0, :], in1=ctmp[:, 1, :], op=ALU.subtract),
                             r=(("scr", "c", 0), ("scr", "c", 1)), w=(("scr", "yre", yb),))
                        P.op("dve", lambda h, Ure=Ure, Kim=Kim: h.tensor_tensor(out=ctmp[:, 2, :], in0=Ure, in1=Kim, op=ALU.mult), r=rk, w=(("scr", "c", 2),))
                        P.op("dve", lambda h, Uim=Uim, Kre=Kre: h.tensor_tensor(out=ctmp[:, 3, :], in0=Uim, in1=Kre, op=ALU.mult), r=rk, w=(("scr", "c", 3),))
                        P.op("dve", lambda h, yb=yb: h.tensor_tensor(out=ybuf[:, yb, 256:512], in0=ctmp[:, 2, :], in1=ctmp[:, 3, :], op=ALU.add),
                             r=(("scr", "c", 2), ("scr", "c", 3)), w=(("scr", "yim", yb),))

                        def mmI(h, Gv=Gv, s=s, yb=yb, jf=jf):
                            ins = None
                            for cc in range(2):
                                for a in range(2):
                                    lhs = ybuf[:, yb, a * 256 + cc * 128:a * 256 + (cc + 1) * 128]
                                    first = (jf == 0 and a == 0)
                                    last = (jf == nj - 1 and a == 1)
                                    if L == 1024:
                                        for tt in range(2):
                                            ins = h.matmul(ps[ACC[cc * 2 + tt]][:, :], lhsT=lhs, rhs=Gv[:, a, tt * 512:(tt + 1) * 512], start=first, stop=last)
                                    else:
                                        ins = h.matmul(ps[ACC[s]][:, cc * 256:(cc + 1) * 256], lhsT=lhs, rhs=Gv[:, a, :],
                                                       start=(first and cc == 0), stop=(last and cc == 1))
                            return ins
                        nxt = (lambda mmI=mmI, gkey=gkey, yb=yb: P.op("pe", mmI, r=(gkey, ("scr", "yre", yb), ("scr", "yim", yb)), w=tuple(psk(a_) for a_ in ACC)))
                        pendI.append(nxt)
                pendI.pop(0)()
                for cc in range(2):
                    c = 2 * ct + cc
                    for q in range(2 if L == 1024 else 4):
                        if L == 1024:
                            accap = ps[ACC[cc * 2 + q]][:, :]
                            tok = slice(q * 512, (q + 1) * 512)
                            n = 512
                            ab = ACC[cc * 2 + q]
                        else:
                            accap = ps[ACC[q]][:, cc * 256:(cc + 1) * 256]
                            tok = slice(q * 256, (q + 1) * 256)
                            n = 256
                            ab = ACC[q]
                        ei = (cc * 4 + q) % 4
                        P.op("dve", lambda h, c=c, tok=tok, accap=accap, ei=ei, n=n: h.scalar_tensor_tensor(
                            out=etmp[:, ei, 0:n], in0=uu[:, c, tok], scalar=pp[:, o_fb + c:o_fb + c + 1], in1=accap, op0=ALU.mult, op1=ALU.add),
                            r=(("big", "u", c), psk(ab), ("c", "pp")), w=(("et", ei),))
                        P.op("dve", lambda h, c=c, tok=tok, ei=ei, n=n: h.tensor_tensor(out=x0[:, c, tok], in0=etmp[:, ei, 0:n], in1=x0[:, c, tok], op=ALU.mult),
                             r=(("et", ei), ("big", "x0", c)), w=(("big", "x0", c),))
            if g == 0 and l == 0:
                MS.drain(0, 7)
                mod_finish(0)
            if g == 0 and l + 1 < depth:
                MS.drain(l + 1, 7)
                mod_finish(l + 1)
            P.phase = "g%d l%d hy-out+ln" % (g, l)
            proj_ln(g, l, 0, 8, lambda k, t: x0[:, k, tsl(t)], lambda k, t: ("big", "x0", k), wload_sq(hy_out_w[j], 8), tail_hook=mixer_tail(g, l))

    def attention(g, l):
        j = l // 2
        latent = (g == 1)
        if g == 0 and l + 1 < depth:
            MS.add_layer(l + 1)
        P.phase = "g%d l%d at-qkv" % (g, l)
        P.fence("big")
        P.fence("scr")
        qT = big[:, 0:8192].rearrange("p (h t) -> p h t", h=8)
        oT = big[:, 8192:16384].rearrange("p (h t) -> p h t", h=8)
        kT = big[:, 16384:19456].rearrange("p (h t) -> p h t", h=2)
        Vv = big[:, 19456:22528].rearrange("p (k n) -> p k n", k=12)
        lsb = carve()
        qkvb = lsb([128, 1536])
        qgain = lsb([128, 128])
        kgain = lsb([128, 128])
        rope = lsb([128, 8, 2, 64])
        kcb = lsb([128, 4, 256], BF16)
        mark = lsb.off[0]
        NQ = 4
        qf = lsb([128, NQ, 512])
        qsq = lsb([128, 2, 512])
        qst = lsb([128, NQ, 4])
        qr = lsb([128, NQ, 512], BF16)
        rtmp = lsb([128, 4, 256])
        lsb.off[0] = mark
        pT = lsb([128, 3, 512], BF16)
        rdt = lsb([128, 2, 512])
        scale = float(128 ** -0.5)

        def al(h):
            return [h.dma_start(out=qkvb[:], in_=bass.AP(qkvb_t, j * 1536, [[0, 128], [1, 1536]])),
                    h.dma_start(out=qgain[:], in_=bass.AP(qgain_t, j * 128, [[0, 128], [1, 128]])),
                    h.dma_start(out=kgain[:], in_=bass.AP(kgain_t, j * 128, [[0, 128], [1, 128]])),
                    h.dma_start(out=rope[:], in_=rope_d[:, :, :, :])]
        P.dma("sp", ain_lane, al, w=(("scr", "ain"),), n=4)
        if latent:
            P.dma("pool", kc_lane, lambda h: [h.dma_start(out=kcb[:], in_=cache_k[j].rearrange("(c p) n -> p c n", p=128))],
                  w=(("scr", "kc"),), n=1)
            P.dma("pool", vc_lane, lambda h: [h.dma_start(out=Vv[:, 8:12, :], in_=cache_v[j].rearrange("(c p) n -> p c n", p=128))],
                  w=tuple(("big", "V", 8 + i) for i in range(4)), n=1)

        it = 0
        def qkv_load(ct):
            def ld(h, view, ct=ct):
                return [h.dma_start(out=view[:, 0:4096].rearrange("p (k n) -> p k n", k=8),
                                    in_=at_qkv_w[j].rearrange("(k p) n -> p k n", p=128)[:, :, ct * 512:(ct + 1) * 512])]
            return load_slab(ld, 1)
        qloaded = [qkv_load(ct) for ct in range(3)]
        def make_item(ct, tc, it, wv, wkey):
            nh = 4 if ct < 2 else 2
            nw = nh * 128
            gain = qgain if ct < 2 else kgain
            qi = it % NQ
            sqi = it % 2
            kq = ("scr", "qf", qi)
            ks_ = ("scr", "qst", qi)
            kr = ("scr", "qr", qi)
            qv = qf[:, qi, 0:nw].rearrange("p (a b) -> p a b", a=nh)
            st_ = {}

            def s0():
                b = bank()
                while b >= 6:
                    b = bank()
                if g == 0 and l + 1 < depth:
                    MS.step(6)
                    MS.step(6)

                def mm(h):
                    ins = None
                    for k in range(8):
                        ins = h.matmul(ps[b][:, :], lhsT=hT[:, k, tc * 128:(tc + 1) * 128], rhs=wv[:, k, :], start=(k == 0), stop=(k == 7))
                    return ins
                P.op("pe", mm, r=(wkey,) + tuple(hk(k, tc // 4) for k in range(8)), w=(psk(b),))
                P.op("dve", lambda h: h.tensor_tensor(out=qf[:, qi, :], in0=ps[b][:, :], in1=qkvb[:, ct * 512:(ct + 1) * 512], op=ALU.add),
                     r=(psk(b), ("scr", "ain")), w=(kq,))
                P.op("act", lambda h: h.activation(out=qsq[:, sqi, 0:nw], in_=qf[:, qi, 0:nw], func=AF.Square), r=(kq,), w=(("scr", "qsq", sqi),))

            def s1():
                P.op("dve", lambda h: h.tensor_reduce(out=qst[:, qi, 0:nh], in_=qsq[:, sqi, 0:nw].rearrange("p (a b) -> p a b", a=nh),
                                                      axis=AX.X, op=ALU.add), r=(("scr", "qsq", sqi),), w=(ks_,))
                P.op("dve", lambda h: h.tensor_scalar(out=qst[:, qi, 0:nh], in0=qst[:, qi, 0:nh], scalar1=1.0 / 128.0, scalar2=QK_EPS,
                                                      op0=ALU.mult, op1=ALU.add), r=(ks_,), w=(ks_,))
                P.op("act", lambda h: h.activation(out=qst[:, qi, 0:nh], in_=qst[:, qi, 0:nh], func=AF.Sqrt), r=(ks_,), w=(ks_,))

            def s2():
                P.op("dve", lambda h: h.reciprocal(out=qst[:, qi, 0:nh], in_=qst[:, qi, 0:nh]), r=(ks_,), w=(ks_,))
                P.op("dve", lambda h: h.tensor_tensor(out=qv, in0=qv, in1=qst[:, qi, 0:nh].unsqueeze(2).to_broadcast([128, nh, 128]), op=ALU.mult),
                     r=(kq, ks_), w=(kq,))
                P.op("pool" if latent else "dve", lambda h: h.tensor_tensor(out=qv, in0=qv, in1=gain[:, :].unsqueeze(1).to_broadcast([128, nh, 128]),
                                                                            op=ALU.mult), r=(kq, ("scr", "ain")), w=(kq,))

            def s3():
                if latent:
                    q4 = qf[:, qi, 0:nw].rearrange("p (a b two) -> p a b two", a=nh, two=2)
                    r4 = qr[:, qi, 0:nw].rearrange("p (a b two) -> p a b two", a=nh, two=2)
                    ev, od = q4[:, :, :, 0], q4[:, :, :, 1]
                    cosb = rope[:, tc, 0, :].unsqueeze(1).to_broadcast([128, nh, 64])
                    sinb = rope[:, tc, 1, :].unsqueeze(1).to_broadcast([128, nh, 64])
                    nr = nh * 64
                    rv = [rtmp[:, i, 0:nr].rearrange("p (a b) -> p a b", a=nh) for i in range(4)]
                    rr = (kq, ("scr", "ain"))
                    P.op("dve", lambda h: h.tensor_tensor(out=rv[0], in0=ev, in1=cosb, op=ALU.mult), r=rr, w=(("scr", "rt", 0),))
                    P.op("dve", lambda h: h.tensor_tensor(out=rv[1], in0=od, in1=sinb, op=ALU.mult), r=rr, w=(("scr", "rt", 1),))
                    P.op("dve", lambda h: h.tensor_tensor(out=r4[:, :, :, 0], in0=rv[0], in1=rv[1], op=ALU.subtract),
                         r=(("scr", "rt", 0), ("scr", "rt", 1)), w=(kr,))
                    P.op("pool", lambda h: h.tensor_tensor(out=rv[2], in0=ev, in1=sinb, op=ALU.mult), r=rr, w=(("scr", "rt", 2),))
                    P.op("pool", lambda h: h.tensor_tensor(out=rv[3], in0=od, in1=cosb, op=ALU.mult), r=rr, w=(("scr", "rt", 3),))
                    P.op("pool", lambda h: h.tensor_tensor(out=r4[:, :, :, 1], in0=rv[2], in1=rv[3], op=ALU.add),
                         r=(("scr", "rt", 2), ("scr", "rt", 3), kr), w=(kr,))
                else:
                    P.op("act", lambda h: h.activation(out=qr[:, qi, 0:nw], in_=qf[:, qi, 0:nw], func=AF.Identity), r=(kq,), w=(kr,))
                if ct == 2:
                    P.op("act", lambda h: h.activation(out=Vv[:, tc, :], in_=qf[:, qi, 256:512], func=AF.Identity), r=(kq,), w=(("big", "V", tc),))
                    if not latent:
                        s_, r0 = tc // 2, (tc % 2) * 128
                        P.dma("sp", kv_lanes[qi], lambda h: [
                            h.dma_start(out=nk_o[s_, j, r0:r0 + 128, :], in_=qf[:, qi, 0:256]),
                            h.dma_start(out=nv_o[s_, j, r0:r0 + 128, :], in_=qf[:, qi, 256:512])], r=(kq,), n=2)

            def s4():
                def tr(h):
                    psb = ps[7][:, :].bitcast(BF16)
                    ins = None
                    for hh in range(nh):
                        ins = h.transpose(psb[:, hh * 128:(hh + 1) * 128], qr[:, qi, hh * 128:(hh + 1) * 128], ident[:, :])
                    return ins
                P.op("pe", tr, r=(kr, ("c", "ident")), w=(psk(7),))
                if ct < 2:
                    P.op("act", lambda h: h.activation(out=qT[:, ct * 4:ct * 4 + 4, tc * 128:(tc + 1) * 128],
                                                       in_=ps[7][:, :].bitcast(BF16)[:, 0:512].rearrange("p (a b) -> p a b", a=4), func=AF.Identity),
                         r=(psk(7),), w=tuple(("big", "qT", ct * 4 + hh) for hh in range(4)))
                else:
                    P.op("act", lambda h: h.activation(out=kT[:, :, tc * 128:(tc + 1) * 128],
                                                       in_=ps[7][:, :].bitcast(BF16)[:, 0:256].rearrange("p (a b) -> p a b", a=2), func=AF.Identity),
                         r=(psk(7),), w=(("big", "kT", 0), ("big", "kT", 1)))
            return [s0, s1, s2, s3, s4]

        items = []
        for ct in range(3):
            view, wkey = qloaded[ct]
            wv = view[:, 0:4096].rearrange("p (k n) -> p k n", k=8)
            for tc in range(8):
                items.append(make_item(ct, tc, len(items), wv, wkey))
        NST = 5
        for n in range(len(items) + NST - 1):
            for sidx in range(NST):
                i = n - sidx
                if 0 <= i < len(items):
                    items[i][sidx]()
        if latent:
            for kc in range(4):
                def trc(h, kc=kc):
                    psb = ps[7][:, :].bitcast(BF16)
                    ins = None
                    for kvh in range(2):
                        ins = h.transpose(psb[:, kvh * 128:(kvh + 1) * 128], kcb[:, kc, kvh * 128:(kvh + 1) * 128], ident[:, :])
                    return ins
                P.op("pe", trc, r=(("scr", "kc"), ("c", "ident")), w=(psk(7),))
                P.op("act", lambda h, kc=kc: h.activation(out=kT[:, :, 1024 + kc * 128:1024 + (kc + 1) * 128],
                                                        in_=ps[7][:, :].bitcast(BF16)[:, 0:256].rearrange("p (a b) -> p a b", a=2), func=AF.Identity),
                     r=(psk(7),), w=(("big", "kT", 0), ("big", "kT", 1)))

        if g == 0 and l + 1 < depth:
            MS.drain(l + 1, 6)
            mod_finish(l + 1)
        P.fence("scr")
        P.phase = "g%d l%d at-core" % (g, l)
        units = []
        if latent:
            for qt in range(2):
                for kvh in range(2):
                    for hh in range(4):
                        hd_ = kvh * 4 + hh
                        units.append(dict(kvh=kvh, rhs=qT[:, hd_, qt * 512:(qt + 1) * 512], rk=(("big", "qT", hd_),), kcs=list(range(12)),
                                          out=oT[:, hd_, qt * 512:(qt + 1) * 512], ok=(("big", "oT", hd_),), v3=False))
        else:
            for s_ in range(4):
                for kvh in range(2):
                    for hp in range(2):
                        h0 = kvh * 4 + hp * 2
                        units.append(dict(kvh=kvh, rhs=qT[:, h0:h0 + 2, s_ * 256:(s_ + 1) * 256], rk=(("big", "qT", h0), ("big", "qT", h0 + 1)),
                                          kcs=[2 * s_, 2 * s_ + 1], out=oT[:, h0:h0 + 2, s_ * 256:(s_ + 1) * 256],
                                          ok=(("big", "oT", h0), ("big", "oT", h0 + 1)), v3=True))
        srot = 0
        for ui, u in enumerate(units):
            ob = 3 + ui % 2
            db = 5 + ui % 2
            kvh = u["kvh"]
            pend = None
            nk_ = len(u["kcs"])

            def od(idx, kc, pi, u=u, ob=ob, db=db, kvh=kvh, nk_=nk_):
                def f(h):
                    h.matmul(ps[ob][:, :], lhsT=Vv[:, kc, kvh * 128:(kvh + 1) * 128], rhs=pT[:, pi, :], start=(idx == 0), stop=(idx == nk_ - 1))
                    return h.matmul(ps[db][:, :], lhsT=onesb[:, :], rhs=pT[:, pi, :], start=(idx == 0), stop=(idx == nk_ - 1))
                P.op("pe", f, r=(("big", "V", kc), ("scr", "pT", pi), ("c", "onesb")), w=(psk(ob), psk(db)))
            for idx, kc in enumerate(u["kcs"]):
                sb_ = srot % 3
                pi = srot % 3
                srot += 1
                P.op("pe", lambda h, sb_=sb_, kc=kc, u=u, kvh=kvh: h.matmul(ps[sb_][:, :], lhsT=kT[:, kvh, kc * 128:(kc + 1) * 128], rhs=u["rhs"],
                                                                         start=True, stop=True),
                     r=(("big", "kT", kvh),) + u["rk"], w=(psk(sb_),))
                P.op("act", lambda h, sb_=sb_, pi=pi: h.activation(out=pT[:, pi, :], in_=ps[sb_][:, :], func=AF.Exp, scale=scale),
                     r=(psk(sb_),), w=(("scr", "pT", pi),))
                if pend is not None:
                    od(*pend)
                pend = (idx, kc, pi)
            od(*pend)
            ri = ui % 2
            P.op("dve", lambda h, db=db, ri=ri: h.reciprocal(out=rdt[:, ri, :], in_=ps[db][:, :]), r=(psk(db),), w=(("scr", "rd", ri),))
            if u["v3"]:
                P.op("dve", lambda h, ob=ob, ri=ri, u=u: h.tensor_tensor(out=u["out"], in0=ps[ob][:, :].rearrange("p (a b) -> p a b", a=2),
                                                                      in1=rdt[:, ri, :].rearrange("p (a b) -> p a b", a=2), op=ALU.mult),
                     r=(psk(ob), ("scr", "rd", ri)), w=u["ok"])
            else:
                P.op("dve", lambda h, ob=ob, ri=ri, u=u: h.tensor_tensor(out=u["out"], in0=ps[ob][:, :], in1=rdt[:, ri, :], op=ALU.mult),
                     r=(psk(ob), ("scr", "rd", ri)), w=u["ok"])
        P.phase = "g%d l%d at-out+ln" % (g, l)
        proj_ln(g, l, 0, 8, lambda k, t: oT[:, k, tsl(t)], lambda k, t: ("big", "oT", k), wload_sq(at_o_w[j], 8), tail_hook=mixer_tail(g, l))

    allx = tuple(xk(m, t) for m in range(8) for t in range(2))
    stop = False
    for g in range(ngroups):
        c = g
        P.dma("sp", x_lane, lambda h, g=g: [h.dma_start(out=xT[:, m, :], in_=xin[g, m]) for m in range(8)], w=allx, n=8)
        for m in range(8):
            for t in range(2):
                P.op("dve", lambda h, m=m, t=t, c=c: h.tensor_scalar(out=hT[:, m, tsl(t)], in0=xT[:, m, tsl(t)], scalar1=modcol(mod1, 0, 1, m, c),
                                                                   scalar2=modcol(mod, 0, 0, m, c), op0=ALU.mult, op1=ALU.add),
                     r=(xk(m, t),) + MODK(0), w=(hk(m, t),))
        for l in range(depth):
            try:
                if l % 2 == 0:
                    hyena(g, l)
                else:
                    attention(g, l)
            except _Stop:
                stop = True
                break
            if dbg_dump((g, l, 0)):
                stop = True
                break
            ffn(g, l)
            if dbg_dump((g, l, 1)):
                stop = True
                break
        if stop:
            break
        P.dma("sp", y_lane, lambda h, g=g: [h.dma_start(out=yT[g, m], in_=xT[:, m, :]) for m in range(8)], r=allx, n=8)

    P.finalize()
    for ln in P.dma_lanes:
        ln.sem = semaphore("d_" + ln.name)
    with nc.Block() as block:
        @block.tensor
        def _(h):
            P.emit("pe", h)

        @block.scalar
        def _(h):
            P.emit("act", h)

        @block.vector
        def _(h):
            P.emit("dve", h)

        @block.gpsimd
        def _(h):
            P.emit("pool", h)

        @block.sync
        def _(h):
            P.emit("sp", h)
    es.close()
    nc._prog_stats = {e: len(P.eng_ops[e]) for e in P.ENGS}
    nc._pe_log = P.pe_log
    return nc


_NC_CACHE = {}


def make_in_maps(inp):
    f32 = lambda a: np.ascontiguousarray(np.asarray(a, np.float32))
    consts = make_consts()
    pp = pack_pp(inp)
    shared = {
        "pp": pp,
        "w_mod": f32(inp["w_mod"]), "hy_in_w": f32(inp["hy_in_w"]), "hy_out_w": f32(inp["hy_out_w"]),
        "at_qkv_w": f32(inp["at_qkv_w"]), "at_o_w": f32(inp["at_o_w"]), "ff_in_w": f32(inp["ff_in_w"]), "ff_out_w": f32(inp["ff_out_w"]),
        "pos_w1": f32(inp["hy_pos_w1"]), "pos_w2": f32(inp["hy_pos_w2"]).reshape(4, 64, 64), "pos_wout": f32(inp["hy_pos_wout"]),
        "qkvb": f32(inp["at_qkv_b"]), "qgain": f32(inp["at_q_gain"]), "kgain": f32(inp["at_k_gain"]),
        "ident": consts["ident"], "zT256": consts["zT256"], "zT1024": consts["zT1024"],
        "F256": consts["F256"], "F1024": consts["F1024"], "G256": consts["G256"], "G1024": consts["G1024"],
        "delta": consts["delta"], "rope": consts["rope"],
    }
    xp = f32(inp["x_prompt"])
    xs = f32(inp["x_sample"])
    ck = f32(inp["cache_k"])
    cv = f32(inp["cache_v"])
    cc = f32(inp["c"])
    cctx = f32(inp["c_ctx"])
    maps = []
    for core in range(8):
        x0 = xp[4 * core:4 * core + 4].reshape(1024, 1024)
        x1 = xs[core]
        xin = np.stack([x0.T.reshape(8, 128, 1024), x1.T.reshape(8, 128, 1024)], 0)
        cond = np.stack([cctx, cc[core]], 0).reshape(2, 8, 128).transpose(2, 1, 0)
        m = dict(shared)
        m["xin"] = np.ascontiguousarray(xin)
        m["cond"] = np.ascontiguousarray(cond)
        m["cache_k"] = np.ascontiguousarray(ck[core].reshape(2, 512, 256))
        m["cache_v"] = np.ascontiguousarray(cv[core].reshape(2, 512, 256))
        maps.append(m)
    return maps


def kernel(**inp):
    if "nc" not in _NC_CACHE:
        _NC_CACHE["nc"] = build_nc()
    nc = _NC_CACHE["nc"]
    maps = make_in_maps(inp)
    res = run_bass_kernel_spmd(nc, maps, core_ids=list(range(8)))
    y_prompt = np.zeros((32, 256, 1024), np.float32)
    y_sample = np.zeros((8, 1024, 1024), np.float32)
    nk = np.zeros((32, 2, 256, 2, 128), np.float32)
    nv = np.zeros((32, 2, 256, 2, 128), np.float32)
    for core in range(8):
        r = res.results[core]
        yT = np.asarray(r["yT"], np.float32)
        y_prompt[4 * core:4 * core + 4] = yT[0].reshape(1024, 1024).T.reshape(4, 256, 1024)
        y_sample[core] = yT[1].reshape(1024, 1024).T
        nk[4 * core:4 * core + 4] = np.asarray(r["nk"], np.float32).reshape(4, 2, 256, 2, 128)
        nv[4 * core:4 * core + 4] = np.asarray(r["nv"], np.float32).reshape(4, 2, 256, 2, 128)
    return (y_prompt, y_sample, nk, nv)
```
